# Optimizing a Trainium2 kernel written in Bass

```python
import math
import jax, jax.numpy as jnp
from jax import lax
import numpy as np

D_MODEL = 2048
BATCH = 2
SEQ = 16384
DEPTH = 1

N_META = 16
SSM_WIDTH = 512
SSM_GROUP = 16
SSM_GROUPS = SSM_WIDTH // SSM_GROUP
SSM_STATE = 64
CONV_WIDTH = 1024
CONV_K = 31
D_FF = 5632
FFN_K = 3
NORM_EPS = 1e-6
LN_EPS = 1e-5

kernel_name = "hybrid_s5_conformer_convffn_block"


def rms_norm(x, g):
    xf = x.astype(jnp.float32)
    y = xf * lax.rsqrt(jnp.mean(xf * xf, axis=-1, keepdims=True) + NORM_EPS)
    return (y * g.astype(jnp.float32)).astype(x.dtype)


def layer_norm(x, g, b):
    xf = x.astype(jnp.float32)
    mu = jnp.mean(xf, axis=-1, keepdims=True)
    var = jnp.mean(jnp.square(xf - mu), axis=-1, keepdims=True)
    y = (xf - mu) * lax.rsqrt(var + LN_EPS)
    return (y * g.astype(jnp.float32) + b.astype(jnp.float32)).astype(x.dtype)


def causal_depthwise_conv(x, w, b):
    k_width = w.shape[0]
    seq_len = x.shape[1]
    xp = jnp.pad(x, ((0, 0), (k_width - 1, 0), (0, 0)))
    y = b + xp[:, 0:seq_len] * w[0]
    for k in range(1, k_width):
        y = y + xp[:, k:k + seq_len] * w[k]
    return y


def _complex_linear_combine(e1, e2):
    a1r, a1i, b1r, b1i = e1
    a2r, a2i, b2r, b2i = e2
    ar = a2r * a1r - a2i * a1i
    ai = a2r * a1i + a2i * a1r
    br = a2r * b1r - a2i * b1i + b2r
    bi = a2r * b1i + a2i * b1r + b2i
    return (ar, ai, br, bi)


def s5_ssm(u, lam_re, lam_im, log_step, b_re, b_im, c_re, c_im, d_skip):
    bsz, seq_len, width = u.shape
    f32 = jnp.float32
    uf = u.astype(f32).reshape(bsz, seq_len, SSM_GROUPS, SSM_GROUP)
    lr = lam_re.astype(f32)
    li = lam_im.astype(f32)
    step = jnp.exp(log_step.astype(f32))[:, None]
    mag = jnp.exp(lr * step)
    ar = mag * jnp.cos(li * step)
    ai = mag * jnp.sin(li * step)
    den = lr * lr + li * li
    cr = ((ar - 1.0) * lr + ai * li) / den
    ci = (ai * lr - (ar - 1.0) * li) / den
    br_ = b_re.astype(f32)
    bi_ = b_im.astype(f32)
    bbar_re = cr[..., None] * br_ - ci[..., None] * bi_
    bbar_im = cr[..., None] * bi_ + ci[..., None] * br_
    bu_re = jnp.einsum('blgk,gpk->blgp', uf, bbar_re)
    bu_im = jnp.einsum('blgk,gpk->blgp', uf, bbar_im)
    a_re = jnp.broadcast_to(ar, bu_re.shape)
    a_im = jnp.broadcast_to(ai, bu_im.shape)
    _, _, xs_re, xs_im = lax.associative_scan(
        _complex_linear_combine, (a_re, a_im, bu_re, bu_im), axis=1)
    y = (jnp.einsum('blgp,gkp->blgk', xs_re, c_re.astype(f32))
         - jnp.einsum('blgp,gkp->blgk', xs_im, c_im.astype(f32))
         + d_skip.astype(f32) * uf)
    return y.reshape(bsz, seq_len, width)


def setup_inputs(seed: int = 0) -> dict:
    key = jax.random.key(seed)
    ks = jax.random.split(key, 32)
    f32 = jnp.float32
    nrm = lambda k, shape, scale: jax.random.normal(k, shape, f32) * scale
    gain = lambda k, shape: 1.0 + 0.01 * jax.random.normal(k, shape, f32)
    in_cols = SSM_WIDTH + 2 * CONV_WIDTH + 2 * D_MODEL
    n_idx = jnp.arange(SSM_STATE, dtype=f32)
    lam_re = -0.5 + 0.01 * jax.random.normal(ks[3], (DEPTH, SSM_GROUPS, SSM_STATE), f32)
    lam_im = math.pi * n_idx + 0.01 * jax.random.normal(ks[4], (DEPTH, SSM_GROUPS, SSM_STATE), f32)
    log_step = jax.random.uniform(ks[5], (DEPTH, SSM_GROUPS), f32,
                                  minval=math.log(1e-3), maxval=math.log(1e-1))
    return {
        "x": jax.random.normal(ks[0], (BATCH, SEQ, D_MODEL), f32),
        "meta_tokens": nrm(ks[1], (N_META, D_MODEL), 1.0),
        "norm_mix_pre": gain(ks[2], (DEPTH, D_MODEL)),
        "w_in": nrm(ks[6], (DEPTH, D_MODEL, in_cols), D_MODEL ** -0.5),
        "lam_re": lam_re,
        "lam_im": lam_im,
        "log_step": log_step,
        "ssm_b_re": nrm(ks[7], (DEPTH, SSM_GROUPS, SSM_STATE, SSM_GROUP), (2 * SSM_GROUP) ** -0.5),
        "ssm_b_im": nrm(ks[8], (DEPTH, SSM_GROUPS, SSM_STATE, SSM_GROUP), (2 * SSM_GROUP) ** -0.5),
        "ssm_c_re": nrm(ks[9], (DEPTH, SSM_GROUPS, SSM_GROUP, SSM_STATE), (2 * SSM_STATE) ** -0.5),
        "ssm_c_im": nrm(ks[10], (DEPTH, SSM_GROUPS, SSM_GROUP, SSM_STATE), (2 * SSM_STATE) ** -0.5),
        "ssm_d": nrm(ks[11], (DEPTH, SSM_GROUPS, SSM_GROUP), 1.0),
        "w_ssm_glu": nrm(ks[12], (DEPTH, SSM_WIDTH, SSM_WIDTH), SSM_WIDTH ** -0.5),
        "w_ssm_proj": nrm(ks[13], (DEPTH, SSM_WIDTH, D_MODEL), SSM_WIDTH ** -0.5),
        "conv_dw_w": nrm(ks[14], (DEPTH, CONV_K, CONV_WIDTH), CONV_K ** -0.5),
        "conv_dw_b": nrm(ks[15], (DEPTH, CONV_WIDTH), 0.01),
        "conv_ln_g": gain(ks[16], (DEPTH, CONV_WIDTH)),
        "conv_ln_b": nrm(ks[17], (DEPTH, CONV_WIDTH), 0.01),
        "w_conv_proj": nrm(ks[18], (DEPTH, CONV_WIDTH, D_MODEL), CONV_WIDTH ** -0.5),
        "w_mix_out": nrm(ks[19], (DEPTH, D_MODEL, D_MODEL), D_MODEL ** -0.5),
        "norm_mix_post": gain(ks[20], (DEPTH, D_MODEL)),
        "norm_ffn_pre": gain(ks[21], (DEPTH, D_MODEL)),
        "w_ffn_up": nrm(ks[22], (DEPTH, D_MODEL, 2 * D_FF), D_MODEL ** -0.5),
        "ffn_dw_w": nrm(ks[23], (DEPTH, FFN_K, 2 * D_FF), FFN_K ** -0.5),
        "ffn_dw_b": nrm(ks[24], (DEPTH, 2 * D_FF), 0.01),
        "w_ffn_down": nrm(ks[25], (DEPTH, D_FF, D_MODEL), D_FF ** -0.5),
        "norm_ffn_post": gain(ks[26], (DEPTH, D_MODEL)),
    }


def reference(x, meta_tokens, norm_mix_pre, w_in, lam_re, lam_im, log_step,
              ssm_b_re, ssm_b_im, ssm_c_re, ssm_c_im, ssm_d, w_ssm_glu, w_ssm_proj,
              conv_dw_w, conv_dw_b, conv_ln_g, conv_ln_b, w_conv_proj, w_mix_out,
              norm_mix_post, norm_ffn_pre, w_ffn_up, ffn_dw_w, ffn_dw_b, w_ffn_down,
              norm_ffn_post):
    bsz = x.shape[0]
    meta = jnp.broadcast_to(meta_tokens.astype(x.dtype)[None], (bsz, N_META, D_MODEL))
    h = jnp.concatenate([meta, x], axis=1)
    split_pts = [SSM_WIDTH, SSM_WIDTH + CONV_WIDTH, SSM_WIDTH + 2 * CONV_WIDTH,
                 SSM_WIDTH + 2 * CONV_WIDTH + D_MODEL]
    for i in range(DEPTH):
        n = rms_norm(h, norm_mix_pre[i])
        z = n @ w_in[i]
        u_ssm, conv_val, conv_gate, gate_ssm, gate_conv = jnp.split(z, split_pts, axis=-1)

        y = s5_ssm(u_ssm, lam_re[i], lam_im[i], log_step[i], ssm_b_re[i], ssm_b_im[i],
                   ssm_c_re[i], ssm_c_im[i], ssm_d[i]).astype(x.dtype)
        y = jax.nn.gelu(y)
        y = y * jax.nn.sigmoid(y @ w_ssm_glu[i])
        branch_a = y @ w_ssm_proj[i]

        c = conv_val * jax.nn.sigmoid(conv_gate)
        c = causal_depthwise_conv(c, conv_dw_w[i], conv_dw_b[i])
        c = jax.nn.silu(layer_norm(c, conv_ln_g[i], conv_ln_b[i]))
        branch_b = c @ w_conv_proj[i]

        merged = jax.nn.sigmoid(gate_ssm) * branch_a + jax.nn.sigmoid(gate_conv) * branch_b
        h = h + rms_norm(merged @ w_mix_out[i], norm_mix_post[i])

        n = rms_norm(h, norm_ffn_pre[i])
        up = causal_depthwise_conv(n @ w_ffn_up[i], ffn_dw_w[i], ffn_dw_b[i])
        f_gate, f_val = jnp.split(up, 2, axis=-1)
        f = (jax.nn.gelu(f_gate) * f_val) @ w_ffn_down[i]
        h = h + rms_norm(f, norm_ffn_post[i])
    return h[:, N_META:]
```

```python
import contextlib
import math
import numpy as np
import concourse.bass as bass
import concourse.mybir as mybir
from concourse.bass_utils import run_bass_kernel_spmd

F32 = mybir.dt.float32
BF16 = mybir.dt.bfloat16
I32 = mybir.dt.int32
AF = mybir.ActivationFunctionType
ALU = mybir.AluOpType

ENGS = ("pe", "act", "dve", "pool", "sp")
SEM_LIMIT = 30000
SAME_ENGINE_SYNC = {"act", "dve", "pool"}

D = 2048
NMETA = 16
SSMW = 512
CONVW = 1024
CK = 31
DFF = 5632
EPS = 1e-6
LNEPS = 1e-5
TWO_PI = 2.0 * math.pi


class Op:
    __slots__ = ("eng", "fn", "deps", "is_dma", "needs_inc", "tok", "dsem", "inc")

    def __init__(self, eng, fn, deps, is_dma):
        self.eng = eng
        self.fn = fn
        self.deps = deps
        self.is_dma = is_dma
        self.needs_inc = False
        self.tok = None
        self.dsem = None
        self.inc = 16 if is_dma else 1


class Res:
    __slots__ = ("lastw", "readers")

    def __init__(self):
        self.lastw = None
        self.readers = []


class K:
    def __init__(self, nc, ndma_sems=8):
        self.nc = nc
        self.ops = {e: [] for e in ENGS}
        self.res = {}
        self.ndma = ndma_sems
        self.ntiles = 0

    def sb(self, shape, dtype, name=None):
        self.ntiles += 1
        return self.nc.alloc_sbuf_tensor(name or f"sb{self.ntiles}", list(shape), dtype).ap()

    def ps(self, shape, dtype=F32, name=None):
        self.ntiles += 1
        return self.nc.alloc_psum_tensor(name or f"ps{self.ntiles}", list(shape), dtype).ap()

    def _r(self, key):
        r = self.res.get(key)
        if r is None:
            r = self.res[key] = Res()
        return r

    def handoff(self, from_keys, to_keys):
        acc = []
        for fk in from_keys:
            r = self._r(fk)
            if r.lastw is not None:
                acc.append(r.lastw)
            acc.extend(r.readers)
        for tk in to_keys:
            self._r(tk).readers.extend(acc)

    @staticmethod
    def _excl(key):
        if isinstance(key, tuple):
            return key[0] == "pm"
        return key in ("pst1", "pst2", "py")

    def _record(self, eng, fn, reads, writes, is_dma):
        xr = [x for x in reads if self._excl(x)]
        if xr:
            writes = list(writes) + [x for x in xr if x not in writes]
            reads = [x for x in reads if not self._excl(x)]
        deps = []
        for key in reads:
            r = self._r(key)
            if r.lastw is not None:
                deps.append(r.lastw)
        for key in writes:
            r = self._r(key)
            if r.lastw is not None:
                deps.append(r.lastw)
            deps.extend(r.readers)
        op = Op(eng, fn, deps, is_dma)
        for d in deps:
            d.needs_inc = True
        self.ops[eng].append(op)
        for key in writes:
            r = self._r(key)
            r.lastw = op
            r.readers = []
        for key in reads:
            r = self._r(key)
            r.readers = [x for x in r.readers if x.is_dma or x.eng != eng or is_dma]
            r.readers.append(op)
        return op

    def op(self, eng, fn, reads=(), writes=()):
        return self._record(eng, fn, reads, writes, False)

    def dma(self, eng, out, in_, reads=(), writes=(), **kw):
        def fn(e):
            return e.dma_start(out=out, in_=in_, **kw)
        op = self._record(eng, fn, reads, writes, True)
        op.needs_inc = True
        return op

    def emit(self, final_wait_ops=()):
        nc = self.nc
        es = contextlib.ExitStack()
        with es:
            def newsem(name):
                return es.enter_context(nc.semaphore(name))

            for e in ENGS:
                cur = None
                cnt = 0
                nsem = 0
                dcount = 0
                dsems = None
                dvals = None
                for op in self.ops[e]:
                    if op.is_dma:
                        if dsems is None:
                            dsems = [newsem(f"d_{e}_{i}") for i in range(self.ndma)]
                            dvals = [0] * self.ndma
                        slot = dcount % self.ndma
                        dcount += 1
                        dvals[slot] += op.inc
                        op.tok = (dsems[slot], dvals[slot])
                        op.dsem = (dsems[slot], dvals[slot] - op.inc)
                    elif op.needs_inc:
                        if cur is None or cnt >= SEM_LIMIT:
                            cur = newsem(f"c_{e}_{nsem}")
                            nsem += 1
                            cnt = 0
                        cnt += 1
                        op.tok = (cur, cnt)
            final_toks = [o.tok for o in final_wait_ops]

            def run_engine(e, h):
                waited = {}

                def wait(tok):
                    s, v = tok
                    if waited.get(id(s), 0) >= v:
                        return
                    waited[id(s)] = v
                    h.wait_ge(s, v)

                for op in self.ops[e]:
                    for d in op.deps:
                        if d.eng == e and (not d.is_dma) and e not in SAME_ENGINE_SYNC:
                            continue
                        wait(d.tok)
                    if op.is_dma:
                        s, prev = op.dsem
                        if prev > 0:
                            wait((s, prev))
                        ins = op.fn(h)
                        ins.then_inc(op.tok[0], op.inc)
                    else:
                        ins = op.fn(h)
                        if op.needs_inc:
                            ins.then_inc(op.tok[0], 1)
                if e == "sp":
                    for t in final_toks:
                        wait(t)

            with nc.Block() as block:
                @block.tensor
                def _(h):
                    run_engine("pe", h)

                @block.scalar
                def _(h):
                    run_engine("act", h)

                @block.vector
                def _(h):
                    run_engine("dve", h)

                @block.gpsimd
                def _(h):
                    run_engine("pool", h)

                @block.sync
                def _(h):
                    run_engine("sp", h)


WSPEC = {
    "w_in": (D, SSMW + 2 * CONVW + 2 * D),
    "w_ssm_glu": (SSMW, SSMW),
    "w_ssm_proj": (SSMW, D),
    "w_conv_proj": (CONVW, D),
    "w_mix_out": (D, D),
    "w_ffn_up": (D, 2 * DFF),
    "w_ffn_down": (DFF, D),
}
VEC_INPUTS = {
    "norm_mix_pre": D, "norm_mix_post": D, "norm_ffn_pre": D, "norm_ffn_post": D,
    "conv_dw_b": CONVW, "conv_ln_g": CONVW, "conv_ln_b": CONVW,
    "ffn_dw_b": 2 * DFF, "ssm_d": SSMW, "lam_re": 2048, "lam_im": 2048,
}


def build(NB, N, H, LQ, ncores=8, use_cc=True, stop=None):
    W = NB * N
    assert W == H + LQ and N % 2 == 0
    Ns = N // 2
    nc = bass.Bass("TRN2", target_bir_lowering=False)
    k = K(nc)
    KC = D // 128

    def din(name, shape):
        return nc.dram_tensor(name, list(shape), F32, kind="ExternalInput").ap()

    xw = din("xw", [W, D])
    maskd = din("mask", [1, W])
    ccoefd = din("ccoef", [1, 24])
    wd = {n: din(n, list(s)) for n, s in WSPEC.items()}
    vd = {n: din(n, [s // 128, 128]) for n, s in VEC_INPUTS.items()}
    log_step_d = din("log_step", [16, 2])
    b_re_d = din("ssm_b_re", [32, 64, 16])
    b_im_d = din("ssm_b_im", [32, 64, 16])
    c_re_d = din("ssm_c_re", [32, 16, 64])
    c_im_d = din("ssm_c_im", [32, 16, 64])
    conv_w_d = din("conv_dw_w", [CK, CONVW])
    ffn_w_d = din("ffn_dw_w", [3, 88, 128])
    outd = nc.dram_tensor("out", [LQ, D], F32, kind="ExternalOutput").ap()
    wscr = {n: nc.dram_tensor("scr_" + n, [s[1] // 128, 128, s[0] // 128, 128], BF16,
                              kind="Internal").ap() for n, s in WSPEC.items()}
    NQ = ncores // 2
    cdg = nc.dram_tensor("scr_convdiag", [8, 128, 32, 128], BF16, kind="Internal").ap()
    cc_in = nc.dram_tensor("cc_in", [8, 512], F32, kind="Internal").ap()
    cc_out = nc.dram_tensor("cc_out", [NQ * 8, 512], F32, kind="Internal", addr_space="Local").ap()

    def act(out, in_, func, reads, writes, scale=None, bias=None):
        kw = {}
        if scale is not None:
            kw["scale"] = scale
        if bias is not None:
            kw["bias"] = bias
        return k.op("act", lambda e: e.activation(out=out, in_=in_, func=func, **kw), reads, writes)

    def tt(out, a, b, op, reads, writes, eng="dve"):
        return k.op(eng, lambda e: e.tensor_tensor(out=out, in0=a, in1=b, op=op), reads, writes)

    def ts(out, a, s1, s2, op0, op1, reads, writes, eng="dve"):
        if op1 is None:
            return k.op(eng, lambda e: e.tensor_scalar(out=out, in0=a, scalar1=s1, scalar2=None, op0=op0),
                        reads, writes)
        return k.op(eng, lambda e: e.tensor_scalar(out=out, in0=a, scalar1=s1, scalar2=s2, op0=op0, op1=op1),
                    reads, writes)

    def stt(out, a, s, b, op0, op1, reads, writes):
        return k.op("dve", lambda e: e.scalar_tensor_tensor(out=out, in0=a, scalar=s, in1=b, op0=op0, op1=op1),
                    reads, writes)

    def cp(out, in_, reads, writes, eng="dve"):
        if eng == "act":
            return act(out, in_, AF.Copy, reads, writes)
        return k.op(eng, lambda e: e.tensor_copy(out=out, in_=in_), reads, writes)

    def mm(out, lhsT, rhs, start, stop, reads, writes):
        return k.op("pe", lambda e: e.matmul(out, lhsT=lhsT, rhs=rhs, start=start, stop=stop), reads, writes)

    def tr(out, in_, ident_ap, reads, writes):
        return k.op("pe", lambda e: e.transpose(out=out, in_=in_, identity=ident_ap), reads, writes)

    NPM = 5
    pm = [k.ps([128, 512], F32, name=f"pm{i}") for i in range(NPM)]
    pst1 = k.ps([128, 512], F32, name="pst1")
    pst2 = k.ps([128, 512], F32, name="pst2")
    py = k.ps([128, 512], F32, name="py")
    pmi = [0]

    def newps():
        i = pmi[0] % NPM
        pmi[0] += 1
        return pm[i], ("pm", i)

    ident = k.sb([128, 128], F32, "ident")
    k.op("pool", lambda e: e.memset(ident, 0.0), writes=["ident"])
    k.op("pool", lambda e: e.affine_select(out=ident, in_=ident, pattern=[[-1, 128]], compare_op=ALU.not_equal,
                                           fill=1.0, base=0, channel_multiplier=1), reads=["ident"], writes=["ident"])
    ones_b = k.sb([128, 128], BF16, "ones_b")
    k.op("pool", lambda e: e.memset(ones_b, 1.0), writes=["ones_b"])

    R1F = (44 * N * 2 + 3) // 4
    R1F = max(R1F, 6400)
    R1 = k.sb([128, R1F], F32, "R1")
    R2F = max(16 * N, 6400)
    R2 = k.sb([128, R2F], F32, "R2")

    class Carver:
        def __init__(self, region):
            self.r = region
            self.off = 0

        def f32(self, a, n):
            v = self.r[:, self.off:self.off + a * n]
            self.off += a * n
            return v.rearrange("p (a n) -> p a n", n=n) if a > 1 else v

        def bf(self, a, n):
            words = (a * n + 1) // 2
            v = self.r[:, self.off:self.off + words].bitcast(BF16)[:, 0:a * n]
            self.off += words
            return v.rearrange("p (a n) -> p a n", n=n) if a > 1 else v

    c1 = Carver(R1)
    u_f = c1.f32(4, N)
    u_b = c1.bf(4, N)
    cin = [c1.f32(1, N + CK - 1) for _ in range(2)]
    cinb = [c_.bitcast(BF16)[:, 0:N + CK - 1] for c_ in cin]
    csil = c1.bf(8, N)
    stmp = [c1.f32(1, Ns) for _ in range(6)]
    qb = [c1.bf(1, Ns) for _ in range(4)]
    ysb = c1.bf(4, N)
    tA = c1.f32(1, N)
    tB = c1.f32(1, N)
    mtmp = c1.f32(1, N)
    assert c1.off <= R1F, (c1.off, R1F)
    mo_f = R1[:, 0:16 * N].rearrange("p (a n) -> p a n", n=N) if 16 * N <= R1F else None
    assert mo_f is not None
    hid = R1[:, 0:22 * N].bitcast(BF16).rearrange("p (a n) -> p a n", n=N)
    stg_f = [R2[:, i * 2048:(i + 1) * 2048] for i in range(2)]
    stg_b = [R2[:, 4096 + i * 1024:4096 + (i + 1) * 1024].bitcast(BF16) for i in range(2)]
    assert 4096 + 2048 <= R2F
    c2 = Carver(R2)
    acc = c2.f32(8, N)
    merged = c2.bf(16, N)
    assert c2.off <= R2F
    fo_f = R2[:, 0:16 * N].rearrange("p (a n) -> p a n", n=N)
    R1_MIX = ["u_f", "u_b", "cin0", "cin1", "csil", "stmp", "qb", "ysb", "tA", "tB", "mtmp"] + \
             [("u_f", i) for i in range(4)] + [("u_b", i) for i in range(4)] + [("csil", i) for i in range(8)] + \
             [("ysb", i) for i in range(4)] + [("stmp", i) for i in range(6)] + [("qb", i) for i in range(4)]
    MO_KEYS = [("mo", i) for i in range(16)]
    HID_KEYS = [("hid", i) for i in range(44)]
    R2_MIX = [("acc", i) for i in range(8)] + [("merged", i) for i in range(16)]
    FO_KEYS = [("fo", i) for i in range(16)]
    STG_KEYS = ["stgf0", "stgf1", "stgb0", "stgb1"]

    csu = [0]

    def sb2(shape, dtype=F32):
        parts = shape[0]
        n = 1
        for d_ in shape[1:]:
            n *= d_
        v = R2[0:parts, csu[0]:csu[0] + n]
        csu[0] += n
        assert csu[0] <= R2F, csu[0]
        if dtype != F32:
            v = v.bitcast(dtype)
        if len(shape) == 3:
            v = v.rearrange("p (a b) -> p a b", b=shape[2])
        return v

    h = k.sb([128, 16, N], F32, "h")
    nb = k.sb([128, 16, N], BF16, "nb")
    sqb = [k.sb([128, N], BF16, f"sqb{i}") for i in range(2)]
    rt = [k.sb([128, N], F32, f"rt{i}") for i in range(3)]
    maskb = k.sb([128, N], F32, "maskb")
    cosT = k.sb([128, 16, Ns], F32, "cosT")
    sinT = k.sb([128, 16, Ns], F32, "sinT")
    CT = k.sb([128, 16, 3, 128], BF16, "CT")
    BT = k.sb([128, 16, 2, 128], BF16, "BT")
    NWB = 4
    wbuf = [k.sb([128, 16, 128], BF16, f"wbuf{i}") for i in range(NWB)]
    xst = k.sb([128, D], F32, "xst")
    stmp2 = [xst[:, i * Ns:(i + 1) * Ns] for i in range(6)]
    qb2 = [xst[:, 6 * Ns + i * ((Ns + 1) // 2):6 * Ns + (i + 1) * ((Ns + 1) // 2)].bitcast(BF16)[:, 0:Ns] for i in range(4)]
    assert 6 * Ns + 4 * ((Ns + 1) // 2) <= D
    SET2 = [("stmp2", i) for i in range(6)] + [("qb2", i) for i in range(4)]
    hist = k.sb([128, 8, CK - 1], F32, "hist")
    carry = k.sb([128, 88, 2], F32, "carry")
    Xre = k.sb([128, 16], F32, "Xre")
    Xim = k.sb([128, 16], F32, "Xim")
    Wend = k.sb([128, 2, 16], F32, "Wend")
    hx = [k.sb([128, 16], F32, f"hx{i}") for i in range(4)]

    pv = {}

    def load_vec(name):
        n = VEC_INPUTS[name] // 128
        t_in = k.sb([n, 128], F32, "vin_" + name)
        k.dma("sp", t_in, vd[name], writes=["vin_" + name])
        ps, pk = newps()
        tr(ps[:, 0:n], t_in, ident[0:n, 0:n], ["vin_" + name, "ident"], [pk])
        t = k.sb([128, n], F32, "pv_" + name)
        cp(t, ps[:, 0:n], [pk], ["pv_" + name])
        pv[name] = t

    for name in VEC_INPUTS:
        load_vec(name)
    PVK = ["pv_" + n for n in VEC_INPUTS]

    cw_in = sb2([CK, CONVW])
    k.dma("sp", cw_in, conv_w_d, writes=["cw_in"])
    convw = k.sb([128, 8, CK], F32, "convw")
    ps, pk = newps()
    for cc in range(8):
        tr(ps[:, cc * CK:(cc + 1) * CK], cw_in[:, cc * 128:(cc + 1) * 128], ident[0:CK, 0:CK], ["cw_in", "ident"], [pk])
    cp(convw.rearrange("p a b -> p (a b)"), ps[:, 0:8 * CK], [pk], ["convw"])
    fw_in = sb2([88, 3, 128])
    k.dma("sp", fw_in, ffn_w_d.rearrange("k r c -> r k c"), writes=["fw_in"])
    ffnw = k.sb([128, 3, 88], F32, "ffnw")
    ps, pk = newps()
    for kk in range(3):
        tr(ps[:, kk * 88:(kk + 1) * 88], fw_in[:, kk, :], ident[0:88, 0:88], ["fw_in", "ident"], [pk])
    cp(ffnw.rearrange("p a b -> p (a b)"), ps[:, 0:264], [pk], ["ffnw"])
    PVK += ["convw", "ffnw"]
    histb = hist.rearrange("p a b -> p (a b)").bitcast(BF16)[:, 0:8 * (CK - 1)].rearrange("p (a b) -> p a b", b=CK - 1)
    for cc in range(8):
        for half in range(2):
            stgw = wbuf[half]
            for j in range(16):
                k_ = half * 16 + j
                if k_ < CK:
                    k.op("pool", lambda e, o_=stgw[:, j, :], s_=convw[:, cc, k_:k_ + 1]: e.tensor_scalar(
                        out=o_, in0=ident, scalar1=s_, scalar2=1.0, op0=ALU.mult, op1=ALU.mult),
                        ["ident", "convw"], [("wbuf", half)])
                else:
                    k.op("pool", lambda e, o_=stgw[:, j, :]: e.memset(o_, 0.0), [], [("wbuf", half)])
            k.dma("sp", cdg[cc, :, half * 16:(half + 1) * 16, :], stgw, reads=[("wbuf", half)],
                  writes=[("cdg", cc, half)])

    if stop == "vec":
        k.emit()
        return nc
    sm = {}

    def smt(name, shape=(128, 16)):
        sm[name] = k.sb(list(shape), F32, "sm_" + name)
        return sm[name]

    ls_in = k.sb([16, 2], F32, "ls_in")
    k.dma("sp", ls_in, log_step_d, writes=["ls_in"])
    ls_x = k.sb([16, 2, 64], F32, "ls_x")
    cp(ls_x, ls_in.unsqueeze(2).to_broadcast([16, 2, 64]), ["ls_in"], ["ls_x"])
    ps, pk = newps()
    tr(ps[:, 0:16], ls_x.rearrange("a g p -> a (g p)"), ident[0:16, 0:16], ["ls_x", "ident"], [pk])
    lsT = smt("lsT")
    cp(lsT, ps[:, 0:16], [pk], ["sm"])
    SM = ["sm"] + PVK
    lr = pv["lam_re"]
    li = pv["lam_im"]
    step = smt("step")
    act(step, lsT, AF.Exp, SM, ["sm"])
    lrs = smt("lrs")
    tt(lrs, lr, step, ALU.mult, SM, ["sm"])
    mag = smt("mag")
    act(mag, lrs, AF.Exp, SM, ["sm"])
    th = smt("th")
    tt(th, li, step, ALU.mult, SM, ["sm"])

    sc_tmp = {}

    def sincos(theta, cos_out, sin_out, shape, key, temps=None):
        if temps is not None:
            t0, t1, ti = temps
        else:
            if shape not in sc_tmp:
                sc_tmp[shape] = (sb2(list(shape)), sb2(list(shape)), sb2(list(shape), I32))
            t0, t1, ti = sc_tmp[shape]
        for shift, outp in ((0.25, cos_out), (0.0, sin_out)):
            ts(t0, theta, 1.0 / TWO_PI, 0.5 + shift, ALU.mult, ALU.add, [key], [key + "_t0"])
            cp(ti, t0, [key + "_t0"], [key + "_ti"])
            cp(t1, ti, [key + "_ti"], [key + "_t1"])
            tt(t0, t0, t1, ALU.subtract, [key + "_t0", key + "_t1"], [key + "_t0"])
            ts(t1, t0, 0.0, None, ALU.is_lt, None, [key + "_t0"], [key + "_t1"])
            tt(t0, t0, t1, ALU.add, [key + "_t0", key + "_t1"], [key + "_t0"])
            ts(t0, t0, -0.5, -0.4999999, ALU.add, ALU.max, [key + "_t0"], [key + "_t0"])
            ts(t0, t0, 0.4999999, None, ALU.min, None, [key + "_t0"], [key + "_t0"])
            act(outp, t0, AF.Sin, [key + "_t0"], [key], scale=TWO_PI)

    cth = smt("cth")
    sth = smt("sth")
    sincos(th, cth, sth, (128, 16), "sm")
    ar = smt("ar")
    ai = smt("ai")
    tt(ar, mag, cth, ALU.mult, SM, ["sm"])
    tt(ai, mag, sth, ALU.mult, SM, ["sm"])
    den = smt("den")
    t_a = smt("t_a")
    t_b = smt("t_b")
    tt(den, lr, lr, ALU.mult, SM, ["sm"])
    tt(t_a, li, li, ALU.mult, SM, ["sm"])
    tt(den, den, t_a, ALU.add, SM, ["sm"])
    rden = smt("rden")
    k.op("dve", lambda e: e.reciprocal(out=rden, in_=den), SM, ["sm"])
    am1 = smt("am1")
    ts(am1, ar, -1.0, None, ALU.add, None, SM, ["sm"])
    cr = smt("cr")
    ci = smt("ci")
    tt(t_a, am1, lr, ALU.mult, SM, ["sm"])
    tt(t_b, ai, li, ALU.mult, SM, ["sm"])
    tt(t_a, t_a, t_b, ALU.add, SM, ["sm"])
    tt(cr, t_a, rden, ALU.mult, SM, ["sm"])
    tt(t_a, ai, lr, ALU.mult, SM, ["sm"])
    tt(t_b, am1, li, ALU.mult, SM, ["sm"])
    tt(t_a, t_a, t_b, ALU.subtract, SM, ["sm"])
    tt(ci, t_a, rden, ALU.mult, SM, ["sm"])

    b_re = sb2([128, 16, 16])
    b_im = sb2([128, 16, 16])
    for t_, d_ in ((b_re, b_re_d), (b_im, b_im_d)):
        src = bass.AP(d_.tensor, 0, [[16, 128], [2048, 16], [1, 16]])
        k.dma("sp", t_, src, writes=["sm"], allow_slow_non_contiguous=True)
    crb = cr.unsqueeze(2).to_broadcast([128, 16, 16])
    cib = ci.unsqueeze(2).to_broadcast([128, 16, 16])
    Bb_re = sb2([128, 16, 16])
    Bb_im = sb2([128, 16, 16])
    t3a = sb2([128, 16, 16])
    tt(Bb_re, b_re, crb, ALU.mult, SM, ["sm"])
    tt(t3a, b_im, cib, ALU.mult, SM, ["sm"])
    tt(Bb_re, Bb_re, t3a, ALU.subtract, SM, ["sm"])
    tt(Bb_im, b_im, crb, ALU.mult, SM, ["sm"])
    tt(t3a, b_re, cib, ALU.mult, SM, ["sm"])
    tt(Bb_im, Bb_im, t3a, ALU.add, SM, ["sm"])
    k.op("pool", lambda e: e.memset(BT.rearrange("p a b c -> p (a b c)"), 0.0), writes=["BT"])
    xb = sb2([128, 128])
    for P in range(16):
        m = P % 4
        for v, Bsrc in ((0, Bb_re), (1, Bb_im)):
            k.op("dve", lambda e: e.memset(xb, 0.0), writes=["xb"])
            for g2 in range(2):
                col = m * 32 + g2 * 16
                cp(xb[g2 * 64:(g2 + 1) * 64, col:col + 16], Bsrc[g2 * 64:(g2 + 1) * 64, P, :], SM, ["xb"])
            ps, pk = newps()
            tr(ps[:, 0:128], xb, ident, ["xb", "ident"], [pk])
            hb = (m // 2) * 64
            cp(BT[hb:hb + 64, P, v, :], ps[hb:hb + 64, 0:128], [pk], ["BT"])
    k.op("pool", lambda e: e.memset(CT.rearrange("p a b c -> p (a b c)"), 0.0), writes=["CT"])
    crow = sb2([128, 2, 128])
    csb = sb2([128, 8, 16])
    for v_re, d_ in ((True, c_re_d), (False, c_im_d)):
        for g2 in range(2):
            src = bass.AP(d_.tensor, g2 * 1024, [[2048, 16], [64, 16], [1, 64]])
            for half in range(2):
                for P8 in range(8):
                    srch = bass.AP(d_.tensor, g2 * 1024 + (half * 8 + P8) * 2048, [[64, 16], [1, 64]])
                    k.dma("sp", crow[P8 * 16:(P8 + 1) * 16, half, g2 * 64:(g2 + 1) * 64], srch, writes=["crow"])
        for half in range(2):
            ps, pk = newps()
            tr(ps[:, 0:128], crow[:, half, :], ident, ["crow", "ident"], [pk])
            cp(csb.rearrange("p a b -> p (a b)"), ps[:, 0:128], [pk], ["csb"])
            for P8 in range(8):
                P = half * 8 + P8
                m = P % 4
                for g2 in range(2):
                    col = m * 32 + g2 * 16
                    src_ = csb[g2 * 64:(g2 + 1) * 64, P8, :]
                    if v_re:
                        cp(CT[g2 * 64:(g2 + 1) * 64, P, 0, col:col + 16], src_, ["csb"], ["CT"])
                        ts(CT[g2 * 64:(g2 + 1) * 64, P, 1, col:col + 16], src_, -1.0, None, ALU.mult, None,
                           ["csb"], ["CT"])
                    else:
                        ts(CT[g2 * 64:(g2 + 1) * 64, P, 2, col:col + 16], src_, -1.0, None, ALU.mult, None,
                           ["csb"], ["CT"])
    io_i = sb2([128, Ns], I32)
    k.op("pool", lambda e: e.iota(io_i, pattern=[[1, Ns]], base=1, channel_multiplier=0), writes=["io"])
    io_f = sb2([128, Ns])
    cp(io_f, io_i, ["io"], ["io_f"])
    TW = 8 * Ns
    angb = R1[:, 0:TW]
    tmpb = (R1[:, TW:2 * TW], R1[:, 2 * TW:3 * TW], R1[:, 3 * TW:4 * TW].bitcast(I32))
    assert 4 * TW <= R1F
    for hh in range(2):
        tt(angb.rearrange("p (a b) -> p a b", b=Ns), io_f.unsqueeze(1).to_broadcast([128, 8, Ns]),
           th[:, hh * 8:(hh + 1) * 8].unsqueeze(2).to_broadcast([128, 8, Ns]), ALU.mult, ["io_f"] + SM, ["tab"])
        sincos(angb, cosT[:, hh * 8:(hh + 1) * 8, :].rearrange("p a b -> p (a b)"),
               sinT[:, hh * 8:(hh + 1) * 8, :].rearrange("p a b -> p (a b)"), (128, TW), "tab", temps=tmpb)
    TAB = ["tab", "sm"]
    k.handoff(["tab", "tab_t0", "tab_t1", "tab_ti"], R1_MIX + MO_KEYS + HID_KEYS)

    if stop == "ssm":
        k.emit()
        return nc
    SETUP_KEYS = ["cw_in", "fw_in", "sm", "xb", "crow", "csb", "io", "io_f", "tab", "tab_t0", "tab_t1", "tab_ti",
                  "sm_t0", "sm_t1", "sm_ti"]
    k.handoff(SETUP_KEYS, STG_KEYS)
    pieces = []
    for name, (Kd, Nout) in WSPEC.items():
        for oc in range(Nout // 128):
            pieces.append((name, oc))

    def conv_piece(name, oc, gate=None):
        src = wd[name][:, oc * 128:(oc + 1) * 128].rearrange("(k p) c -> p k c", p=128)
        k.dma("pool", wscr[name][oc], src, reads=([gate] if gate is not None else []), writes=[("scr", name, oc)])

    def conv_some(n, gate=None):
        for _ in range(n):
            if pieces:
                conv_piece(*pieces.pop(0), gate=gate)

    if stop == "conv":
        k.emit()
        return nc
    wbi = [0]

    def getw(name, oc, k0=0, k1=None):
        kcn = WSPEC[name][0] // 128
        if k1 is None:
            k1 = kcn
        i = wbi[0] % NWB
        wbi[0] += 1
        k.dma("sp", wbuf[i][:, 0:k1 - k0, :], wscr[name][oc, :, k0:k1, :], reads=[("scr", name, oc)],
              writes=[("wbuf", i)])
        return wbuf[i], ("wbuf", i)

    def proj(name, oc, rhs_fn, rhs_keys, ncols, out_ps=None):
        kcn = WSPEC[name][0] // 128
        if out_ps is None:
            ps, pk = newps()
        else:
            ps, pk = out_ps
        for k0 in range(0, kcn, 16):
            k1 = min(kcn, k0 + 16)
            wt, wk = getw(name, oc, k0, k1)
            for kc in range(k0, k1):
                mm(ps[:, 0:ncols], wt[:, kc - k0, :], rhs_fn(kc), kc == 0, kc == kcn - 1,
                   [wk] + list(rhs_keys), [pk])
        return ps, pk

    def load_x_block(b, gate_key=None):
        for t0 in range(0, N, 128):
            rows = min(128, N - t0)
            k.dma("sp", xst[0:rows, :], xw[b * N + t0:b * N + t0 + rows, :],
                  writes=["xst"] + ([gate_key] if (gate_key is not None and t0 == 0) else []))
            for c4 in range(4):
                ps, pk = newps()
                for c in range(4):
                    cidx = c4 * 4 + c
                    tr(ps[:, c * 128:c * 128 + rows], xst[0:rows, cidx * 128:(cidx + 1) * 128], ident[0:rows, 0:rows],
                       ["xst", "ident"], [pk])
                cp(h[:, c4 * 4:(c4 + 1) * 4, t0:t0 + rows],
                   ps[:, 0:512].rearrange("p (c r) -> p c r", r=128)[:, :, 0:rows], [pk],
                   [("h", c4 * 4 + c) for c in range(4)], eng=("act" if c4 % 2 else "dve"))

    def rstd_of(psum_ap, pkey, scale, eps, out, okey):
        ts(out, psum_ap, scale, eps, ALU.mult, ALU.add, [pkey], [okey])
        act(out, out, AF.Sqrt, [okey], [okey])
        k.op("dve", lambda e: e.reciprocal(out=out, in_=out), [okey], [okey])

    def rms_to_nb(gname, extra_mask=False):
        for c in range(16):
            s = sqb[c % 2]
            act(s, h[:, c, :], AF.Square, [("h", c)], [("sqb", c % 2)])
            mm(pst1[:, 0:N], ones_b, s, c == 0, c == 15, [("sqb", c % 2), "ones_b"], ["pst1"])
        rstd_of(pst1[:, 0:N], "pst1", 1.0 / D, EPS, rt[0], "rt0")
        if extra_mask:
            tt(rt[0], rt[0], maskb, ALU.mult, ["rt0", "maskb"], ["rt0"])
        for c in range(16):
            stt(nb[:, c, :], h[:, c, :], pv[gname][:, c:c + 1], rt[0], ALU.mult, ALU.mult,
                [("h", c), "rt0"] + PVK, [("nb", c)])

    NBK = [("nb", c) for c in range(16)]

    def u_proj():
        for oc in range(4):
            ps, pk = proj("w_in", oc, lambda kc: nb[:, kc, :], NBK, N)
            act(u_f[:, oc, :], ps[:, 0:N], AF.Copy, [pk], [("u_f", oc)])
            cp(u_b[:, oc, :], ps[:, 0:N], [pk], [("u_b", oc)])

    def ssm_pass(with_out, extract=None):
        k.handoff(["xst"], SET2)
        _ssm_pass(with_out, extract)
        k.handoff(SET2, ["xst"])

    def _ssm_pass(with_out, extract=None):
        for sbi in range(2):
            c0 = sbi * Ns
            def pair_gen(P, sbi=sbi, c0=c0):
                ch = P // 4
                hb = ((P % 4) // 2) * 64
                psr, pkr = newps()
                psi, pki = newps()
                mm(psr[:, 0:Ns], BT[hb:hb + 64, P, 0, :], u_b[hb:hb + 64, ch, c0:c0 + Ns], True, True,
                   ["BT", ("u_b", ch)], [pkr])
                mm(psi[:, 0:Ns], BT[hb:hb + 64, P, 1, :], u_b[hb:hb + 64, ch, c0:c0 + Ns], True, True,
                   ["BT", ("u_b", ch)], [pki])
                yield
                cs = cosT[:, P, :]
                sn = sinT[:, P, :]
                if P % 2 == 0:
                    m1, m2, m3, m4, wr, wi = stmp
                    K_ = [("stmp", i) for i in range(6)]
                    qbs = qb
                    Q_ = [("qb", i) for i in range(4)]
                else:
                    m1, m2, m3, m4, wr, wi = stmp2
                    K_ = [("stmp2", i) for i in range(6)]
                    qbs = qb2
                    Q_ = [("qb2", i) for i in range(4)]
                tt(m1, psr[:, 0:Ns], cs, ALU.mult, [pkr] + TAB, [K_[0]])
                yield
                tt(m2, psi[:, 0:Ns], sn, ALU.mult, [pki] + TAB, [K_[1]])
                yield
                tt(m3, psi[:, 0:Ns], cs, ALU.mult, [pki] + TAB, [K_[2]])
                yield
                tt(m4, psr[:, 0:Ns], sn, ALU.mult, [pkr] + TAB, [K_[3]])
                yield
                tt(m1, m1, m2, ALU.add, [K_[0], K_[1]], [K_[0]])
                yield
                tt(m3, m3, m4, ALU.subtract, [K_[2], K_[3]], [K_[2]])
                yield
                dec = mag[:, P:P + 1].to_broadcast([128, Ns])
                k.op("dve", lambda e, wr=wr, m1=m1, dec=dec, P=P: e.tensor_tensor_scan(
                    out=wr, data0=dec, data1=m1, initial=Xre[:, P:P + 1], op0=ALU.mult, op1=ALU.add),
                    [K_[0], "X"] + TAB, [K_[4]])
                yield
                k.op("dve", lambda e, wi=wi, m3=m3, dec=dec, P=P: e.tensor_tensor_scan(
                    out=wi, data0=dec, data1=m3, initial=Xim[:, P:P + 1], op0=ALU.mult, op1=ALU.add),
                    [K_[2], "X"] + TAB, [K_[5]])
                yield
                col = Ns - 1
                if extract is not None and extract[0] == sbi:
                    col = extract[1]
                act(Wend[:, 0, P:P + 1], wr[:, col:col + 1], AF.Copy, [K_[4]], ["Wend"])
                act(Wend[:, 1, P:P + 1], wi[:, col:col + 1], AF.Copy, [K_[5]], ["Wend"])
                yield
                if with_out:
                    tt(qbs[0], wr, cs, ALU.mult, [K_[4]] + TAB, [Q_[0]], eng="pool")
                    tt(qbs[1], wi, sn, ALU.mult, [K_[5]] + TAB, [Q_[1]], eng="pool")
                    tt(qbs[2], wi, cs, ALU.mult, [K_[5]] + TAB, [Q_[2]], eng="pool")
                    tt(qbs[3], wr, sn, ALU.mult, [K_[4]] + TAB, [Q_[3]], eng="pool")
                    yield
                    for qi, v in ((0, 0), (1, 1), (2, 2), (3, 2)):
                        mm(py[:, c0:c0 + Ns], CT[:, P, v, :], qbs[qi], (P % 4 == 0 and qi == 0),
                           (P % 4 == 3 and qi == 3), [Q_[qi], "CT"], ["py"])
                    if P % 4 == 3:
                        stt(u_f[:, ch, c0:c0 + Ns], u_f[:, ch, c0:c0 + Ns], pv["ssm_d"][:, ch:ch + 1],
                            py[:, c0:c0 + Ns], ALU.mult, ALU.add, ["py", ("u_f", ch)] + PVK, [("u_f", ch)])
                        act(u_f[:, ch, c0:c0 + Ns], u_f[:, ch, c0:c0 + Ns], AF.Gelu_apprx_tanh,
                            [("u_f", ch)], [("u_f", ch)])

            for P0 in range(0, 16, 2):
                gens = [pair_gen(P0), pair_gen(P0 + 1)]
                while gens:
                    for g_ in list(gens):
                        try:
                            next(g_)
                        except StopIteration:
                            gens.remove(g_)
            col = Ns - 1
            if extract is not None and extract[0] == sbi:
                col = extract[1]
            cN = cosT[:, :, col]
            sN = sinT[:, :, col]
            tt(hx[0], Wend[:, 0, :], cN, ALU.mult, ["Wend"] + TAB, ["hx0"])
            tt(hx[1], Wend[:, 1, :], sN, ALU.mult, ["Wend"] + TAB, ["hx1"])
            tt(hx[2], Wend[:, 1, :], cN, ALU.mult, ["Wend"] + TAB, ["hx2"])
            tt(hx[3], Wend[:, 0, :], sN, ALU.mult, ["Wend"] + TAB, ["hx3"])
            tt(Xre, hx[0], hx[1], ALU.subtract, ["hx0", "hx1"], ["X"])
            tt(Xim, hx[2], hx[3], ALU.add, ["hx2", "hx3"], ["X"])
            if extract is not None and extract[0] == sbi:
                return

    k.op("dve", lambda e: e.memset(Xre, 0.0), writes=["X"])
    k.op("dve", lambda e: e.memset(Xim, 0.0), writes=["X"])
    if use_cc:
        eb, ecol = divmod(LQ - 1, N)
        esb, ecl = divmod(ecol, Ns)
        nsq = int(round(math.log2(LQ)))
        assert 2 ** nsq == LQ
        p1r = smt("p1r"); p1i = smt("p1i"); p2r = smt("p2r"); p2i = smt("p2i")
        cp(p1r, ar, SM, ["pw"])
        cp(p1i, ai, SM, ["pw"])
        PW = ["pw", "sm"]

        def csq(outr, outi, inr, ini):
            tt(t_a, inr, inr, ALU.mult, PW, ["pw"])
            tt(t_b, ini, ini, ALU.mult, PW, ["pw"])
            tt(den, inr, ini, ALU.mult, PW, ["pw"])
            tt(outr, t_a, t_b, ALU.subtract, PW, ["pw"])
            ts(outi, den, 2.0, None, ALU.mult, None, PW, ["pw"])
        for _ in range(nsq):
            csq(p1r, p1i, p1r, p1i)
        csq(p2r, p2i, p1r, p1i)
        conv_some(4)
        per_blk = (len(pieces) + eb - 1) // max(eb, 1)
        for b in range(eb + 1):
            load_x_block(b, gate_key=("p1", b))
            conv_some(per_blk, gate=("p1", b))
            rms_to_nb("norm_mix_pre")
            u_proj()
            ssm_pass(False, extract=(esb, ecl) if b == eb else None)
    conv_some(len(pieces))
    if use_cc:
        d1 = k.dma("sp", cc_in[0:4, :].rearrange("a (p c) -> (a p) c", c=16), Xre, reads=["X"], writes=["cc_in"])
        d2 = k.dma("sp", cc_in[4:8, :].rearrange("a (p c) -> (a p) c", c=16), Xim, reads=["X"],
                   writes=["cc_in"])

        def ccfn(e):
            return e.collective_compute("AllGather", ALU.bypass,
                                        replica_groups=[list(range(g * NQ, (g + 1) * NQ)) for g in range(2)],
                                        ins=[cc_in], outs=[cc_out])
        ccop = k._record("pool", ccfn, ["cc_in"], ["cc_out"], True)
        ccop.needs_inc = True
        ccop.inc = 1
        G = k.sb([128, NQ, 2, 16], F32, "G")
        for r in range(NQ):
            for v in range(2):
                k.dma("sp", G[:, r, v, :],
                      cc_out[r * 8 + v * 4:r * 8 + v * 4 + 4, :].rearrange("a (p c) -> (a p) c", c=16),
                      reads=["cc_out"], writes=["G"])
        cco = k.sb([128, 24], F32, "cco")
        k.dma("sp", cco, bass.AP(ccoefd.tensor, 0, [[0, 128], [1, 24]]), writes=["cco"])
        k.op("dve", lambda e: e.memset(Xre, 0.0), reads=["X"], writes=["X"])
        k.op("dve", lambda e: e.memset(Xim, 0.0), reads=["X"], writes=["X"])
        cfr = smt("cfr"); cfi = smt("cfi")
        for r in range(NQ):
            ts(cfr, p1r, cco[:, r * 3 + 1:r * 3 + 2], cco[:, r * 3:r * 3 + 1], ALU.mult, ALU.add, PW + ["cco"], ["cf"])
            stt(cfr, p2r, cco[:, r * 3 + 2:r * 3 + 3], cfr, ALU.mult, ALU.add, PW + ["cco", "cf"], ["cf"])
            ts(cfi, p1i, cco[:, r * 3 + 1:r * 3 + 2], None, ALU.mult, None, PW + ["cco"], ["cf"])
            stt(cfi, p2i, cco[:, r * 3 + 2:r * 3 + 3], cfi, ALU.mult, ALU.add, PW + ["cco", "cf"], ["cf"])
            Sr = G[:, r, 0, :]
            Si = G[:, r, 1, :]
            tt(t_a, cfr, Sr, ALU.mult, ["cf", "G"], ["pw"])
            tt(Xre, Xre, t_a, ALU.add, ["X", "pw"], ["X"])
            tt(t_a, cfi, Si, ALU.mult, ["cf", "G"], ["pw"])
            tt(Xre, Xre, t_a, ALU.subtract, ["X", "pw"], ["X"])
            tt(t_a, cfr, Si, ALU.mult, ["cf", "G"], ["pw"])
            tt(Xim, Xim, t_a, ALU.add, ["X", "pw"], ["X"])
            tt(t_a, cfi, Sr, ALU.mult, ["cf", "G"], ["pw"])
            tt(Xim, Xim, t_a, ALU.add, ["X", "pw"], ["X"])

    k.handoff(SETUP_KEYS + STG_KEYS, R2_MIX + FO_KEYS)
    k.op("pool", lambda e: e.memset(hist.rearrange("p a b -> p (a b)"), 0.0), writes=["hist"])
    k.op("pool", lambda e: e.memset(carry.rearrange("p a b -> p (a b)"), 0.0), writes=[("carry", i) for i in range(88)])
    out_dmas = []
    for b in range(NB):
        load_x_block(b)
        k.dma("sp", maskb, bass.AP(maskd.tensor, b * N, [[0, 128], [1, N]]), writes=["maskb"])
        if stop == "x":
            k.emit()
            return nc
        rms_to_nb("norm_mix_pre")
        if stop == "norm":
            k.emit()
            return nc
        u_proj()
        if stop == "uproj":
            k.emit()
            return nc
        for cc in range(8):
            ci_ = cin[cc % 2]
            ck = f"cin{cc % 2}"
            psv, pkv = proj("w_in", 4 + cc, lambda kc: nb[:, kc, :], NBK, N)
            psg, pkg = proj("w_in", 12 + cc, lambda kc: nb[:, kc, :], NBK, N)
            act(tA, psg[:, 0:N], AF.Sigmoid, [pkg], ["tA"])
            cb_ = cinb[cc % 2]
            cp(cb_[:, 0:CK - 1], histb[:, cc, :], ["hist"], [ck], eng="act")
            tt(cb_[:, CK - 1:CK - 1 + N], psv[:, 0:N], tA, ALU.mult, [pkv, "tA"], [ck])
            cp(histb[:, cc, :], cb_[:, N:N + CK - 1], [ck], ["hist"], eng="act")
            a_ = acc[:, cc, :]
            psc_, pkc_ = newps()
            for half in range(2):
                wi_ = wbi[0] % NWB
                wbi[0] += 1
                k.dma("sp", wbuf[wi_], cdg[cc, :, half * 16:(half + 1) * 16, :], reads=[("cdg", cc, half)],
                      writes=[("wbuf", wi_)])
                for j in range(16):
                    k_ = half * 16 + j
                    if k_ >= CK:
                        continue
                    mm(psc_[:, 0:N], wbuf[wi_][:, j, :], cb_[:, k_:k_ + N], k_ == 0, k_ == CK - 1,
                       [("wbuf", wi_), ck], [pkc_])
            act(a_, psc_[:, 0:N], AF.Identity, [pkc_] + PVK, [("acc", cc)], bias=pv["conv_dw_b"][:, cc:cc + 1])
            s0 = sqb[0]
            s1 = sqb[1]
            act(s0, a_, AF.Copy, [("acc", cc)], [("sqb", 0)])
            act(s1, a_, AF.Square, [("acc", cc)], [("sqb", 1)])
            mm(pst1[:, 0:N], ones_b, s0, cc == 0, cc == 7, [("sqb", 0), "ones_b"], ["pst1"])
            mm(pst2[:, 0:N], ones_b, s1, cc == 0, cc == 7, [("sqb", 1), "ones_b"], ["pst2"])
        ts(rt[1], pst1[:, 0:N], 1.0 / CONVW, None, ALU.mult, None, ["pst1"], ["rt1"])
        tt(rt[2], rt[1], rt[1], ALU.mult, ["rt1"], ["rt2"])
        stt(rt[2], pst2[:, 0:N], 1.0 / CONVW, rt[2], ALU.mult, ALU.subtract, ["pst2", "rt2"], ["rt2"])
        ts(rt[2], rt[2], LNEPS, None, ALU.add, None, ["rt2"], ["rt2"])
        act(rt[2], rt[2], AF.Sqrt, ["rt2"], ["rt2"])
        k.op("dve", lambda e: e.reciprocal(out=rt[2], in_=rt[2]), ["rt2"], ["rt2"])
        for cc in range(8):
            a_ = acc[:, cc, :]
            tt(a_, a_, rt[1], ALU.subtract, [("acc", cc), "rt1"], [("acc", cc)])
            tt(a_, a_, rt[2], ALU.mult, [("acc", cc), "rt2"], [("acc", cc)])
            act(csil[:, cc, :], a_, AF.Silu, [("acc", cc)] + PVK, [("csil", cc)],
                scale=pv["conv_ln_g"][:, cc:cc + 1], bias=pv["conv_ln_b"][:, cc:cc + 1])
        if stop == "convb":
            k.emit()
            return nc
        ssm_pass(True)
        if stop == "ssmb":
            k.emit()
            return nc
        for ch in range(4):
            cp(u_b[:, ch, :], u_f[:, ch, :], [("u_f", ch)], [("u_b", ch)], eng="act")
        UBK = [("u_b", i) for i in range(4)]
        for oc in range(4):
            ps, pk = proj("w_ssm_glu", oc, lambda kc: u_b[:, kc, :], UBK, N)
            act(tA, ps[:, 0:N], AF.Sigmoid, [pk], ["tA"])
            tt(ysb[:, oc, :], u_f[:, oc, :], tA, ALU.mult, [("u_f", oc), "tA"], [("ysb", oc)])
        YK = [("ysb", i) for i in range(4)]
        CSK = [("csil", i) for i in range(8)]
        for oc in range(16):
            psa, pka = proj("w_in", 20 + oc, lambda kc: nb[:, kc, :], NBK, N)
            act(tA, psa[:, 0:N], AF.Sigmoid, [pka], ["tA"])
            psb, pkb = proj("w_ssm_proj", oc, lambda kc: ysb[:, kc, :], YK, N)
            tt(mtmp, psb[:, 0:N], tA, ALU.mult, [pkb, "tA"], ["mtmp"])
            psc, pkc = proj("w_in", 36 + oc, lambda kc: nb[:, kc, :], NBK, N)
            act(tB, psc[:, 0:N], AF.Sigmoid, [pkc], ["tB"])
            psd, pkd = proj("w_conv_proj", oc, lambda kc: csil[:, kc, :], CSK, N)
            tt(tB, psd[:, 0:N], tB, ALU.mult, [pkd, "tB"], ["tB"])
            tt(merged[:, oc, :], mtmp, tB, ALU.add, ["mtmp", "tB"], [("merged", oc)])
        MK = [("merged", i) for i in range(16)]
        if stop == "merge":
            k.emit()
            return nc
        k.handoff(R1_MIX, MO_KEYS)
        for oc in range(16):
            ps, pk = proj("w_mix_out", oc, lambda kc: merged[:, kc, :], MK, N)
            act(mo_f[:, oc, :], ps[:, 0:N], AF.Copy, [pk], [("mo", oc)])
            s = sqb[oc % 2]
            act(s, ps[:, 0:N], AF.Square, [pk], [("sqb", oc % 2)])
            mm(pst1[:, 0:N], ones_b, s, oc == 0, oc == 15, [("sqb", oc % 2), "ones_b"], ["pst1"])
        rstd_of(pst1[:, 0:N], "pst1", 1.0 / D, EPS, rt[0], "rt0")
        for oc in range(16):
            stt(mo_f[:, oc, :], mo_f[:, oc, :], pv["norm_mix_post"][:, oc:oc + 1], rt[0], ALU.mult, ALU.mult,
                [("mo", oc), "rt0"] + PVK, [("mo", oc)])
            tt(h[:, oc, :], h[:, oc, :], mo_f[:, oc, :], ALU.add, [("h", oc), ("mo", oc)], [("h", oc)])
        rms_to_nb("norm_ffn_pre", extra_mask=True)
        if stop == "mix":
            k.emit()
            return nc
        k.handoff(MO_KEYS, HID_KEYS)
        k.handoff(R2_MIX, FO_KEYS)
        for fc in range(44):
            tg = rt[1]
            tv = rt[2]
            for (ocx, tbuf, tkey) in ((fc, tg, "rt1"), (44 + fc, tv, "rt2")):
                ps, pk = proj("w_ffn_up", ocx, lambda kc: nb[:, kc, :], NBK, N)
                w0 = ffnw[:, 0, ocx:ocx + 1]
                w1 = ffnw[:, 1, ocx:ocx + 1]
                w2 = ffnw[:, 2, ocx:ocx + 1]
                act(tbuf, ps[:, 0:N], AF.Identity, [pk] + PVK, [tkey], scale=w2,
                    bias=pv["ffn_dw_b"][:, ocx:ocx + 1])
                stt(tbuf[:, 1:N], ps[:, 0:N - 1], w1, tbuf[:, 1:N], ALU.mult, ALU.add, [pk, tkey] + PVK, [tkey])
                stt(tbuf[:, 2:N], ps[:, 0:N - 2], w0, tbuf[:, 2:N], ALU.mult, ALU.add, [pk, tkey] + PVK, [tkey])
                stt(tbuf[:, 0:1], carry[:, ocx, 1:2], w1, tbuf[:, 0:1], ALU.mult, ALU.add,
                    [("carry", ocx), tkey] + PVK, [tkey])
                stt(tbuf[:, 0:2], carry[:, ocx, 0:2], w0, tbuf[:, 0:2], ALU.mult, ALU.add,
                    [("carry", ocx), tkey] + PVK, [tkey])
                act(carry[:, ocx, :], ps[:, N - 2:N], AF.Copy, [pk], [("carry", ocx)])
            act(tg, tg, AF.Gelu_apprx_tanh, ["rt1"], ["rt1"])
            tt(hid[:, fc, :], tg, tv, ALU.mult, ["rt1", "rt2"], [("hid", fc)])
        for oc in range(16):
            ps, pk = proj("w_ffn_down", oc, lambda kc: hid[:, kc, :], HID_KEYS, N)
            act(fo_f[:, oc, :], ps[:, 0:N], AF.Copy, [pk], [("fo", oc)])
            s = sqb[oc % 2]
            act(s, ps[:, 0:N], AF.Square, [pk], [("sqb", oc % 2)])
            mm(pst1[:, 0:N], ones_b, s, oc == 0, oc == 15, [("sqb", oc % 2), "ones_b"], ["pst1"])
        rstd_of(pst1[:, 0:N], "pst1", 1.0 / D, EPS, rt[0], "rt0")
        for oc in range(16):
            stt(fo_f[:, oc, :], fo_f[:, oc, :], pv["norm_ffn_post"][:, oc:oc + 1], rt[0], ALU.mult, ALU.mult,
                [("fo", oc), "rt0"] + PVK, [("fo", oc)])
            tt(h[:, oc, :], h[:, oc, :], fo_f[:, oc, :], ALU.add, [("h", oc), ("fo", oc)], [("h", oc)])
        for t0 in range(0, N, 128):
            rows = min(128, N - t0)
            g0 = b * N + t0
            lo = max(g0, H)
            hi = g0 + rows
            if hi <= lo:
                continue
            for c4 in range(4):
                ps, pk = newps()
                for c in range(4):
                    cidx = c4 * 4 + c
                    tr(ps[0:rows, c * 128:(c + 1) * 128], h[:, cidx, t0:t0 + rows], ident, [("h", cidx), "ident"], [pk])
                cp(xst[0:rows, c4 * 512:(c4 + 1) * 512], ps[0:rows, 0:512], [pk], ["xst"],
                   eng=("act" if c4 % 2 else "dve"))
            od = k.dma("sp", outd[lo - H:hi - H, :], xst[lo - g0:rows, :], reads=["xst"], writes=["out"])
            out_dmas.append(od)
        k.handoff(HID_KEYS, R1_MIX)
        k.handoff(FO_KEYS, R2_MIX)
    k.emit(final_wait_ops=out_dmas)
    return nc


def make_in_maps(inputs, NB, N, H, LQ, ncores=8):
    x = np.asarray(inputs["x"], dtype=np.float32)
    B, S, _ = x.shape
    nq = ncores // B
    assert S == nq * LQ
    W = NB * N
    meta = np.asarray(inputs["meta_tokens"], dtype=np.float32)
    shared = {}
    for n in WSPEC:
        shared[n] = np.ascontiguousarray(np.asarray(inputs[n], dtype=np.float32)[0])
    for n, s in VEC_INPUTS.items():
        shared[n] = np.ascontiguousarray(np.asarray(inputs[n], dtype=np.float32).reshape(s // 128, 128))
    shared["log_step"] = np.ascontiguousarray(np.asarray(inputs["log_step"], dtype=np.float32).reshape(16, 2))
    for n in ("ssm_b_re", "ssm_b_im", "ssm_c_re", "ssm_c_im"):
        shared[n] = np.ascontiguousarray(np.asarray(inputs[n], dtype=np.float32)[0])
    shared["conv_dw_w"] = np.ascontiguousarray(np.asarray(inputs["conv_dw_w"], dtype=np.float32)[0])
    shared["ffn_dw_w"] = np.ascontiguousarray(np.asarray(inputs["ffn_dw_w"], dtype=np.float32).reshape(3, 88, 128))
    maps = []
    for core in range(ncores):
        b, q = divmod(core, nq)
        hseq = np.concatenate([meta, x[b]], axis=0)
        a = NMETA + LQ * q - H
        xw = np.zeros((W, D), np.float32)
        mask = np.ones((1, W), np.float32)
        lo = max(a, 0)
        xw[lo - a:W] = hseq[lo:a + W]
        if a < 0:
            mask[0, 0:-a] = 0.0
        cco = np.zeros((1, 24), np.float32)
        for r in range(nq):
            e = q - 1 - r
            if 0 <= e <= 2:
                cco[0, r * 3 + e] = 1.0
        m = dict(shared)
        m["xw"] = xw
        m["mask"] = mask
        m["ccoef"] = cco
        maps.append(m)
    return maps


CFG = dict(NB=10, N=414, H=44, LQ=4096)


def kernel(**inputs):
    cfg = CFG
    nc = build(**cfg)
    maps = make_in_maps(inputs, **cfg)
    res = run_bass_kernel_spmd(nc, maps, core_ids=list(range(8)))
    x = inputs["x"]
    B, S, _ = x.shape
    out = np.empty((B, S, D), np.float32)
    nq = 8 // B
    for core in range(8):
        b, q = divmod(core, nq)
        out[b, q * cfg["LQ"]:(q + 1) * cfg["LQ"]] = res.results[core]["out"]
    return out
```

```python
import contextlib
import math
import numpy as np
import concourse.bass as bass
import concourse.mybir as mybir
from concourse.bass_utils import run_bass_kernel_spmd

F32 = mybir.dt.float32
BF16 = mybir.dt.bfloat16
I32 = mybir.dt.int32
AF = mybir.ActivationFunctionType
ALU = mybir.AluOpType

ENGS = ("pe", "act", "dve", "pool", "sp")
SEM_LIMIT = 30000
SAME_ENGINE_SYNC = {"act", "dve", "pool"}

D = 2048
NMETA = 16
SSMW = 512
CONVW = 1024
CK = 31
DFF = 5632
EPS = 1e-6
LNEPS = 1e-5
TWO_PI = 2.0 * math.pi


class Op:
    __slots__ = ("eng", "fn", "deps", "is_dma", "needs_inc", "tok", "dsem", "inc")

    def __init__(self, eng, fn, deps, is_dma):
        self.eng = eng
        self.fn = fn
        self.deps = deps
        self.is_dma = is_dma
        self.needs_inc = False
        self.tok = None
        self.dsem = None
        self.inc = 16 if is_dma else 1


class Res:
    __slots__ = ("lastw", "readers")

    def __init__(self):
        self.lastw = None
        self.readers = []


class K:
    def __init__(self, nc, ndma_sems=8):
        self.nc = nc
        self.ops = {e: [] for e in ENGS}
        self.res = {}
        self.ndma = ndma_sems
        self.ntiles = 0

    def sb(self, shape, dtype, name=None):
        self.ntiles += 1
        return self.nc.alloc_sbuf_tensor(name or f"sb{self.ntiles}", list(shape), dtype).ap()

    def ps(self, shape, dtype=F32, name=None):
        self.ntiles += 1
        return self.nc.alloc_psum_tensor(name or f"ps{self.ntiles}", list(shape), dtype).ap()

    def _r(self, key):
        r = self.res.get(key)
        if r is None:
            r = self.res[key] = Res()
        return r

    def handoff(self, from_keys, to_keys):
        acc = []
        for fk in from_keys:
            r = self._r(fk)
            if r.lastw is not None:
                acc.append(r.lastw)
            acc.extend(r.readers)
        for tk in to_keys:
            self._r(tk).readers.extend(acc)

    @staticmethod
    def _excl(key):
        if isinstance(key, tuple):
            return key[0] == "pm"
        return key in ("pst1", "pst2", "py")

    def _record(self, eng, fn, reads, writes, is_dma):
        xr = [x for x in reads if self._excl(x)]
        if xr:
            writes = list(writes) + [x for x in xr if x not in writes]
            reads = [x for x in reads if not self._excl(x)]
        deps = []
        for key in reads:
            r = self._r(key)
            if r.lastw is not None:
                deps.append(r.lastw)
        for key in writes:
            r = self._r(key)
            if r.lastw is not None:
                deps.append(r.lastw)
            deps.extend(r.readers)
        op = Op(eng, fn, deps, is_dma)
        for d in deps:
            d.needs_inc = True
        self.ops[eng].append(op)
        for key in writes:
            r = self._r(key)
            r.lastw = op
            r.readers = []
        for key in reads:
            r = self._r(key)
            r.readers = [x for x in r.readers if x.is_dma or x.eng != eng or is_dma]
            r.readers.append(op)
        return op

    def op(self, eng, fn, reads=(), writes=()):
        return self._record(eng, fn, reads, writes, False)

    def dma(self, eng, out, in_, reads=(), writes=(), **kw):
        def fn(e):
            return e.dma_start(out=out, in_=in_, **kw)
        op = self._record(eng, fn, reads, writes, True)
        op.needs_inc = True
        return op

    def emit(self, final_wait_ops=()):
        nc = self.nc
        es = contextlib.ExitStack()
        with es:
            def newsem(name):
                return es.enter_context(nc.semaphore(name))

            for e in ENGS:
                cur = None
                cnt = 0
                nsem = 0
                dcount = 0
                dsems = None
                dvals = None
                for op in self.ops[e]:
                    if op.is_dma:
                        if dsems is None:
                            dsems = [newsem(f"d_{e}_{i}") for i in range(self.ndma)]
                            dvals = [0] * self.ndma
                        slot = dcount % self.ndma
                        dcount += 1
                        dvals[slot] += op.inc
                        op.tok = (dsems[slot], dvals[slot])
                        op.dsem = (dsems[slot], dvals[slot] - op.inc)
                    elif op.needs_inc:
                        if cur is None or cnt >= SEM_LIMIT:
                            cur = newsem(f"c_{e}_{nsem}")
                            nsem += 1
                            cnt = 0
                        cnt += 1
                        op.tok = (cur, cnt)
            final_toks = [o.tok for o in final_wait_ops]

            def run_engine(e, h):
                waited = {}

                def wait(tok):
                    s, v = tok
                    if waited.get(id(s), 0) >= v:
                        return
                    waited[id(s)] = v
                    h.wait_ge(s, v)

                for op in self.ops[e]:
                    for d in op.deps:
                        if d.eng == e and (not d.is_dma) and e not in SAME_ENGINE_SYNC:
                            continue
                        wait(d.tok)
                    if op.is_dma:
                        s, prev = op.dsem
                        if prev > 0:
                            wait((s, prev))
                        ins = op.fn(h)
                        ins.then_inc(op.tok[0], op.inc)
                    else:
                        ins = op.fn(h)
                        if op.needs_inc:
                            ins.then_inc(op.tok[0], 1)
                if e == "sp":
                    for t in final_toks:
                        wait(t)

            with nc.Block() as block:
                @block.tensor
                def _(h):
                    run_engine("pe", h)

                @block.scalar
                def _(h):
                    run_engine("act", h)

                @block.vector
                def _(h):
                    run_engine("dve", h)

                @block.gpsimd
                def _(h):
                    run_engine("pool", h)

                @block.sync
                def _(h):
                    run_engine("sp", h)


WSPEC = {
    "w_in": (D, SSMW + 2 * CONVW + 2 * D),
    "w_ssm_glu": (SSMW, SSMW),
    "w_ssm_proj": (SSMW, D),
    "w_conv_proj": (CONVW, D),
    "w_mix_out": (D, D),
    "w_ffn_up": (D, 2 * DFF),
    "w_ffn_down": (DFF, D),
}
VEC_INPUTS = {
    "norm_mix_pre": D, "norm_mix_post": D, "norm_ffn_pre": D, "norm_ffn_post": D,
    "conv_dw_b": CONVW, "conv_ln_g": CONVW, "conv_ln_b": CONVW,
    "ffn_dw_b": 2 * DFF, "ssm_d": SSMW, "lam_re": 2048, "lam_im": 2048,
}


def build(NB, N, H, LQ, ncores=8, use_cc=True, stop=None):
    W = NB * N
    assert W == H + LQ and N % 2 == 0
    Ns = N // 2
    nc = bass.Bass("TRN2", target_bir_lowering=False)
    k = K(nc)
    KC = D // 128

    def din(name, shape):
        return nc.dram_tensor(name, list(shape), F32, kind="ExternalInput").ap()

    xw = din("xw", [W, D])
    maskd = din("mask", [1, W])
    ccoefd = din("ccoef", [1, 24])
    wd = {n: din(n, list(s)) for n, s in WSPEC.items()}
    vd = {n: din(n, [s // 128, 128]) for n, s in VEC_INPUTS.items()}
    log_step_d = din("log_step", [16, 2])
    b_re_d = din("ssm_b_re", [32, 64, 16])
    b_im_d = din("ssm_b_im", [32, 64, 16])
    c_re_d = din("ssm_c_re", [32, 16, 64])
    c_im_d = din("ssm_c_im", [32, 16, 64])
    conv_w_d = din("conv_dw_w", [CK, CONVW])
    ffn_w_d = din("ffn_dw_w", [3, 88, 128])
    outd = nc.dram_tensor("out", [LQ, D], F32, kind="ExternalOutput").ap()
    wscr = {n: nc.dram_tensor("scr_" + n, [s[1] // 128, 128, s[0] // 128, 128], BF16,
                              kind="Internal").ap() for n, s in WSPEC.items()}
    NQ = ncores // 2
    cdg = nc.dram_tensor("scr_convdiag", [8, 128, 32, 128], BF16, kind="Internal").ap()
    cc_in = nc.dram_tensor("cc_in", [8, 512], F32, kind="Internal").ap()
    cc_out = nc.dram_tensor("cc_out", [NQ * 8, 512], F32, kind="Internal", addr_space="Local").ap()

    def act(out, in_, func, reads, writes, scale=None, bias=None):
        kw = {}
        if scale is not None:
            kw["scale"] = scale
        if bias is not None:
            kw["bias"] = bias
        return k.op("act", lambda e: e.activation(out=out, in_=in_, func=func, **kw), reads, writes)

    def tt(out, a, b, op, reads, writes, eng="dve"):
        return k.op(eng, lambda e: e.tensor_tensor(out=out, in0=a, in1=b, op=op), reads, writes)

    def ts(out, a, s1, s2, op0, op1, reads, writes, eng="dve"):
        if op1 is None:
            return k.op(eng, lambda e: e.tensor_scalar(out=out, in0=a, scalar1=s1, scalar2=None, op0=op0),
                        reads, writes)
        return k.op(eng, lambda e: e.tensor_scalar(out=out, in0=a, scalar1=s1, scalar2=s2, op0=op0, op1=op1),
                    reads, writes)

    def stt(out, a, s, b, op0, op1, reads, writes):
        return k.op("dve", lambda e: e.scalar_tensor_tensor(out=out, in0=a, scalar=s, in1=b, op0=op0, op1=op1),
                    reads, writes)

    def cp(out, in_, reads, writes, eng="dve"):
        if eng == "act":
            return act(out, in_, AF.Copy, reads, writes)
        return k.op(eng, lambda e: e.tensor_copy(out=out, in_=in_), reads, writes)

    def mm(out, lhsT, rhs, start, stop, reads, writes):
        return k.op("pe", lambda e: e.matmul(out, lhsT=lhsT, rhs=rhs, start=start, stop=stop), reads, writes)

    def tr(out, in_, ident_ap, reads, writes):
        return k.op("pe", lambda e: e.transpose(out=out, in_=in_, identity=ident_ap), reads, writes)

    NPM = 5
    pm = [k.ps([128, 512], F32, name=f"pm{i}") for i in range(NPM)]
    pst1 = k.ps([128, 512], F32, name="pst1")
    pst2 = k.ps([128, 512], F32, name="pst2")
    py = k.ps([128, 512], F32, name="py")
    pmi = [0]

    def newps():
        i = pmi[0] % NPM
        pmi[0] += 1
        return pm[i], ("pm", i)

    ident = k.sb([128, 128], F32, "ident")
    k.op("pool", lambda e: e.memset(ident, 0.0), writes=["ident"])
    k.op("pool", lambda e: e.affine_select(out=ident, in_=ident, pattern=[[-1, 128]], compare_op=ALU.not_equal,
                                           fill=1.0, base=0, channel_multiplier=1), reads=["ident"], writes=["ident"])
    ones_b = k.sb([128, 128], BF16, "ones_b")
    k.op("pool", lambda e: e.memset(ones_b, 1.0), writes=["ones_b"])

    R1F = (44 * N * 2 + 3) // 4
    R1F = max(R1F, 6400)
    R1 = k.sb([128, R1F], F32, "R1")
    R2F = max(16 * N, 6400)
    R2 = k.sb([128, R2F], F32, "R2")

    class Carver:
        def __init__(self, region):
            self.r = region
            self.off = 0

        def f32(self, a, n):
            v = self.r[:, self.off:self.off + a * n]
            self.off += a * n
            return v.rearrange("p (a n) -> p a n", n=n) if a > 1 else v

        def bf(self, a, n):
            words = (a * n + 1) // 2
            v = self.r[:, self.off:self.off + words].bitcast(BF16)[:, 0:a * n]
            self.off += words
            return v.rearrange("p (a n) -> p a n", n=n) if a > 1 else v

    c1 = Carver(R1)
    u_f = c1.f32(4, N)
    u_b = c1.bf(4, N)
    cin = [c1.f32(1, N + CK - 1) for _ in range(2)]
    cinb = [c_.bitcast(BF16)[:, 0:N + CK - 1] for c_ in cin]
    csil = c1.bf(8, N)
    stmp = [c1.f32(1, Ns) for _ in range(6)]
    qb = [c1.bf(1, Ns) for _ in range(4)]
    ysb = c1.bf(4, N)
    tA = c1.f32(1, N)
    tB = c1.f32(1, N)
    mtmp = c1.f32(1, N)
    assert c1.off <= R1F, (c1.off, R1F)
    mo_f = R1[:, 0:16 * N].rearrange("p (a n) -> p a n", n=N) if 16 * N <= R1F else None
    assert mo_f is not None
    hid = R1[:, 0:22 * N].bitcast(BF16).rearrange("p (a n) -> p a n", n=N)
    stg_f = [R2[:, i * 2048:(i + 1) * 2048] for i in range(2)]
    stg_b = [R2[:, 4096 + i * 1024:4096 + (i + 1) * 1024].bitcast(BF16) for i in range(2)]
    assert 4096 + 2048 <= R2F
    c2 = Carver(R2)
    acc = c2.f32(8, N)
    merged = c2.bf(16, N)
    assert c2.off <= R2F
    fo_f = R2[:, 0:16 * N].rearrange("p (a n) -> p a n", n=N)
    R1_MIX = ["u_f", "u_b", "cin0", "cin1", "csil", "stmp", "qb", "ysb", "tA", "tB", "mtmp"] + \
             [("u_f", i) for i in range(4)] + [("u_b", i) for i in range(4)] + [("csil", i) for i in range(8)] + \
             [("ysb", i) for i in range(4)] + [("stmp", i) for i in range(6)] + [("qb", i) for i in range(4)]
    MO_KEYS = [("mo", i) for i in range(16)]
    HID_KEYS = [("hid", i) for i in range(44)]
    R2_MIX = [("acc", i) for i in range(8)] + [("merged", i) for i in range(16)]
    FO_KEYS = [("fo", i) for i in range(16)]
    STG_KEYS = ["stgf0", "stgf1", "stgb0", "stgb1"]

    csu = [0]

    def sb2(shape, dtype=F32):
        parts = shape[0]
        n = 1
        for d_ in shape[1:]:
            n *= d_
        v = R2[0:parts, csu[0]:csu[0] + n]
        csu[0] += n
        assert csu[0] <= R2F, csu[0]
        if dtype != F32:
            v = v.bitcast(dtype)
        if len(shape) == 3:
            v = v.rearrange("p (a b) -> p a b", b=shape[2])
        return v

    h = k.sb([128, 16, N], F32, "h")
    nb = k.sb([128, 16, N], BF16, "nb")
    sqb = [k.sb([128, N], BF16, f"sqb{i}") for i in range(2)]
    rt = [k.sb([128, N], F32, f"rt{i}") for i in range(5)]
    maskb = k.sb([128, N], F32, "maskb")
    cosT = k.sb([128, 16, Ns], F32, "cosT")
    sinT = k.sb([128, 16, Ns], F32, "sinT")
    CT = k.sb([128, 16, 3, 128], BF16, "CT")
    BT = k.sb([128, 16, 2, 128], BF16, "BT")
    NWB = 6
    wbuf = [k.sb([128, 16, 128], BF16, f"wbuf{i}") for i in range(NWB)]
    xst = k.sb([128, D], F32, "xst")
    stmp2 = [xst[:, i * Ns:(i + 1) * Ns] for i in range(6)]
    qb2 = [xst[:, 6 * Ns + i * ((Ns + 1) // 2):6 * Ns + (i + 1) * ((Ns + 1) // 2)].bitcast(BF16)[:, 0:Ns] for i in range(4)]
    assert 6 * Ns + 4 * ((Ns + 1) // 2) <= D
    SET2 = [("stmp2", i) for i in range(6)] + [("qb2", i) for i in range(4)]
    hist = k.sb([128, 8, CK - 1], F32, "hist")
    carry = k.sb([128, 88, 2], F32, "carry")
    Xre = k.sb([128, 16], F32, "Xre")
    Xim = k.sb([128, 16], F32, "Xim")
    Wend = k.sb([128, 2, 16], F32, "Wend")
    hx = [k.sb([128, 16], F32, f"hx{i}") for i in range(4)]

    pv = {}

    def load_vec(name):
        n = VEC_INPUTS[name] // 128
        t_in = k.sb([n, 128], F32, "vin_" + name)
        k.dma("sp", t_in, vd[name], writes=["vin_" + name])
        ps, pk = newps()
        tr(ps[:, 0:n], t_in, ident[0:n, 0:n], ["vin_" + name, "ident"], [pk])
        t = k.sb([128, n], F32, "pv_" + name)
        cp(t, ps[:, 0:n], [pk], ["pv_" + name])
        pv[name] = t

    for name in VEC_INPUTS:
        load_vec(name)
    PVK = ["pv_" + n for n in VEC_INPUTS]

    cw_in = sb2([CK, CONVW])
    k.dma("sp", cw_in, conv_w_d, writes=["cw_in"])
    convw = k.sb([128, 8, CK], F32, "convw")
    ps, pk = newps()
    for cc in range(8):
        tr(ps[:, cc * CK:(cc + 1) * CK], cw_in[:, cc * 128:(cc + 1) * 128], ident[0:CK, 0:CK], ["cw_in", "ident"], [pk])
    cp(convw.rearrange("p a b -> p (a b)"), ps[:, 0:8 * CK], [pk], ["convw"])
    fw_in = sb2([88, 3, 128])
    k.dma("sp", fw_in, ffn_w_d.rearrange("k r c -> r k c"), writes=["fw_in"])
    ffnw = k.sb([128, 3, 88], F32, "ffnw")
    ps, pk = newps()
    for kk in range(3):
        tr(ps[:, kk * 88:(kk + 1) * 88], fw_in[:, kk, :], ident[0:88, 0:88], ["fw_in", "ident"], [pk])
    cp(ffnw.rearrange("p a b -> p (a b)"), ps[:, 0:264], [pk], ["ffnw"])
    PVK += ["convw", "ffnw"]
    histb = hist.rearrange("p a b -> p (a b)").bitcast(BF16)[:, 0:8 * (CK - 1)].rearrange("p (a b) -> p a b", b=CK - 1)
    for cc in range(8):
        for half in range(2):
            stgw = wbuf[half]
            for j in range(16):
                k_ = half * 16 + j
                if k_ < CK:
                    k.op("pool", lambda e, o_=stgw[:, j, :], s_=convw[:, cc, k_:k_ + 1]: e.tensor_scalar(
                        out=o_, in0=ident, scalar1=s_, scalar2=1.0, op0=ALU.mult, op1=ALU.mult),
                        ["ident", "convw"], [("wbuf", half)])
                else:
                    k.op("pool", lambda e, o_=stgw[:, j, :]: e.memset(o_, 0.0), [], [("wbuf", half)])
            k.dma("sp", cdg[cc, :, half * 16:(half + 1) * 16, :], stgw, reads=[("wbuf", half)],
                  writes=[("cdg", cc, half)])

    if stop == "vec":
        k.emit()
        return nc
    sm = {}

    def smt(name, shape=(128, 16)):
        sm[name] = k.sb(list(shape), F32, "sm_" + name)
        return sm[name]

    ls_in = k.sb([16, 2], F32, "ls_in")
    k.dma("sp", ls_in, log_step_d, writes=["ls_in"])
    ls_x = k.sb([16, 2, 64], F32, "ls_x")
    cp(ls_x, ls_in.unsqueeze(2).to_broadcast([16, 2, 64]), ["ls_in"], ["ls_x"])
    ps, pk = newps()
    tr(ps[:, 0:16], ls_x.rearrange("a g p -> a (g p)"), ident[0:16, 0:16], ["ls_x", "ident"], [pk])
    lsT = smt("lsT")
    cp(lsT, ps[:, 0:16], [pk], ["sm"])
    SM = ["sm"] + PVK
    lr = pv["lam_re"]
    li = pv["lam_im"]
    step = smt("step")
    act(step, lsT, AF.Exp, SM, ["sm"])
    lrs = smt("lrs")
    tt(lrs, lr, step, ALU.mult, SM, ["sm"])
    mag = smt("mag")
    act(mag, lrs, AF.Exp, SM, ["sm"])
    th = smt("th")
    tt(th, li, step, ALU.mult, SM, ["sm"])

    sc_tmp = {}

    def sincos(theta, cos_out, sin_out, shape, key, temps=None):
        if temps is not None:
            t0, t1, ti = temps
        else:
            if shape not in sc_tmp:
                sc_tmp[shape] = (sb2(list(shape)), sb2(list(shape)), sb2(list(shape), I32))
            t0, t1, ti = sc_tmp[shape]
        for shift, outp in ((0.25, cos_out), (0.0, sin_out)):
            ts(t0, theta, 1.0 / TWO_PI, 0.5 + shift, ALU.mult, ALU.add, [key], [key + "_t0"])
            cp(ti, t0, [key + "_t0"], [key + "_ti"])
            cp(t1, ti, [key + "_ti"], [key + "_t1"])
            tt(t0, t0, t1, ALU.subtract, [key + "_t0", key + "_t1"], [key + "_t0"])
            ts(t1, t0, 0.0, None, ALU.is_lt, None, [key + "_t0"], [key + "_t1"])
            tt(t0, t0, t1, ALU.add, [key + "_t0", key + "_t1"], [key + "_t0"])
            ts(t0, t0, -0.5, -0.4999999, ALU.add, ALU.max, [key + "_t0"], [key + "_t0"])
            ts(t0, t0, 0.4999999, None, ALU.min, None, [key + "_t0"], [key + "_t0"])
            act(outp, t0, AF.Sin, [key + "_t0"], [key], scale=TWO_PI)

    cth = smt("cth")
    sth = smt("sth")
    sincos(th, cth, sth, (128, 16), "sm")
    ar = smt("ar")
    ai = smt("ai")
    tt(ar, mag, cth, ALU.mult, SM, ["sm"])
    tt(ai, mag, sth, ALU.mult, SM, ["sm"])
    den = smt("den")
    t_a = smt("t_a")
    t_b = smt("t_b")
    tt(den, lr, lr, ALU.mult, SM, ["sm"])
    tt(t_a, li, li, ALU.mult, SM, ["sm"])
    tt(den, den, t_a, ALU.add, SM, ["sm"])
    rden = smt("rden")
    k.op("dve", lambda e: e.reciprocal(out=rden, in_=den), SM, ["sm"])
    am1 = smt("am1")
    ts(am1, ar, -1.0, None, ALU.add, None, SM, ["sm"])
    cr = smt("cr")
    ci = smt("ci")
    tt(t_a, am1, lr, ALU.mult, SM, ["sm"])
    tt(t_b, ai, li, ALU.mult, SM, ["sm"])
    tt(t_a, t_a, t_b, ALU.add, SM, ["sm"])
    tt(cr, t_a, rden, ALU.mult, SM, ["sm"])
    tt(t_a, ai, lr, ALU.mult, SM, ["sm"])
    tt(t_b, am1, li, ALU.mult, SM, ["sm"])
    tt(t_a, t_a, t_b, ALU.subtract, SM, ["sm"])
    tt(ci, t_a, rden, ALU.mult, SM, ["sm"])

    b_re = sb2([128, 16, 16])
    b_im = sb2([128, 16, 16])
    for t_, d_ in ((b_re, b_re_d), (b_im, b_im_d)):
        src = bass.AP(d_.tensor, 0, [[16, 128], [2048, 16], [1, 16]])
        k.dma("sp", t_, src, writes=["sm"], allow_slow_non_contiguous=True)
    crb = cr.unsqueeze(2).to_broadcast([128, 16, 16])
    cib = ci.unsqueeze(2).to_broadcast([128, 16, 16])
    Bb_re = sb2([128, 16, 16])
    Bb_im = sb2([128, 16, 16])
    t3a = sb2([128, 16, 16])
    tt(Bb_re, b_re, crb, ALU.mult, SM, ["sm"])
    tt(t3a, b_im, cib, ALU.mult, SM, ["sm"])
    tt(Bb_re, Bb_re, t3a, ALU.subtract, SM, ["sm"])
    tt(Bb_im, b_im, crb, ALU.mult, SM, ["sm"])
    tt(t3a, b_re, cib, ALU.mult, SM, ["sm"])
    tt(Bb_im, Bb_im, t3a, ALU.add, SM, ["sm"])
    k.op("pool", lambda e: e.memset(BT.rearrange("p a b c -> p (a b c)"), 0.0), writes=["BT"])
    xb = sb2([128, 128])
    for P in range(16):
        m = P % 4
        for v, Bsrc in ((0, Bb_re), (1, Bb_im)):
            k.op("dve", lambda e: e.memset(xb, 0.0), writes=["xb"])
            for g2 in range(2):
                col = m * 32 + g2 * 16
                cp(xb[g2 * 64:(g2 + 1) * 64, col:col + 16], Bsrc[g2 * 64:(g2 + 1) * 64, P, :], SM, ["xb"])
            ps, pk = newps()
            tr(ps[:, 0:128], xb, ident, ["xb", "ident"], [pk])
            hb = (m // 2) * 64
            cp(BT[hb:hb + 64, P, v, :], ps[hb:hb + 64, 0:128], [pk], ["BT"])
    k.op("pool", lambda e: e.memset(CT.rearrange("p a b c -> p (a b c)"), 0.0), writes=["CT"])
    crow = sb2([128, 2, 128])
    csb = sb2([128, 8, 16])
    for v_re, d_ in ((True, c_re_d), (False, c_im_d)):
        for g2 in range(2):
            src = bass.AP(d_.tensor, g2 * 1024, [[2048, 16], [64, 16], [1, 64]])
            for half in range(2):
                for P8 in range(8):
                    srch = bass.AP(d_.tensor, g2 * 1024 + (half * 8 + P8) * 2048, [[64, 16], [1, 64]])
                    k.dma("sp", crow[P8 * 16:(P8 + 1) * 16, half, g2 * 64:(g2 + 1) * 64], srch, writes=["crow"])
        for half in range(2):
            ps, pk = newps()
            tr(ps[:, 0:128], crow[:, half, :], ident, ["crow", "ident"], [pk])
            cp(csb.rearrange("p a b -> p (a b)"), ps[:, 0:128], [pk], ["csb"])
            for P8 in range(8):
                P = half * 8 + P8
                m = P % 4
                for g2 in range(2):
                    col = m * 32 + g2 * 16
                    src_ = csb[g2 * 64:(g2 + 1) * 64, P8, :]
                    if v_re:
                        cp(CT[g2 * 64:(g2 + 1) * 64, P, 0, col:col + 16], src_, ["csb"], ["CT"])
                        ts(CT[g2 * 64:(g2 + 1) * 64, P, 1, col:col + 16], src_, -1.0, None, ALU.mult, None,
                           ["csb"], ["CT"])
                    else:
                        ts(CT[g2 * 64:(g2 + 1) * 64, P, 2, col:col + 16], src_, -1.0, None, ALU.mult, None,
                           ["csb"], ["CT"])
    io_i = sb2([128, Ns], I32)
    k.op("pool", lambda e: e.iota(io_i, pattern=[[1, Ns]], base=1, channel_multiplier=0), writes=["io"])
    io_f = sb2([128, Ns])
    cp(io_f, io_i, ["io"], ["io_f"])
    TW = 8 * Ns
    angb = R1[:, 0:TW]
    tmpb = (R1[:, TW:2 * TW], R1[:, 2 * TW:3 * TW], R1[:, 3 * TW:4 * TW].bitcast(I32))
    assert 4 * TW <= R1F
    for hh in range(2):
        tt(angb.rearrange("p (a b) -> p a b", b=Ns), io_f.unsqueeze(1).to_broadcast([128, 8, Ns]),
           th[:, hh * 8:(hh + 1) * 8].unsqueeze(2).to_broadcast([128, 8, Ns]), ALU.mult, ["io_f"] + SM, ["tab"])
        sincos(angb, cosT[:, hh * 8:(hh + 1) * 8, :].rearrange("p a b -> p (a b)"),
               sinT[:, hh * 8:(hh + 1) * 8, :].rearrange("p a b -> p (a b)"), (128, TW), "tab", temps=tmpb)
    TAB = ["tab", "sm"]
    k.handoff(["tab", "tab_t0", "tab_t1", "tab_ti"], R1_MIX + MO_KEYS + HID_KEYS)

    if stop == "ssm":
        k.emit()
        return nc
    SETUP_KEYS = ["cw_in", "fw_in", "sm", "xb", "crow", "csb", "io", "io_f", "tab", "tab_t0", "tab_t1", "tab_ti",
                  "sm_t0", "sm_t1", "sm_ti"]
    k.handoff(SETUP_KEYS, STG_KEYS)
    pieces = []
    for name, (Kd, Nout) in WSPEC.items():
        for oc in range(Nout // 128):
            pieces.append((name, oc))

    def conv_piece(name, oc, gate=None):
        src = wd[name][:, oc * 128:(oc + 1) * 128].rearrange("(k p) c -> p k c", p=128)
        k.dma("pool", wscr[name][oc], src, reads=([gate] if gate is not None else []), writes=[("scr", name, oc)])

    def conv_some(n, gate=None):
        for _ in range(n):
            if pieces:
                conv_piece(*pieces.pop(0), gate=gate)

    if stop == "conv":
        k.emit()
        return nc
    wbi = [0]

    def getw(name, oc, k0=0, k1=None):
        kcn = WSPEC[name][0] // 128
        if k1 is None:
            k1 = kcn
        i = wbi[0] % NWB
        wbi[0] += 1
        k.dma("sp", wbuf[i][:, 0:k1 - k0, :], wscr[name][oc, :, k0:k1, :], reads=[("scr", name, oc)],
              writes=[("wbuf", i)])
        return wbuf[i], ("wbuf", i)

    def proj(name, oc, rhs_fn, rhs_keys, ncols, out_ps=None):
        kcn = WSPEC[name][0] // 128
        if out_ps is None:
            ps, pk = newps()
        else:
            ps, pk = out_ps
        for k0 in range(0, kcn, 16):
            k1 = min(kcn, k0 + 16)
            wt, wk = getw(name, oc, k0, k1)
            for kc in range(k0, k1):
                mm(ps[:, 0:ncols], wt[:, kc - k0, :], rhs_fn(kc), kc == 0, kc == kcn - 1,
                   [wk] + list(rhs_keys), [pk])
        return ps, pk

    def load_x_block(b, gate_key=None):
        for t0 in range(0, N, 128):
            rows = min(128, N - t0)
            k.dma("sp", xst[0:rows, :], xw[b * N + t0:b * N + t0 + rows, :],
                  writes=["xst"] + ([gate_key] if (gate_key is not None and t0 == 0) else []))
            for c4 in range(4):
                ps, pk = newps()
                for c in range(4):
                    cidx = c4 * 4 + c
                    tr(ps[:, c * 128:c * 128 + rows], xst[0:rows, cidx * 128:(cidx + 1) * 128], ident[0:rows, 0:rows],
                       ["xst", "ident"], [pk])
                cp(h[:, c4 * 4:(c4 + 1) * 4, t0:t0 + rows],
                   ps[:, 0:512].rearrange("p (c r) -> p c r", r=128)[:, :, 0:rows], [pk],
                   [("h", c4 * 4 + c) for c in range(4)], eng=("act" if c4 % 2 else "dve"))

    def rstd_of(psum_ap, pkey, scale, eps, out, okey):
        ts(out, psum_ap, scale, eps, ALU.mult, ALU.add, [pkey], [okey])
        act(out, out, AF.Sqrt, [okey], [okey])
        k.op("dve", lambda e: e.reciprocal(out=out, in_=out), [okey], [okey])

    def rms_to_nb(gname, extra_mask=False):
        for c in range(16):
            s = sqb[c % 2]
            act(s, h[:, c, :], AF.Square, [("h", c)], [("sqb", c % 2)])
            mm(pst1[:, 0:N], ones_b, s, c == 0, c == 15, [("sqb", c % 2), "ones_b"], ["pst1"])
        rstd_of(pst1[:, 0:N], "pst1", 1.0 / D, EPS, rt[0], "rt0")
        if extra_mask:
            tt(rt[0], rt[0], maskb, ALU.mult, ["rt0", "maskb"], ["rt0"])
        for c in range(16):
            stt(nb[:, c, :], h[:, c, :], pv[gname][:, c:c + 1], rt[0], ALU.mult, ALU.mult,
                [("h", c), "rt0"] + PVK, [("nb", c)])

    NBK = [("nb", c) for c in range(16)]

    def u_proj():
        for oc in range(4):
            ps, pk = proj("w_in", oc, lambda kc: nb[:, kc, :], NBK, N)
            act(u_f[:, oc, :], ps[:, 0:N], AF.Copy, [pk], [("u_f", oc)])
            cp(u_b[:, oc, :], ps[:, 0:N], [pk], [("u_b", oc)])

    def ssm_pass(with_out, extract=None):
        k.handoff(["xst"], SET2)
        _ssm_pass(with_out, extract)
        k.handoff(SET2, ["xst"])

    def _ssm_pass(with_out, extract=None):
        for sbi in range(2):
            c0 = sbi * Ns
            def pair_gen(P, sbi=sbi, c0=c0):
                ch = P // 4
                hb = ((P % 4) // 2) * 64
                psr, pkr = newps()
                psi, pki = newps()
                mm(psr[:, 0:Ns], BT[hb:hb + 64, P, 0, :], u_b[hb:hb + 64, ch, c0:c0 + Ns], True, True,
                   ["BT", ("u_b", ch)], [pkr])
                mm(psi[:, 0:Ns], BT[hb:hb + 64, P, 1, :], u_b[hb:hb + 64, ch, c0:c0 + Ns], True, True,
                   ["BT", ("u_b", ch)], [pki])
                yield
                cs = cosT[:, P, :]
                sn = sinT[:, P, :]
                if P % 2 == 0:
                    m1, m2, m3, m4, wr, wi = stmp
                    K_ = [("stmp", i) for i in range(6)]
                    qbs = qb
                    Q_ = [("qb", i) for i in range(4)]
                else:
                    m1, m2, m3, m4, wr, wi = stmp2
                    K_ = [("stmp2", i) for i in range(6)]
                    qbs = qb2
                    Q_ = [("qb2", i) for i in range(4)]
                tt(m1, psr[:, 0:Ns], cs, ALU.mult, [pkr] + TAB, [K_[0]])
                yield
                tt(m2, psi[:, 0:Ns], sn, ALU.mult, [pki] + TAB, [K_[1]])
                yield
                tt(m3, psi[:, 0:Ns], cs, ALU.mult, [pki] + TAB, [K_[2]])
                yield
                tt(m4, psr[:, 0:Ns], sn, ALU.mult, [pkr] + TAB, [K_[3]])
                yield
                tt(m1, m1, m2, ALU.add, [K_[0], K_[1]], [K_[0]])
                yield
                tt(m3, m3, m4, ALU.subtract, [K_[2], K_[3]], [K_[2]])
                yield
                dec = mag[:, P:P + 1].to_broadcast([128, Ns])
                k.op("dve", lambda e, wr=wr, m1=m1, dec=dec, P=P: e.tensor_tensor_scan(
                    out=wr, data0=dec, data1=m1, initial=Xre[:, P:P + 1], op0=ALU.mult, op1=ALU.add),
                    [K_[0], "X"] + TAB, [K_[4]])
                yield
                k.op("dve", lambda e, wi=wi, m3=m3, dec=dec, P=P: e.tensor_tensor_scan(
                    out=wi, data0=dec, data1=m3, initial=Xim[:, P:P + 1], op0=ALU.mult, op1=ALU.add),
                    [K_[2], "X"] + TAB, [K_[5]])
                yield
                col = Ns - 1
                if extract is not None and extract[0] == sbi:
                    col = extract[1]
                act(Wend[:, 0, P:P + 1], wr[:, col:col + 1], AF.Copy, [K_[4]], ["Wend"])
                act(Wend[:, 1, P:P + 1], wi[:, col:col + 1], AF.Copy, [K_[5]], ["Wend"])
                yield
                if with_out:
                    tt(qbs[0], wr, cs, ALU.mult, [K_[4]] + TAB, [Q_[0]], eng="pool")
                    tt(qbs[1], wi, sn, ALU.mult, [K_[5]] + TAB, [Q_[1]], eng="pool")
                    tt(qbs[2], wi, cs, ALU.mult, [K_[5]] + TAB, [Q_[2]], eng="pool")
                    tt(qbs[3], wr, sn, ALU.mult, [K_[4]] + TAB, [Q_[3]], eng="pool")
                    yield
                    for qi, v in ((0, 0), (1, 1), (2, 2), (3, 2)):
                        mm(py[:, c0:c0 + Ns], CT[:, P, v, :], qbs[qi], (P % 4 == 0 and qi == 0),
                           (P % 4 == 3 and qi == 3), [Q_[qi], "CT"], ["py"])
                    if P % 4 == 3:
                        stt(u_f[:, ch, c0:c0 + Ns], u_f[:, ch, c0:c0 + Ns], pv["ssm_d"][:, ch:ch + 1],
                            py[:, c0:c0 + Ns], ALU.mult, ALU.add, ["py", ("u_f", ch)] + PVK, [("u_f", ch)])
                        act(u_f[:, ch, c0:c0 + Ns], u_f[:, ch, c0:c0 + Ns], AF.Gelu_apprx_tanh,
                            [("u_f", ch)], [("u_f", ch)])

            for P0 in range(0, 16, 2):
                gens = [pair_gen(P0), pair_gen(P0 + 1)]
                while gens:
                    for g_ in list(gens):
                        try:
                            next(g_)
                        except StopIteration:
                            gens.remove(g_)
            col = Ns - 1
            if extract is not None and extract[0] == sbi:
                col = extract[1]
            cN = cosT[:, :, col]
            sN = sinT[:, :, col]
            tt(hx[0], Wend[:, 0, :], cN, ALU.mult, ["Wend"] + TAB, ["hx0"])
            tt(hx[1], Wend[:, 1, :], sN, ALU.mult, ["Wend"] + TAB, ["hx1"])
            tt(hx[2], Wend[:, 1, :], cN, ALU.mult, ["Wend"] + TAB, ["hx2"])
            tt(hx[3], Wend[:, 0, :], sN, ALU.mult, ["Wend"] + TAB, ["hx3"])
            tt(Xre, hx[0], hx[1], ALU.subtract, ["hx0", "hx1"], ["X"])
            tt(Xim, hx[2], hx[3], ALU.add, ["hx2", "hx3"], ["X"])
            if extract is not None and extract[0] == sbi:
                return

    k.op("dve", lambda e: e.memset(Xre, 0.0), writes=["X"])
    k.op("dve", lambda e: e.memset(Xim, 0.0), writes=["X"])
    if use_cc:
        eb, ecol = divmod(LQ - 1, N)
        esb, ecl = divmod(ecol, Ns)
        conv_some(4)
        per_blk = (len(pieces) + eb - 1) // max(eb, 1)
        for b in range(eb + 1):
            load_x_block(b, gate_key=("p1", b))
            conv_some(per_blk, gate=("p1", b))
            rms_to_nb("norm_mix_pre")
            u_proj()
            ssm_pass(False, extract=(esb, ecl) if b == eb else None)
    conv_some(len(pieces))
    if use_cc:
        d1 = k.dma("sp", cc_in[0:4, :].rearrange("a (p c) -> (a p) c", c=16), Xre, reads=["X"], writes=["cc_in"])
        d2 = k.dma("sp", cc_in[4:8, :].rearrange("a (p c) -> (a p) c", c=16), Xim, reads=["X"],
                   writes=["cc_in"])

        def ccfn(e):
            return e.collective_compute("AllGather", ALU.bypass,
                                        replica_groups=[list(range(g * NQ, (g + 1) * NQ)) for g in range(2)],
                                        ins=[cc_in], outs=[cc_out])
        ccop = k._record("pool", ccfn, ["cc_in"], ["cc_out"], True)
        ccop.needs_inc = True
        ccop.inc = 1
        G = k.sb([128, NQ, 2, 16], F32, "G")
        for r in range(NQ):
            for v in range(2):
                k.dma("sp", G[:, r, v, :],
                      cc_out[r * 8 + v * 4:r * 8 + v * 4 + 4, :].rearrange("a (p c) -> (a p) c", c=16),
                      reads=["cc_out"], writes=["G"])
        cco = k.sb([128, 24], F32, "cco")
        k.dma("sp", cco, bass.AP(ccoefd.tensor, 0, [[0, 128], [1, 24]]), writes=["cco"])
        nsq = int(round(math.log2(LQ)))
        assert 2 ** nsq == LQ
        p1r = smt("p1r"); p1i = smt("p1i"); p2r = smt("p2r"); p2i = smt("p2i")
        cp(p1r, ar, SM, ["pw"])
        cp(p1i, ai, SM, ["pw"])
        PW = ["pw", "sm"]

        def csq(outr, outi, inr, ini):
            tt(t_a, inr, inr, ALU.mult, PW, ["pw"])
            tt(t_b, ini, ini, ALU.mult, PW, ["pw"])
            tt(den, inr, ini, ALU.mult, PW, ["pw"])
            tt(outr, t_a, t_b, ALU.subtract, PW, ["pw"])
            ts(outi, den, 2.0, None, ALU.mult, None, PW, ["pw"])
        for _ in range(nsq):
            csq(p1r, p1i, p1r, p1i)
        csq(p2r, p2i, p1r, p1i)
        k.op("dve", lambda e: e.memset(Xre, 0.0), reads=["X"], writes=["X"])
        k.op("dve", lambda e: e.memset(Xim, 0.0), reads=["X"], writes=["X"])
        cfr = smt("cfr"); cfi = smt("cfi")
        for r in range(NQ):
            ts(cfr, p1r, cco[:, r * 3 + 1:r * 3 + 2], cco[:, r * 3:r * 3 + 1], ALU.mult, ALU.add, PW + ["cco"], ["cf"])
            stt(cfr, p2r, cco[:, r * 3 + 2:r * 3 + 3], cfr, ALU.mult, ALU.add, PW + ["cco", "cf"], ["cf"])
            ts(cfi, p1i, cco[:, r * 3 + 1:r * 3 + 2], None, ALU.mult, None, PW + ["cco"], ["cf"])
            stt(cfi, p2i, cco[:, r * 3 + 2:r * 3 + 3], cfi, ALU.mult, ALU.add, PW + ["cco", "cf"], ["cf"])
            Sr = G[:, r, 0, :]
            Si = G[:, r, 1, :]
            tt(t_a, cfr, Sr, ALU.mult, ["cf", "G"], ["pw"])
            tt(Xre, Xre, t_a, ALU.add, ["X", "pw"], ["X"])
            tt(t_a, cfi, Si, ALU.mult, ["cf", "G"], ["pw"])
            tt(Xre, Xre, t_a, ALU.subtract, ["X", "pw"], ["X"])
            tt(t_a, cfr, Si, ALU.mult, ["cf", "G"], ["pw"])
            tt(Xim, Xim, t_a, ALU.add, ["X", "pw"], ["X"])
            tt(t_a, cfi, Sr, ALU.mult, ["cf", "G"], ["pw"])
            tt(Xim, Xim, t_a, ALU.add, ["X", "pw"], ["X"])

    k.handoff(SETUP_KEYS + STG_KEYS, R2_MIX + FO_KEYS)
    k.op("pool", lambda e: e.memset(hist.rearrange("p a b -> p (a b)"), 0.0), writes=["hist"])
    k.op("pool", lambda e: e.memset(carry.rearrange("p a b -> p (a b)"), 0.0), writes=[("carry", i) for i in range(88)])
    out_dmas = []
    for b in range(NB):
        load_x_block(b)
        k.dma("sp", maskb, bass.AP(maskd.tensor, b * N, [[0, 128], [1, N]]), writes=["maskb"])
        if stop == "x":
            k.emit()
            return nc
        rms_to_nb("norm_mix_pre")
        if stop == "norm":
            k.emit()
            return nc
        u_proj()
        if stop == "uproj":
            k.emit()
            return nc
        for cc in range(8):
            ci_ = cin[cc % 2]
            ck = f"cin{cc % 2}"
            psv, pkv = proj("w_in", 4 + cc, lambda kc: nb[:, kc, :], NBK, N)
            psg, pkg = proj("w_in", 12 + cc, lambda kc: nb[:, kc, :], NBK, N)
            act(tA, psg[:, 0:N], AF.Sigmoid, [pkg], ["tA"])
            cb_ = cinb[cc % 2]
            cp(cb_[:, 0:CK - 1], histb[:, cc, :], ["hist"], [ck], eng="act")
            tt(cb_[:, CK - 1:CK - 1 + N], psv[:, 0:N], tA, ALU.mult, [pkv, "tA"], [ck])
            cp(histb[:, cc, :], cb_[:, N:N + CK - 1], [ck], ["hist"], eng="act")
            a_ = acc[:, cc, :]
            psc_, pkc_ = newps()
            for half in range(2):
                wi_ = wbi[0] % NWB
                wbi[0] += 1
                k.dma("sp", wbuf[wi_], cdg[cc, :, half * 16:(half + 1) * 16, :], reads=[("cdg", cc, half)],
                      writes=[("wbuf", wi_)])
                for j in range(16):
                    k_ = half * 16 + j
                    if k_ >= CK:
                        continue
                    mm(psc_[:, 0:N], wbuf[wi_][:, j, :], cb_[:, k_:k_ + N], k_ == 0, k_ == CK - 1,
                       [("wbuf", wi_), ck], [pkc_])
            act(a_, psc_[:, 0:N], AF.Identity, [pkc_] + PVK, [("acc", cc)], bias=pv["conv_dw_b"][:, cc:cc + 1])
            s0 = sqb[0]
            s1 = sqb[1]
            act(s0, a_, AF.Copy, [("acc", cc)], [("sqb", 0)])
            act(s1, a_, AF.Square, [("acc", cc)], [("sqb", 1)])
            mm(pst1[:, 0:N], ones_b, s0, cc == 0, cc == 7, [("sqb", 0), "ones_b"], ["pst1"])
            mm(pst2[:, 0:N], ones_b, s1, cc == 0, cc == 7, [("sqb", 1), "ones_b"], ["pst2"])
        ts(rt[1], pst1[:, 0:N], 1.0 / CONVW, None, ALU.mult, None, ["pst1"], ["rt1"])
        tt(rt[2], rt[1], rt[1], ALU.mult, ["rt1"], ["rt2"])
        stt(rt[2], pst2[:, 0:N], 1.0 / CONVW, rt[2], ALU.mult, ALU.subtract, ["pst2", "rt2"], ["rt2"])
        ts(rt[2], rt[2], LNEPS, None, ALU.add, None, ["rt2"], ["rt2"])
        act(rt[2], rt[2], AF.Sqrt, ["rt2"], ["rt2"])
        k.op("dve", lambda e: e.reciprocal(out=rt[2], in_=rt[2]), ["rt2"], ["rt2"])
        for cc in range(8):
            a_ = acc[:, cc, :]
            tt(a_, a_, rt[1], ALU.subtract, [("acc", cc), "rt1"], [("acc", cc)])
            tt(a_, a_, rt[2], ALU.mult, [("acc", cc), "rt2"], [("acc", cc)])
            act(csil[:, cc, :], a_, AF.Silu, [("acc", cc)] + PVK, [("csil", cc)],
                scale=pv["conv_ln_g"][:, cc:cc + 1], bias=pv["conv_ln_b"][:, cc:cc + 1])
        if stop == "convb":
            k.emit()
            return nc
        ssm_pass(True)
        if stop == "ssmb":
            k.emit()
            return nc
        for ch in range(4):
            cp(u_b[:, ch, :], u_f[:, ch, :], [("u_f", ch)], [("u_b", ch)], eng="act")
        UBK = [("u_b", i) for i in range(4)]
        for oc in range(4):
            ps, pk = proj("w_ssm_glu", oc, lambda kc: u_b[:, kc, :], UBK, N)
            act(tA, ps[:, 0:N], AF.Sigmoid, [pk], ["tA"])
            tt(ysb[:, oc, :], u_f[:, oc, :], tA, ALU.mult, [("u_f", oc), "tA"], [("ysb", oc)])
        YK = [("ysb", i) for i in range(4)]
        CSK = [("csil", i) for i in range(8)]
        for oc in range(16):
            psa, pka = proj("w_in", 20 + oc, lambda kc: nb[:, kc, :], NBK, N)
            act(tA, psa[:, 0:N], AF.Sigmoid, [pka], ["tA"])
            psb, pkb = proj("w_ssm_proj", oc, lambda kc: ysb[:, kc, :], YK, N)
            tt(mtmp, psb[:, 0:N], tA, ALU.mult, [pkb, "tA"], ["mtmp"])
            psc, pkc = proj("w_in", 36 + oc, lambda kc: nb[:, kc, :], NBK, N)
            act(tB, psc[:, 0:N], AF.Sigmoid, [pkc], ["tB"])
            psd, pkd = proj("w_conv_proj", oc, lambda kc: csil[:, kc, :], CSK, N)
            tt(tB, psd[:, 0:N], tB, ALU.mult, [pkd, "tB"], ["tB"])
            tt(merged[:, oc, :], mtmp, tB, ALU.add, ["mtmp", "tB"], [("merged", oc)])
        MK = [("merged", i) for i in range(16)]
        if stop == "merge":
            k.emit()
            return nc
        k.handoff(R1_MIX, MO_KEYS)
        for oc in range(16):
            ps, pk = proj("w_mix_out", oc, lambda kc: merged[:, kc, :], MK, N)
            act(mo_f[:, oc, :], ps[:, 0:N], AF.Copy, [pk], [("mo", oc)])
            s = sqb[oc % 2]
            act(s, ps[:, 0:N], AF.Square, [pk], [("sqb", oc % 2)])
            mm(pst1[:, 0:N], ones_b, s, oc == 0, oc == 15, [("sqb", oc % 2), "ones_b"], ["pst1"])
        rstd_of(pst1[:, 0:N], "pst1", 1.0 / D, EPS, rt[0], "rt0")
        for oc in range(16):
            stt(mo_f[:, oc, :], mo_f[:, oc, :], pv["norm_mix_post"][:, oc:oc + 1], rt[0], ALU.mult, ALU.mult,
                [("mo", oc), "rt0"] + PVK, [("mo", oc)])
            tt(h[:, oc, :], h[:, oc, :], mo_f[:, oc, :], ALU.add, [("h", oc), ("mo", oc)], [("h", oc)])
        rms_to_nb("norm_ffn_pre", extra_mask=True)
        if stop == "mix":
            k.emit()
            return nc
        k.handoff(MO_KEYS, HID_KEYS)
        k.handoff(R2_MIX, FO_KEYS)
        for fc in range(44):
            gi_, vi_ = (1, 2) if fc % 2 == 0 else (3, 4)
            tg = rt[gi_]
            tv = rt[vi_]
            gk_, vk_ = f"rt{gi_}", f"rt{vi_}"
            for (ocx, tbuf, tkey) in ((fc, tg, gk_), (44 + fc, tv, vk_)):
                ps, pk = proj("w_ffn_up", ocx, lambda kc: nb[:, kc, :], NBK, N)
                w0 = ffnw[:, 0, ocx:ocx + 1]
                w1 = ffnw[:, 1, ocx:ocx + 1]
                w2 = ffnw[:, 2, ocx:ocx + 1]
                act(tbuf, ps[:, 0:N], AF.Identity, [pk] + PVK, [tkey], scale=w2,
                    bias=pv["ffn_dw_b"][:, ocx:ocx + 1])
                stt(tbuf[:, 1:N], ps[:, 0:N - 1], w1, tbuf[:, 1:N], ALU.mult, ALU.add, [pk, tkey] + PVK, [tkey])
                stt(tbuf[:, 2:N], ps[:, 0:N - 2], w0, tbuf[:, 2:N], ALU.mult, ALU.add, [pk, tkey] + PVK, [tkey])
                stt(tbuf[:, 0:1], carry[:, ocx, 1:2], w1, tbuf[:, 0:1], ALU.mult, ALU.add,
                    [("carry", ocx), tkey] + PVK, [tkey])
                stt(tbuf[:, 0:2], carry[:, ocx, 0:2], w0, tbuf[:, 0:2], ALU.mult, ALU.add,
                    [("carry", ocx), tkey] + PVK, [tkey])
                act(carry[:, ocx, :], ps[:, N - 2:N], AF.Copy, [pk], [("carry", ocx)])
            act(tg, tg, AF.Gelu_apprx_tanh, [gk_], [gk_])
            tt(hid[:, fc, :], tg, tv, ALU.mult, [gk_, vk_], [("hid", fc)])
        for oc in range(16):
            ps, pk = proj("w_ffn_down", oc, lambda kc: hid[:, kc, :], HID_KEYS, N)
            act(fo_f[:, oc, :], ps[:, 0:N], AF.Copy, [pk], [("fo", oc)])
            s = sqb[oc % 2]
            act(s, ps[:, 0:N], AF.Square, [pk], [("sqb", oc % 2)])
            mm(pst1[:, 0:N], ones_b, s, oc == 0, oc == 15, [("sqb", oc % 2), "ones_b"], ["pst1"])
        rstd_of(pst1[:, 0:N], "pst1", 1.0 / D, EPS, rt[0], "rt0")
        for oc in range(16):
            stt(fo_f[:, oc, :], fo_f[:, oc, :], pv["norm_ffn_post"][:, oc:oc + 1], rt[0], ALU.mult, ALU.mult,
                [("fo", oc), "rt0"] + PVK, [("fo", oc)])
            tt(h[:, oc, :], h[:, oc, :], fo_f[:, oc, :], ALU.add, [("h", oc), ("fo", oc)], [("h", oc)])
        for t0 in range(0, N, 128):
            rows = min(128, N - t0)
            g0 = b * N + t0
            lo = max(g0, H)
            hi = g0 + rows
            if hi <= lo:
                continue
            for c4 in range(4):
                ps, pk = newps()
                for c in range(4):
                    cidx = c4 * 4 + c
                    tr(ps[0:rows, c * 128:(c + 1) * 128], h[:, cidx, t0:t0 + rows], ident, [("h", cidx), "ident"], [pk])
                cp(xst[0:rows, c4 * 512:(c4 + 1) * 512], ps[0:rows, 0:512], [pk], ["xst"],
                   eng=("act" if c4 % 2 else "dve"))
            od = k.dma("sp", outd[lo - H:hi - H, :], xst[lo - g0:rows, :], reads=["xst"], writes=["out"])
            out_dmas.append(od)
        k.handoff(HID_KEYS, R1_MIX)
        k.handoff(FO_KEYS, R2_MIX)
    k.emit(final_wait_ops=out_dmas)
    return nc


def make_in_maps(inputs, NB, N, H, LQ, ncores=8):
    x = np.asarray(inputs["x"], dtype=np.float32)
    B, S, _ = x.shape
    nq = ncores // B
    assert S == nq * LQ
    W = NB * N
    meta = np.asarray(inputs["meta_tokens"], dtype=np.float32)
    shared = {}
    for n in WSPEC:
        shared[n] = np.ascontiguousarray(np.asarray(inputs[n], dtype=np.float32)[0])
    for n, s in VEC_INPUTS.items():
        shared[n] = np.ascontiguousarray(np.asarray(inputs[n], dtype=np.float32).reshape(s // 128, 128))
    shared["log_step"] = np.ascontiguousarray(np.asarray(inputs["log_step"], dtype=np.float32).reshape(16, 2))
    for n in ("ssm_b_re", "ssm_b_im", "ssm_c_re", "ssm_c_im"):
        shared[n] = np.ascontiguousarray(np.asarray(inputs[n], dtype=np.float32)[0])
    shared["conv_dw_w"] = np.ascontiguousarray(np.asarray(inputs["conv_dw_w"], dtype=np.float32)[0])
    shared["ffn_dw_w"] = np.ascontiguousarray(np.asarray(inputs["ffn_dw_w"], dtype=np.float32).reshape(3, 88, 128))
    maps = []
    for core in range(ncores):
        b, q = divmod(core, nq)
        hseq = np.concatenate([meta, x[b]], axis=0)
        a = NMETA + LQ * q - H
        xw = np.zeros((W, D), np.float32)
        mask = np.ones((1, W), np.float32)
        lo = max(a, 0)
        xw[lo - a:W] = hseq[lo:a + W]
        if a < 0:
            mask[0, 0:-a] = 0.0
        cco = np.zeros((1, 24), np.float32)
        for r in range(nq):
            e = q - 1 - r
            if 0 <= e <= 2:
                cco[0, r * 3 + e] = 1.0
        m = dict(shared)
        m["xw"] = xw
        m["mask"] = mask
        m["ccoef"] = cco
        maps.append(m)
    return maps


CFG = dict(NB=10, N=414, H=44, LQ=4096)


def kernel(**inputs):
    cfg = CFG
    nc = build(**cfg)
    maps = make_in_maps(inputs, **cfg)
    res = run_bass_kernel_spmd(nc, maps, core_ids=list(range(8)))
    x = inputs["x"]
    B, S, _ = x.shape
    out = np.empty((B, S, D), np.float32)
    nq = 8 // B
    for core in range(8):
        b, q = divmod(core, nq)
        out[b, q * cfg["LQ"]:(q + 1) * cfg["LQ"]] = res.results[core]["out"]
    return out
```

```python
import contextlib
import math
import numpy as np
import concourse.bass as bass
import concourse.mybir as mybir
from concourse.bass_utils import run_bass_kernel_spmd

F32 = mybir.dt.float32
BF16 = mybir.dt.bfloat16
I32 = mybir.dt.int32
AF = mybir.ActivationFunctionType
ALU = mybir.AluOpType

ENGS = ("pe", "act", "dve", "pool", "sp")
SEM_LIMIT = 30000
SAME_ENGINE_SYNC = {"act", "dve", "pool"}

D = 2048
NMETA = 16
SSMW = 512
CONVW = 1024
CK = 31
DFF = 5632
EPS = 1e-6
LNEPS = 1e-5
TWO_PI = 2.0 * math.pi


class Op:
    __slots__ = ("eng", "fn", "deps", "is_dma", "needs_inc", "tok", "dsem", "inc")

    def __init__(self, eng, fn, deps, is_dma):
        self.eng = eng
        self.fn = fn
        self.deps = deps
        self.is_dma = is_dma
        self.needs_inc = False
        self.tok = None
        self.dsem = None
        self.inc = 16 if is_dma else 1


class Res:
    __slots__ = ("lastw", "readers")

    def __init__(self):
        self.lastw = None
        self.readers = []


class K:
    def __init__(self, nc, ndma_sems=8):
        self.nc = nc
        self.ops = {e: [] for e in ENGS}
        self.res = {}
        self.ndma = ndma_sems
        self.ntiles = 0

    def sb(self, shape, dtype, name=None):
        self.ntiles += 1
        return self.nc.alloc_sbuf_tensor(name or f"sb{self.ntiles}", list(shape), dtype).ap()

    def ps(self, shape, dtype=F32, name=None):
        self.ntiles += 1
        return self.nc.alloc_psum_tensor(name or f"ps{self.ntiles}", list(shape), dtype).ap()

    def _r(self, key):
        r = self.res.get(key)
        if r is None:
            r = self.res[key] = Res()
        return r

    def handoff(self, from_keys, to_keys):
        acc = []
        for fk in from_keys:
            r = self._r(fk)
            if r.lastw is not None:
                acc.append(r.lastw)
            acc.extend(r.readers)
        for tk in to_keys:
            self._r(tk).readers.extend(acc)

    @staticmethod
    def _excl(key):
        if isinstance(key, tuple):
            return key[0] == "pm"
        return key in ("pst1", "pst2", "py")

    def _record(self, eng, fn, reads, writes, is_dma):
        xr = [x for x in reads if self._excl(x)]
        if xr:
            writes = list(writes) + [x for x in xr if x not in writes]
            reads = [x for x in reads if not self._excl(x)]
        deps = []
        for key in reads:
            r = self._r(key)
            if r.lastw is not None:
                deps.append(r.lastw)
        for key in writes:
            r = self._r(key)
            if r.lastw is not None:
                deps.append(r.lastw)
            deps.extend(r.readers)
        op = Op(eng, fn, deps, is_dma)
        for d in deps:
            d.needs_inc = True
        self.ops[eng].append(op)
        for key in writes:
            r = self._r(key)
            r.lastw = op
            r.readers = []
        for key in reads:
            r = self._r(key)
            r.readers = [x for x in r.readers if x.is_dma or x.eng != eng or is_dma]
            r.readers.append(op)
        return op

    def op(self, eng, fn, reads=(), writes=()):
        return self._record(eng, fn, reads, writes, False)

    def dma(self, eng, out, in_, reads=(), writes=(), **kw):
        def fn(e):
            return e.dma_start(out=out, in_=in_, **kw)
        op = self._record(eng, fn, reads, writes, True)
        op.needs_inc = True
        return op

    def emit(self, final_wait_ops=()):
        nc = self.nc
        es = contextlib.ExitStack()
        with es:
            def newsem(name):
                return es.enter_context(nc.semaphore(name))

            for e in ENGS:
                cur = None
                cnt = 0
                nsem = 0
                dcount = 0
                dsems = None
                dvals = None
                for op in self.ops[e]:
                    if op.is_dma:
                        if dsems is None:
                            dsems = [newsem(f"d_{e}_{i}") for i in range(self.ndma)]
                            dvals = [0] * self.ndma
                        slot = dcount % self.ndma
                        dcount += 1
                        dvals[slot] += op.inc
                        op.tok = (dsems[slot], dvals[slot])
                        op.dsem = (dsems[slot], dvals[slot] - op.inc)
                    elif op.needs_inc:
                        if cur is None or cnt >= SEM_LIMIT:
                            cur = newsem(f"c_{e}_{nsem}")
                            nsem += 1
                            cnt = 0
                        cnt += 1
                        op.tok = (cur, cnt)
            final_toks = [o.tok for o in final_wait_ops]

            def run_engine(e, h):
                waited = {}

                def wait(tok):
                    s, v = tok
                    if waited.get(id(s), 0) >= v:
                        return
                    waited[id(s)] = v
                    h.wait_ge(s, v)

                for op in self.ops[e]:
                    for d in op.deps:
                        if d.eng == e and (not d.is_dma) and e not in SAME_ENGINE_SYNC:
                            continue
                        wait(d.tok)
                    if op.is_dma:
                        s, prev = op.dsem
                        if prev > 0:
                            wait((s, prev))
                        ins = op.fn(h)
                        ins.then_inc(op.tok[0], op.inc)
                    else:
                        ins = op.fn(h)
                        if op.needs_inc:
                            ins.then_inc(op.tok[0], 1)
                if e == "sp":
                    for t in final_toks:
                        wait(t)

            with nc.Block() as block:
                @block.tensor
                def _(h):
                    run_engine("pe", h)

                @block.scalar
                def _(h):
                    run_engine("act", h)

                @block.vector
                def _(h):
                    run_engine("dve", h)

                @block.gpsimd
                def _(h):
                    run_engine("pool", h)

                @block.sync
                def _(h):
                    run_engine("sp", h)


WSPEC = {
    "w_in": (D, SSMW + 2 * CONVW + 2 * D),
    "w_ssm_glu": (SSMW, SSMW),
    "w_ssm_proj": (SSMW, D),
    "w_conv_proj": (CONVW, D),
    "w_mix_out": (D, D),
    "w_ffn_up": (D, 2 * DFF),
    "w_ffn_down": (DFF, D),
}
VEC_INPUTS = {
    "norm_mix_pre": D, "norm_mix_post": D, "norm_ffn_pre": D, "norm_ffn_post": D,
    "conv_dw_b": CONVW, "conv_ln_g": CONVW, "conv_ln_b": CONVW,
    "ffn_dw_b": 2 * DFF, "ssm_d": SSMW, "lam_re": 2048, "lam_im": 2048,
}


def build(NB, N, H, LQ, ncores=8, use_cc=True, stop=None):
    W = NB * N
    assert W == H + LQ and N % 2 == 0
    Ns = N // 2
    nc = bass.Bass("TRN2", target_bir_lowering=False)
    k = K(nc)
    KC = D // 128

    def din(name, shape):
        return nc.dram_tensor(name, list(shape), F32, kind="ExternalInput").ap()

    xw = din("xw", [W, D])
    maskd = din("mask", [1, W])
    ccoefd = din("ccoef", [1, 24])
    wd = {n: din(n, list(s)) for n, s in WSPEC.items()}
    vd = {n: din(n, [s // 128, 128]) for n, s in VEC_INPUTS.items()}
    log_step_d = din("log_step", [16, 2])
    b_re_d = din("ssm_b_re", [32, 64, 16])
    b_im_d = din("ssm_b_im", [32, 64, 16])
    c_re_d = din("ssm_c_re", [32, 16, 64])
    c_im_d = din("ssm_c_im", [32, 16, 64])
    conv_w_d = din("conv_dw_w", [CK, CONVW])
    ffn_w_d = din("ffn_dw_w", [3, 88, 128])
    outd = nc.dram_tensor("out", [LQ, D], F32, kind="ExternalOutput").ap()
    wscr = {n: nc.dram_tensor("scr_" + n, [s[1] // 128, 128, s[0] // 128, 128], BF16,
                              kind="Internal").ap() for n, s in WSPEC.items()}
    NQ = ncores // 2
    cdg = nc.dram_tensor("scr_convdiag", [8, 128, 32, 128], BF16, kind="Internal").ap()
    cc_in = nc.dram_tensor("cc_in", [8, 512], F32, kind="Internal").ap()
    cc_out = nc.dram_tensor("cc_out", [NQ * 8, 512], F32, kind="Internal", addr_space="Local").ap()

    def act(out, in_, func, reads, writes, scale=None, bias=None):
        kw = {}
        if scale is not None:
            kw["scale"] = scale
        if bias is not None:
            kw["bias"] = bias
        return k.op("act", lambda e: e.activation(out=out, in_=in_, func=func, **kw), reads, writes)

    def tt(out, a, b, op, reads, writes, eng="dve"):
        return k.op(eng, lambda e: e.tensor_tensor(out=out, in0=a, in1=b, op=op), reads, writes)

    def ts(out, a, s1, s2, op0, op1, reads, writes, eng="dve"):
        if op1 is None:
            return k.op(eng, lambda e: e.tensor_scalar(out=out, in0=a, scalar1=s1, scalar2=None, op0=op0),
                        reads, writes)
        return k.op(eng, lambda e: e.tensor_scalar(out=out, in0=a, scalar1=s1, scalar2=s2, op0=op0, op1=op1),
                    reads, writes)

    def stt(out, a, s, b, op0, op1, reads, writes):
        return k.op("dve", lambda e: e.scalar_tensor_tensor(out=out, in0=a, scalar=s, in1=b, op0=op0, op1=op1),
                    reads, writes)

    def cp(out, in_, reads, writes, eng="dve"):
        if eng == "act":
            return act(out, in_, AF.Copy, reads, writes)
        return k.op(eng, lambda e: e.tensor_copy(out=out, in_=in_), reads, writes)

    def mm(out, lhsT, rhs, start, stop, reads, writes):
        return k.op("pe", lambda e: e.matmul(out, lhsT=lhsT, rhs=rhs, start=start, stop=stop), reads, writes)

    def tr(out, in_, ident_ap, reads, writes):
        return k.op("pe", lambda e: e.transpose(out=out, in_=in_, identity=ident_ap), reads, writes)

    NPM = 5
    pm = [k.ps([128, 512], F32, name=f"pm{i}") for i in range(NPM)]
    pst1 = k.ps([128, 512], F32, name="pst1")
    pst2 = k.ps([128, 512], F32, name="pst2")
    py = k.ps([128, 512], F32, name="py")
    pmi = [0]

    def newps():
        i = pmi[0] % NPM
        pmi[0] += 1
        return pm[i], ("pm", i)

    ident = k.sb([128, 128], F32, "ident")
    k.op("pool", lambda e: e.memset(ident, 0.0), writes=["ident"])
    k.op("pool", lambda e: e.affine_select(out=ident, in_=ident, pattern=[[-1, 128]], compare_op=ALU.not_equal,
                                           fill=1.0, base=0, channel_multiplier=1), reads=["ident"], writes=["ident"])
    ones_b = k.sb([128, 128], BF16, "ones_b")
    k.op("pool", lambda e: e.memset(ones_b, 1.0), writes=["ones_b"])

    R1F = (44 * N * 2 + 3) // 4
    R1F = max(R1F, 6400)
    R1 = k.sb([128, R1F], F32, "R1")
    R2F = max(16 * N, 6400)
    R2 = k.sb([128, R2F], F32, "R2")

    class Carver:
        def __init__(self, region):
            self.r = region
            self.off = 0

        def f32(self, a, n):
            v = self.r[:, self.off:self.off + a * n]
            self.off += a * n
            return v.rearrange("p (a n) -> p a n", n=n) if a > 1 else v

        def bf(self, a, n):
            words = (a * n + 1) // 2
            v = self.r[:, self.off:self.off + words].bitcast(BF16)[:, 0:a * n]
            self.off += words
            return v.rearrange("p (a n) -> p a n", n=n) if a > 1 else v

    c1 = Carver(R1)
    u_f = c1.f32(4, N)
    u_b = c1.bf(4, N)
    cin = [c1.f32(1, N + CK - 1) for _ in range(2)]
    cinb = [c_.bitcast(BF16)[:, 0:N + CK - 1] for c_ in cin]
    csil = c1.bf(8, N)
    stmp = [c1.f32(1, Ns) for _ in range(6)]
    qb = [c1.bf(1, Ns) for _ in range(4)]
    ysb = c1.bf(4, N)
    tA = c1.f32(1, N)
    tB = c1.f32(1, N)
    mtmp = c1.f32(1, N)
    assert c1.off <= R1F, (c1.off, R1F)
    mo_f = R1[:, 0:16 * N].rearrange("p (a n) -> p a n", n=N) if 16 * N <= R1F else None
    assert mo_f is not None
    hid = R1[:, 0:22 * N].bitcast(BF16).rearrange("p (a n) -> p a n", n=N)
    stg_f = [R2[:, i * 2048:(i + 1) * 2048] for i in range(2)]
    stg_b = [R2[:, 4096 + i * 1024:4096 + (i + 1) * 1024].bitcast(BF16) for i in range(2)]
    assert 4096 + 2048 <= R2F
    c2 = Carver(R2)
    acc = c2.f32(8, N)
    merged = c2.bf(16, N)
    assert c2.off <= R2F
    fo_f = R2[:, 0:16 * N].rearrange("p (a n) -> p a n", n=N)
    R1_MIX = ["u_f", "u_b", "cin0", "cin1", "csil", "stmp", "qb", "ysb", "tA", "tB", "mtmp"] + \
             [("u_f", i) for i in range(4)] + [("u_b", i) for i in range(4)] + [("csil", i) for i in range(8)] + \
             [("ysb", i) for i in range(4)] + [("stmp", i) for i in range(6)] + [("qb", i) for i in range(4)]
    MO_KEYS = [("mo", i) for i in range(16)]
    HID_KEYS = [("hid", i) for i in range(44)]
    R2_MIX = [("acc", i) for i in range(8)] + [("merged", i) for i in range(16)]
    FO_KEYS = [("fo", i) for i in range(16)]
    STG_KEYS = ["stgf0", "stgf1", "stgb0", "stgb1"]

    csu = [0]

    def sb2(shape, dtype=F32):
        parts = shape[0]
        n = 1
        for d_ in shape[1:]:
            n *= d_
        v = R2[0:parts, csu[0]:csu[0] + n]
        csu[0] += n
        assert csu[0] <= R2F, csu[0]
        if dtype != F32:
            v = v.bitcast(dtype)
        if len(shape) == 3:
            v = v.rearrange("p (a b) -> p a b", b=shape[2])
        return v

    h = k.sb([128, 16, N], F32, "h")
    nb = k.sb([128, 16, N], BF16, "nb")
    sqb = [k.sb([128, N], BF16, f"sqb{i}") for i in range(2)]
    rt = [k.sb([128, N], F32, f"rt{i}") for i in range(5)]
    maskb = k.sb([128, N], F32, "maskb")
    cosT = k.sb([128, 16, Ns], F32, "cosT")
    sinT = k.sb([128, 16, Ns], F32, "sinT")
    CT = k.sb([128, 16, 3, 128], BF16, "CT")
    BT = k.sb([128, 16, 2, 128], BF16, "BT")
    NWB = 7
    wbuf = [k.sb([128, 16, 128], BF16, f"wbuf{i}") for i in range(NWB)]
    xst = k.sb([128, D], F32, "xst")
    stmp2 = [xst[:, i * Ns:(i + 1) * Ns] for i in range(6)]
    qb2 = [xst[:, 6 * Ns + i * ((Ns + 1) // 2):6 * Ns + (i + 1) * ((Ns + 1) // 2)].bitcast(BF16)[:, 0:Ns] for i in range(4)]
    assert 6 * Ns + 4 * ((Ns + 1) // 2) <= D
    SET2 = [("stmp2", i) for i in range(6)] + [("qb2", i) for i in range(4)]
    hist = k.sb([128, 8, CK - 1], F32, "hist")
    carry = k.sb([128, 88, 2], F32, "carry")
    Xre = k.sb([128, 16], F32, "Xre")
    Xim = k.sb([128, 16], F32, "Xim")
    Wend = k.sb([128, 2, 16], F32, "Wend")
    hx = [k.sb([128, 16], F32, f"hx{i}") for i in range(4)]

    pv = {}

    def load_vec(name):
        n = VEC_INPUTS[name] // 128
        t_in = k.sb([n, 128], F32, "vin_" + name)
        k.dma("sp", t_in, vd[name], writes=["vin_" + name])
        ps, pk = newps()
        tr(ps[:, 0:n], t_in, ident[0:n, 0:n], ["vin_" + name, "ident"], [pk])
        t = k.sb([128, n], F32, "pv_" + name)
        cp(t, ps[:, 0:n], [pk], ["pv_" + name])
        pv[name] = t

    for name in VEC_INPUTS:
        load_vec(name)
    PVK = ["pv_" + n for n in VEC_INPUTS]

    cw_in = sb2([CK, CONVW])
    k.dma("sp", cw_in, conv_w_d, writes=["cw_in"])
    convw = k.sb([128, 8, CK], F32, "convw")
    ps, pk = newps()
    for cc in range(8):
        tr(ps[:, cc * CK:(cc + 1) * CK], cw_in[:, cc * 128:(cc + 1) * 128], ident[0:CK, 0:CK], ["cw_in", "ident"], [pk])
    cp(convw.rearrange("p a b -> p (a b)"), ps[:, 0:8 * CK], [pk], ["convw"])
    fw_in = sb2([88, 3, 128])
    k.dma("sp", fw_in, ffn_w_d.rearrange("k r c -> r k c"), writes=["fw_in"])
    ffnw = k.sb([128, 3, 88], F32, "ffnw")
    ps, pk = newps()
    for kk in range(3):
        tr(ps[:, kk * 88:(kk + 1) * 88], fw_in[:, kk, :], ident[0:88, 0:88], ["fw_in", "ident"], [pk])
    cp(ffnw.rearrange("p a b -> p (a b)"), ps[:, 0:264], [pk], ["ffnw"])
    PVK += ["convw", "ffnw"]
    histb = hist.rearrange("p a b -> p (a b)").bitcast(BF16)[:, 0:8 * (CK - 1)].rearrange("p (a b) -> p a b", b=CK - 1)
    for cc in range(8):
        for half in range(2):
            stgw = wbuf[half]
            for j in range(16):
                k_ = half * 16 + j
                if k_ < CK:
                    k.op("pool", lambda e, o_=stgw[:, j, :], s_=convw[:, cc, k_:k_ + 1]: e.tensor_scalar(
                        out=o_, in0=ident, scalar1=s_, scalar2=1.0, op0=ALU.mult, op1=ALU.mult),
                        ["ident", "convw"], [("wbuf", half)])
                else:
                    k.op("pool", lambda e, o_=stgw[:, j, :]: e.memset(o_, 0.0), [], [("wbuf", half)])
            k.dma("sp", cdg[cc, :, half * 16:(half + 1) * 16, :], stgw, reads=[("wbuf", half)],
                  writes=[("cdg", cc, half)])

    if stop == "vec":
        k.emit()
        return nc
    sm = {}

    def smt(name, shape=(128, 16)):
        sm[name] = k.sb(list(shape), F32, "sm_" + name)
        return sm[name]

    ls_in = k.sb([16, 2], F32, "ls_in")
    k.dma("sp", ls_in, log_step_d, writes=["ls_in"])
    ls_x = k.sb([16, 2, 64], F32, "ls_x")
    cp(ls_x, ls_in.unsqueeze(2).to_broadcast([16, 2, 64]), ["ls_in"], ["ls_x"])
    ps, pk = newps()
    tr(ps[:, 0:16], ls_x.rearrange("a g p -> a (g p)"), ident[0:16, 0:16], ["ls_x", "ident"], [pk])
    lsT = smt("lsT")
    cp(lsT, ps[:, 0:16], [pk], ["sm"])
    SM = ["sm"] + PVK
    lr = pv["lam_re"]
    li = pv["lam_im"]
    step = smt("step")
    act(step, lsT, AF.Exp, SM, ["sm"])
    lrs = smt("lrs")
    tt(lrs, lr, step, ALU.mult, SM, ["sm"])
    mag = smt("mag")
    act(mag, lrs, AF.Exp, SM, ["sm"])
    th = smt("th")
    tt(th, li, step, ALU.mult, SM, ["sm"])

    sc_tmp = {}

    def sincos(theta, cos_out, sin_out, shape, key, temps=None):
        if temps is not None:
            t0, t1, ti = temps
        else:
            if shape not in sc_tmp:
                sc_tmp[shape] = (sb2(list(shape)), sb2(list(shape)), sb2(list(shape), I32))
            t0, t1, ti = sc_tmp[shape]
        for shift, outp in ((0.25, cos_out), (0.0, sin_out)):
            ts(t0, theta, 1.0 / TWO_PI, 0.5 + shift, ALU.mult, ALU.add, [key], [key + "_t0"])
            cp(ti, t0, [key + "_t0"], [key + "_ti"])
            cp(t1, ti, [key + "_ti"], [key + "_t1"])
            tt(t0, t0, t1, ALU.subtract, [key + "_t0", key + "_t1"], [key + "_t0"])
            ts(t1, t0, 0.0, None, ALU.is_lt, None, [key + "_t0"], [key + "_t1"])
            tt(t0, t0, t1, ALU.add, [key + "_t0", key + "_t1"], [key + "_t0"])
            ts(t0, t0, -0.5, -0.4999999, ALU.add, ALU.max, [key + "_t0"], [key + "_t0"])
            ts(t0, t0, 0.4999999, None, ALU.min, None, [key + "_t0"], [key + "_t0"])
            act(outp, t0, AF.Sin, [key + "_t0"], [key], scale=TWO_PI)

    cth = smt("cth")
    sth = smt("sth")
    sincos(th, cth, sth, (128, 16), "sm")
    ar = smt("ar")
    ai = smt("ai")
    tt(ar, mag, cth, ALU.mult, SM, ["sm"])
    tt(ai, mag, sth, ALU.mult, SM, ["sm"])
    den = smt("den")
    t_a = smt("t_a")
    t_b = smt("t_b")
    tt(den, lr, lr, ALU.mult, SM, ["sm"])
    tt(t_a, li, li, ALU.mult, SM, ["sm"])
    tt(den, den, t_a, ALU.add, SM, ["sm"])
    rden = smt("rden")
    k.op("dve", lambda e: e.reciprocal(out=rden, in_=den), SM, ["sm"])
    am1 = smt("am1")
    ts(am1, ar, -1.0, None, ALU.add, None, SM, ["sm"])
    cr = smt("cr")
    ci = smt("ci")
    tt(t_a, am1, lr, ALU.mult, SM, ["sm"])
    tt(t_b, ai, li, ALU.mult, SM, ["sm"])
    tt(t_a, t_a, t_b, ALU.add, SM, ["sm"])
    tt(cr, t_a, rden, ALU.mult, SM, ["sm"])
    tt(t_a, ai, lr, ALU.mult, SM, ["sm"])
    tt(t_b, am1, li, ALU.mult, SM, ["sm"])
    tt(t_a, t_a, t_b, ALU.subtract, SM, ["sm"])
    tt(ci, t_a, rden, ALU.mult, SM, ["sm"])

    b_re = sb2([128, 16, 16])
    b_im = sb2([128, 16, 16])
    for t_, d_ in ((b_re, b_re_d), (b_im, b_im_d)):
        src = bass.AP(d_.tensor, 0, [[16, 128], [2048, 16], [1, 16]])
        k.dma("sp", t_, src, writes=["sm"], allow_slow_non_contiguous=True)
    crb = cr.unsqueeze(2).to_broadcast([128, 16, 16])
    cib = ci.unsqueeze(2).to_broadcast([128, 16, 16])
    Bb_re = sb2([128, 16, 16])
    Bb_im = sb2([128, 16, 16])
    t3a = sb2([128, 16, 16])
    tt(Bb_re, b_re, crb, ALU.mult, SM, ["sm"])
    tt(t3a, b_im, cib, ALU.mult, SM, ["sm"])
    tt(Bb_re, Bb_re, t3a, ALU.subtract, SM, ["sm"])
    tt(Bb_im, b_im, crb, ALU.mult, SM, ["sm"])
    tt(t3a, b_re, cib, ALU.mult, SM, ["sm"])
    tt(Bb_im, Bb_im, t3a, ALU.add, SM, ["sm"])
    k.op("pool", lambda e: e.memset(BT.rearrange("p a b c -> p (a b c)"), 0.0), writes=["BT"])
    xb = sb2([128, 128])
    for P in range(16):
        m = P % 4
        for v, Bsrc in ((0, Bb_re), (1, Bb_im)):
            k.op("dve", lambda e: e.memset(xb, 0.0), writes=["xb"])
            for g2 in range(2):
                col = m * 32 + g2 * 16
                cp(xb[g2 * 64:(g2 + 1) * 64, col:col + 16], Bsrc[g2 * 64:(g2 + 1) * 64, P, :], SM, ["xb"])
            ps, pk = newps()
            tr(ps[:, 0:128], xb, ident, ["xb", "ident"], [pk])
            hb = (m // 2) * 64
            cp(BT[hb:hb + 64, P, v, :], ps[hb:hb + 64, 0:128], [pk], ["BT"])
    k.op("pool", lambda e: e.memset(CT.rearrange("p a b c -> p (a b c)"), 0.0), writes=["CT"])
    crow = sb2([128, 2, 128])
    csb = sb2([128, 8, 16])
    for v_re, d_ in ((True, c_re_d), (False, c_im_d)):
        for g2 in range(2):
            src = bass.AP(d_.tensor, g2 * 1024, [[2048, 16], [64, 16], [1, 64]])
            for half in range(2):
                for P8 in range(8):
                    srch = bass.AP(d_.tensor, g2 * 1024 + (half * 8 + P8) * 2048, [[64, 16], [1, 64]])
                    k.dma("sp", crow[P8 * 16:(P8 + 1) * 16, half, g2 * 64:(g2 + 1) * 64], srch, writes=["crow"])
        for half in range(2):
            ps, pk = newps()
            tr(ps[:, 0:128], crow[:, half, :], ident, ["crow", "ident"], [pk])
            cp(csb.rearrange("p a b -> p (a b)"), ps[:, 0:128], [pk], ["csb"])
            for P8 in range(8):
                P = half * 8 + P8
                m = P % 4
                for g2 in range(2):
                    col = m * 32 + g2 * 16
                    src_ = csb[g2 * 64:(g2 + 1) * 64, P8, :]
                    if v_re:
                        cp(CT[g2 * 64:(g2 + 1) * 64, P, 0, col:col + 16], src_, ["csb"], ["CT"])
                        ts(CT[g2 * 64:(g2 + 1) * 64, P, 1, col:col + 16], src_, -1.0, None, ALU.mult, None,
                           ["csb"], ["CT"])
                    else:
                        ts(CT[g2 * 64:(g2 + 1) * 64, P, 2, col:col + 16], src_, -1.0, None, ALU.mult, None,
                           ["csb"], ["CT"])
    io_i = sb2([128, Ns], I32)
    k.op("pool", lambda e: e.iota(io_i, pattern=[[1, Ns]], base=1, channel_multiplier=0), writes=["io"])
    io_f = sb2([128, Ns])
    cp(io_f, io_i, ["io"], ["io_f"])
    TW = 8 * Ns
    angb = R1[:, 0:TW]
    tmpb = (R1[:, TW:2 * TW], R1[:, 2 * TW:3 * TW], R1[:, 3 * TW:4 * TW].bitcast(I32))
    assert 4 * TW <= R1F
    for hh in range(2):
        tt(angb.rearrange("p (a b) -> p a b", b=Ns), io_f.unsqueeze(1).to_broadcast([128, 8, Ns]),
           th[:, hh * 8:(hh + 1) * 8].unsqueeze(2).to_broadcast([128, 8, Ns]), ALU.mult, ["io_f"] + SM, ["tab"])
        sincos(angb, cosT[:, hh * 8:(hh + 1) * 8, :].rearrange("p a b -> p (a b)"),
               sinT[:, hh * 8:(hh + 1) * 8, :].rearrange("p a b -> p (a b)"), (128, TW), "tab", temps=tmpb)
    TAB = ["tab", "sm"]
    k.handoff(["tab", "tab_t0", "tab_t1", "tab_ti"], R1_MIX + MO_KEYS + HID_KEYS)

    if stop == "ssm":
        k.emit()
        return nc
    SETUP_KEYS = ["cw_in", "fw_in", "sm", "xb", "crow", "csb", "io", "io_f", "tab", "tab_t0", "tab_t1", "tab_ti",
                  "sm_t0", "sm_t1", "sm_ti"]
    k.handoff(SETUP_KEYS, STG_KEYS)
    pieces = []
    for name, (Kd, Nout) in WSPEC.items():
        for oc in range(Nout // 128):
            pieces.append((name, oc))

    def conv_piece(name, oc, gate=None):
        src = wd[name][:, oc * 128:(oc + 1) * 128].rearrange("(k p) c -> p k c", p=128)
        k.dma("pool", wscr[name][oc], src, reads=([gate] if gate is not None else []), writes=[("scr", name, oc)])

    def conv_some(n, gate=None):
        for _ in range(n):
            if pieces:
                conv_piece(*pieces.pop(0), gate=gate)

    if stop == "conv":
        k.emit()
        return nc
    wbi = [0]

    def getw(name, oc, k0=0, k1=None):
        kcn = WSPEC[name][0] // 128
        if k1 is None:
            k1 = kcn
        i = wbi[0] % NWB
        wbi[0] += 1
        k.dma("sp", wbuf[i][:, 0:k1 - k0, :], wscr[name][oc, :, k0:k1, :], reads=[("scr", name, oc)],
              writes=[("wbuf", i)])
        return wbuf[i], ("wbuf", i)

    def proj(name, oc, rhs_fn, rhs_keys, ncols, out_ps=None):
        kcn = WSPEC[name][0] // 128
        if out_ps is None:
            ps, pk = newps()
        else:
            ps, pk = out_ps
        for k0 in range(0, kcn, 16):
            k1 = min(kcn, k0 + 16)
            wt, wk = getw(name, oc, k0, k1)
            for kc in range(k0, k1):
                mm(ps[:, 0:ncols], wt[:, kc - k0, :], rhs_fn(kc), kc == 0, kc == kcn - 1,
                   [wk] + list(rhs_keys), [pk])
        return ps, pk

    def load_x_block(b, gate_key=None):
        for t0 in range(0, N, 128):
            rows = min(128, N - t0)
            k.dma("sp", xst[0:rows, :], xw[b * N + t0:b * N + t0 + rows, :],
                  writes=["xst"] + ([gate_key] if (gate_key is not None and t0 == 0) else []))
            for c4 in range(4):
                ps, pk = newps()
                for c in range(4):
                    cidx = c4 * 4 + c
                    tr(ps[:, c * 128:c * 128 + rows], xst[0:rows, cidx * 128:(cidx + 1) * 128], ident[0:rows, 0:rows],
                       ["xst", "ident"], [pk])
                cp(h[:, c4 * 4:(c4 + 1) * 4, t0:t0 + rows],
                   ps[:, 0:512].rearrange("p (c r) -> p c r", r=128)[:, :, 0:rows], [pk],
                   [("h", c4 * 4 + c) for c in range(4)], eng=("act" if c4 % 2 else "dve"))

    def rstd_of(psum_ap, pkey, scale, eps, out, okey):
        ts(out, psum_ap, scale, eps, ALU.mult, ALU.add, [pkey], [okey])
        act(out, out, AF.Sqrt, [okey], [okey])
        k.op("dve", lambda e: e.reciprocal(out=out, in_=out), [okey], [okey])

    def rms_to_nb(gname, extra_mask=False):
        for c in range(16):
            s = sqb[c % 2]
            act(s, h[:, c, :], AF.Square, [("h", c)], [("sqb", c % 2)])
            mm(pst1[:, 0:N], ones_b, s, c == 0, c == 15, [("sqb", c % 2), "ones_b"], ["pst1"])
        rstd_of(pst1[:, 0:N], "pst1", 1.0 / D, EPS, rt[0], "rt0")
        if extra_mask:
            tt(rt[0], rt[0], maskb, ALU.mult, ["rt0", "maskb"], ["rt0"])
        for c in range(16):
            stt(nb[:, c, :], h[:, c, :], pv[gname][:, c:c + 1], rt[0], ALU.mult, ALU.mult,
                [("h", c), "rt0"] + PVK, [("nb", c)])

    NBK = [("nb", c) for c in range(16)]

    def u_proj():
        for oc in range(4):
            ps, pk = proj("w_in", oc, lambda kc: nb[:, kc, :], NBK, N)
            act(u_f[:, oc, :], ps[:, 0:N], AF.Copy, [pk], [("u_f", oc)])
            cp(u_b[:, oc, :], ps[:, 0:N], [pk], [("u_b", oc)])

    def ssm_pass(with_out, extract=None):
        k.handoff(["xst"], SET2)
        _ssm_pass(with_out, extract)
        k.handoff(SET2, ["xst"])

    def _ssm_pass(with_out, extract=None):
        for sbi in range(2):
            c0 = sbi * Ns
            def pair_gen(P, sbi=sbi, c0=c0):
                ch = P // 4
                hb = ((P % 4) // 2) * 64
                psr, pkr = newps()
                psi, pki = newps()
                mm(psr[:, 0:Ns], BT[hb:hb + 64, P, 0, :], u_b[hb:hb + 64, ch, c0:c0 + Ns], True, True,
                   ["BT", ("u_b", ch)], [pkr])
                mm(psi[:, 0:Ns], BT[hb:hb + 64, P, 1, :], u_b[hb:hb + 64, ch, c0:c0 + Ns], True, True,
                   ["BT", ("u_b", ch)], [pki])
                yield
                cs = cosT[:, P, :]
                sn = sinT[:, P, :]
                if P % 2 == 0:
                    m1, m2, m3, m4, wr, wi = stmp
                    K_ = [("stmp", i) for i in range(6)]
                    qbs = qb
                    Q_ = [("qb", i) for i in range(4)]
                else:
                    m1, m2, m3, m4, wr, wi = stmp2
                    K_ = [("stmp2", i) for i in range(6)]
                    qbs = qb2
                    Q_ = [("qb2", i) for i in range(4)]
                tt(m1, psr[:, 0:Ns], cs, ALU.mult, [pkr] + TAB, [K_[0]])
                yield
                tt(m2, psi[:, 0:Ns], sn, ALU.mult, [pki] + TAB, [K_[1]])
                yield
                tt(m3, psi[:, 0:Ns], cs, ALU.mult, [pki] + TAB, [K_[2]])
                yield
                tt(m4, psr[:, 0:Ns], sn, ALU.mult, [pkr] + TAB, [K_[3]])
                yield
                tt(m1, m1, m2, ALU.add, [K_[0], K_[1]], [K_[0]])
                yield
                tt(m3, m3, m4, ALU.subtract, [K_[2], K_[3]], [K_[2]])
                yield
                dec = mag[:, P:P + 1].to_broadcast([128, Ns])
                k.op("dve", lambda e, wr=wr, m1=m1, dec=dec, P=P: e.tensor_tensor_scan(
                    out=wr, data0=dec, data1=m1, initial=Xre[:, P:P + 1], op0=ALU.mult, op1=ALU.add),
                    [K_[0], "X"] + TAB, [K_[4]])
                yield
                k.op("dve", lambda e, wi=wi, m3=m3, dec=dec, P=P: e.tensor_tensor_scan(
                    out=wi, data0=dec, data1=m3, initial=Xim[:, P:P + 1], op0=ALU.mult, op1=ALU.add),
                    [K_[2], "X"] + TAB, [K_[5]])
                yield
                col = Ns - 1
                if extract is not None and extract[0] == sbi:
                    col = extract[1]
                act(Wend[:, 0, P:P + 1], wr[:, col:col + 1], AF.Copy, [K_[4]], ["Wend"])
                act(Wend[:, 1, P:P + 1], wi[:, col:col + 1], AF.Copy, [K_[5]], ["Wend"])
                yield
                if with_out:
                    tt(qbs[0], wr, cs, ALU.mult, [K_[4]] + TAB, [Q_[0]], eng="pool")
                    tt(qbs[1], wi, sn, ALU.mult, [K_[5]] + TAB, [Q_[1]], eng="pool")
                    tt(qbs[2], wi, cs, ALU.mult, [K_[5]] + TAB, [Q_[2]], eng="pool")
                    tt(qbs[3], wr, sn, ALU.mult, [K_[4]] + TAB, [Q_[3]], eng="pool")
                    yield
                    for qi, v in ((0, 0), (1, 1), (2, 2), (3, 2)):
                        mm(py[:, c0:c0 + Ns], CT[:, P, v, :], qbs[qi], (P % 4 == 0 and qi == 0),
                           (P % 4 == 3 and qi == 3), [Q_[qi], "CT"], ["py"])
                    if P % 4 == 3:
                        stt(u_f[:, ch, c0:c0 + Ns], u_f[:, ch, c0:c0 + Ns], pv["ssm_d"][:, ch:ch + 1],
                            py[:, c0:c0 + Ns], ALU.mult, ALU.add, ["py", ("u_f", ch)] + PVK, [("u_f", ch)])
                        act(u_f[:, ch, c0:c0 + Ns], u_f[:, ch, c0:c0 + Ns], AF.Gelu_apprx_tanh,
                            [("u_f", ch)], [("u_f", ch)])

            for P0 in range(0, 16, 2):
                gens = [pair_gen(P0), pair_gen(P0 + 1)]
                while gens:
                    for g_ in list(gens):
                        try:
                            next(g_)
                        except StopIteration:
                            gens.remove(g_)
            col = Ns - 1
            if extract is not None and extract[0] == sbi:
                col = extract[1]
            cN = cosT[:, :, col]
            sN = sinT[:, :, col]
            tt(hx[0], Wend[:, 0, :], cN, ALU.mult, ["Wend"] + TAB, ["hx0"])
            tt(hx[1], Wend[:, 1, :], sN, ALU.mult, ["Wend"] + TAB, ["hx1"])
            tt(hx[2], Wend[:, 1, :], cN, ALU.mult, ["Wend"] + TAB, ["hx2"])
            tt(hx[3], Wend[:, 0, :], sN, ALU.mult, ["Wend"] + TAB, ["hx3"])
            tt(Xre, hx[0], hx[1], ALU.subtract, ["hx0", "hx1"], ["X"])
            tt(Xim, hx[2], hx[3], ALU.add, ["hx2", "hx3"], ["X"])
            if extract is not None and extract[0] == sbi:
                return

    k.op("dve", lambda e: e.memset(Xre, 0.0), writes=["X"])
    k.op("dve", lambda e: e.memset(Xim, 0.0), writes=["X"])
    if use_cc:
        eb, ecol = divmod(LQ - 1, N)
        esb, ecl = divmod(ecol, Ns)
        conv_some(4)
        per_blk = (len(pieces) + eb - 1) // max(eb, 1)
        for b in range(eb + 1):
            load_x_block(b, gate_key=("p1", b))
            conv_some(per_blk, gate=("p1", b))
            rms_to_nb("norm_mix_pre")
            u_proj()
            ssm_pass(False, extract=(esb, ecl) if b == eb else None)
    conv_some(len(pieces))
    if use_cc:
        d1 = k.dma("sp", cc_in[0:4, :].rearrange("a (p c) -> (a p) c", c=16), Xre, reads=["X"], writes=["cc_in"])
        d2 = k.dma("sp", cc_in[4:8, :].rearrange("a (p c) -> (a p) c", c=16), Xim, reads=["X"],
                   writes=["cc_in"])

        def ccfn(e):
            return e.collective_compute("AllGather", ALU.bypass,
                                        replica_groups=[list(range(g * NQ, (g + 1) * NQ)) for g in range(2)],
                                        ins=[cc_in], outs=[cc_out])
        ccop = k._record("pool", ccfn, ["cc_in"], ["cc_out"], True)
        ccop.needs_inc = True
        ccop.inc = 1
        G = k.sb([128, NQ, 2, 16], F32, "G")
        for r in range(NQ):
            for v in range(2):
                k.dma("sp", G[:, r, v, :],
                      cc_out[r * 8 + v * 4:r * 8 + v * 4 + 4, :].rearrange("a (p c) -> (a p) c", c=16),
                      reads=["cc_out"], writes=["G"])
        cco = k.sb([128, 24], F32, "cco")
        k.dma("sp", cco, bass.AP(ccoefd.tensor, 0, [[0, 128], [1, 24]]), writes=["cco"])
        nsq = int(round(math.log2(LQ)))
        assert 2 ** nsq == LQ
        p1r = smt("p1r"); p1i = smt("p1i"); p2r = smt("p2r"); p2i = smt("p2i")
        cp(p1r, ar, SM, ["pw"])
        cp(p1i, ai, SM, ["pw"])
        PW = ["pw", "sm"]

        def csq(outr, outi, inr, ini):
            tt(t_a, inr, inr, ALU.mult, PW, ["pw"])
            tt(t_b, ini, ini, ALU.mult, PW, ["pw"])
            tt(den, inr, ini, ALU.mult, PW, ["pw"])
            tt(outr, t_a, t_b, ALU.subtract, PW, ["pw"])
            ts(outi, den, 2.0, None, ALU.mult, None, PW, ["pw"])
        for _ in range(nsq):
            csq(p1r, p1i, p1r, p1i)
        csq(p2r, p2i, p1r, p1i)
        k.op("dve", lambda e: e.memset(Xre, 0.0), reads=["X"], writes=["X"])
        k.op("dve", lambda e: e.memset(Xim, 0.0), reads=["X"], writes=["X"])
        cfr = smt("cfr"); cfi = smt("cfi")
        for r in range(NQ):
            ts(cfr, p1r, cco[:, r * 3 + 1:r * 3 + 2], cco[:, r * 3:r * 3 + 1], ALU.mult, ALU.add, PW + ["cco"], ["cf"])
            stt(cfr, p2r, cco[:, r * 3 + 2:r * 3 + 3], cfr, ALU.mult, ALU.add, PW + ["cco", "cf"], ["cf"])
            ts(cfi, p1i, cco[:, r * 3 + 1:r * 3 + 2], None, ALU.mult, None, PW + ["cco"], ["cf"])
            stt(cfi, p2i, cco[:, r * 3 + 2:r * 3 + 3], cfi, ALU.mult, ALU.add, PW + ["cco", "cf"], ["cf"])
            Sr = G[:, r, 0, :]
            Si = G[:, r, 1, :]
            tt(t_a, cfr, Sr, ALU.mult, ["cf", "G"], ["pw"])
            tt(Xre, Xre, t_a, ALU.add, ["X", "pw"], ["X"])
            tt(t_a, cfi, Si, ALU.mult, ["cf", "G"], ["pw"])
            tt(Xre, Xre, t_a, ALU.subtract, ["X", "pw"], ["X"])
            tt(t_a, cfr, Si, ALU.mult, ["cf", "G"], ["pw"])
            tt(Xim, Xim, t_a, ALU.add, ["X", "pw"], ["X"])
            tt(t_a, cfi, Sr, ALU.mult, ["cf", "G"], ["pw"])
            tt(Xim, Xim, t_a, ALU.add, ["X", "pw"], ["X"])

    k.handoff(SETUP_KEYS + STG_KEYS, R2_MIX + FO_KEYS)
    k.op("pool", lambda e: e.memset(hist.rearrange("p a b -> p (a b)"), 0.0), writes=["hist"])
    k.op("pool", lambda e: e.memset(carry.rearrange("p a b -> p (a b)"), 0.0), writes=[("carry", i) for i in range(88)])
    out_dmas = []
    for b in range(NB):
        load_x_block(b)
        k.dma("sp", maskb, bass.AP(maskd.tensor, b * N, [[0, 128], [1, N]]), writes=["maskb"])
        if stop == "x":
            k.emit()
            return nc
        rms_to_nb("norm_mix_pre")
        if stop == "norm":
            k.emit()
            return nc
        u_proj()
        if stop == "uproj":
            k.emit()
            return nc
        for cc in range(8):
            ci_ = cin[cc % 2]
            ck = f"cin{cc % 2}"
            psv, pkv = proj("w_in", 4 + cc, lambda kc: nb[:, kc, :], NBK, N)
            psg, pkg = proj("w_in", 12 + cc, lambda kc: nb[:, kc, :], NBK, N)
            act(tA, psg[:, 0:N], AF.Sigmoid, [pkg], ["tA"])
            cb_ = cinb[cc % 2]
            cp(cb_[:, 0:CK - 1], histb[:, cc, :], ["hist"], [ck], eng="act")
            tt(cb_[:, CK - 1:CK - 1 + N], psv[:, 0:N], tA, ALU.mult, [pkv, "tA"], [ck])
            cp(histb[:, cc, :], cb_[:, N:N + CK - 1], [ck], ["hist"], eng="act")
            a_ = acc[:, cc, :]
            psc_, pkc_ = newps()
            for half in range(2):
                wi_ = wbi[0] % NWB
                wbi[0] += 1
                k.dma("sp", wbuf[wi_], cdg[cc, :, half * 16:(half + 1) * 16, :], reads=[("cdg", cc, half)],
                      writes=[("wbuf", wi_)])
                for j in range(16):
                    k_ = half * 16 + j
                    if k_ >= CK:
                        continue
                    mm(psc_[:, 0:N], wbuf[wi_][:, j, :], cb_[:, k_:k_ + N], k_ == 0, k_ == CK - 1,
                       [("wbuf", wi_), ck], [pkc_])
            act(a_, psc_[:, 0:N], AF.Identity, [pkc_] + PVK, [("acc", cc)], bias=pv["conv_dw_b"][:, cc:cc + 1])
            s0 = sqb[0]
            s1 = sqb[1]
            act(s0, a_, AF.Copy, [("acc", cc)], [("sqb", 0)])
            act(s1, a_, AF.Square, [("acc", cc)], [("sqb", 1)])
            mm(pst1[:, 0:N], ones_b, s0, cc == 0, cc == 7, [("sqb", 0), "ones_b"], ["pst1"])
            mm(pst2[:, 0:N], ones_b, s1, cc == 0, cc == 7, [("sqb", 1), "ones_b"], ["pst2"])
        ts(rt[1], pst1[:, 0:N], 1.0 / CONVW, None, ALU.mult, None, ["pst1"], ["rt1"])
        tt(rt[2], rt[1], rt[1], ALU.mult, ["rt1"], ["rt2"])
        stt(rt[2], pst2[:, 0:N], 1.0 / CONVW, rt[2], ALU.mult, ALU.subtract, ["pst2", "rt2"], ["rt2"])
        ts(rt[2], rt[2], LNEPS, None, ALU.add, None, ["rt2"], ["rt2"])
        act(rt[2], rt[2], AF.Sqrt, ["rt2"], ["rt2"])
        k.op("dve", lambda e: e.reciprocal(out=rt[2], in_=rt[2]), ["rt2"], ["rt2"])
        for cc in range(8):
            a_ = acc[:, cc, :]
            tt(a_, a_, rt[1], ALU.subtract, [("acc", cc), "rt1"], [("acc", cc)])
            tt(a_, a_, rt[2], ALU.mult, [("acc", cc), "rt2"], [("acc", cc)])
            act(csil[:, cc, :], a_, AF.Silu, [("acc", cc)] + PVK, [("csil", cc)],
                scale=pv["conv_ln_g"][:, cc:cc + 1], bias=pv["conv_ln_b"][:, cc:cc + 1])
        if stop == "convb":
            k.emit()
            return nc
        ssm_pass(True)
        if stop == "ssmb":
            k.emit()
            return nc
        for ch in range(4):
            cp(u_b[:, ch, :], u_f[:, ch, :], [("u_f", ch)], [("u_b", ch)], eng="act")
        UBK = [("u_b", i) for i in range(4)]
        for oc in range(4):
            ps, pk = proj("w_ssm_glu", oc, lambda kc: u_b[:, kc, :], UBK, N)
            act(tA, ps[:, 0:N], AF.Sigmoid, [pk], ["tA"])
            tt(ysb[:, oc, :], u_f[:, oc, :], tA, ALU.mult, [("u_f", oc), "tA"], [("ysb", oc)])
        YK = [("ysb", i) for i in range(4)]
        CSK = [("csil", i) for i in range(8)]
        for oc in range(16):
            psa, pka = proj("w_in", 20 + oc, lambda kc: nb[:, kc, :], NBK, N)
            act(tA, psa[:, 0:N], AF.Sigmoid, [pka], ["tA"])
            psb, pkb = proj("w_ssm_proj", oc, lambda kc: ysb[:, kc, :], YK, N)
            tt(mtmp, psb[:, 0:N], tA, ALU.mult, [pkb, "tA"], ["mtmp"])
            psc, pkc = proj("w_in", 36 + oc, lambda kc: nb[:, kc, :], NBK, N)
            act(tB, psc[:, 0:N], AF.Sigmoid, [pkc], ["tB"])
            psd, pkd = proj("w_conv_proj", oc, lambda kc: csil[:, kc, :], CSK, N)
            tt(tB, psd[:, 0:N], tB, ALU.mult, [pkd, "tB"], ["tB"])
            tt(merged[:, oc, :], mtmp, tB, ALU.add, ["mtmp", "tB"], [("merged", oc)])
        MK = [("merged", i) for i in range(16)]
        if stop == "merge":
            k.emit()
            return nc
        k.handoff(R1_MIX, MO_KEYS)
        for oc in range(16):
            ps, pk = proj("w_mix_out", oc, lambda kc: merged[:, kc, :], MK, N)
            act(mo_f[:, oc, :], ps[:, 0:N], AF.Copy, [pk], [("mo", oc)])
            s = sqb[oc % 2]
            act(s, ps[:, 0:N], AF.Square, [pk], [("sqb", oc % 2)])
            mm(pst1[:, 0:N], ones_b, s, oc == 0, oc == 15, [("sqb", oc % 2), "ones_b"], ["pst1"])
        rstd_of(pst1[:, 0:N], "pst1", 1.0 / D, EPS, rt[0], "rt0")
        for oc in range(16):
            stt(mo_f[:, oc, :], mo_f[:, oc, :], pv["norm_mix_post"][:, oc:oc + 1], rt[0], ALU.mult, ALU.mult,
                [("mo", oc), "rt0"] + PVK, [("mo", oc)])
            tt(h[:, oc, :], h[:, oc, :], mo_f[:, oc, :], ALU.add, [("h", oc), ("mo", oc)], [("h", oc)])
        rms_to_nb("norm_ffn_pre", extra_mask=True)
        if stop == "mix":
            k.emit()
            return nc
        k.handoff(MO_KEYS, HID_KEYS)
        k.handoff(R2_MIX, FO_KEYS)
        for fc in range(44):
            gi_, vi_ = (1, 2) if fc % 2 == 0 else (3, 4)
            tg = rt[gi_]
            tv = rt[vi_]
            gk_, vk_ = f"rt{gi_}", f"rt{vi_}"
            for (ocx, tbuf, tkey) in ((fc, tg, gk_), (44 + fc, tv, vk_)):
                ps, pk = proj("w_ffn_up", ocx, lambda kc: nb[:, kc, :], NBK, N)
                w0 = ffnw[:, 0, ocx:ocx + 1]
                w1 = ffnw[:, 1, ocx:ocx + 1]
                w2 = ffnw[:, 2, ocx:ocx + 1]
                act(tbuf, ps[:, 0:N], AF.Identity, [pk] + PVK, [tkey], scale=w2,
                    bias=pv["ffn_dw_b"][:, ocx:ocx + 1])
                stt(tbuf[:, 1:N], ps[:, 0:N - 1], w1, tbuf[:, 1:N], ALU.mult, ALU.add, [pk, tkey] + PVK, [tkey])
                stt(tbuf[:, 2:N], ps[:, 0:N - 2], w0, tbuf[:, 2:N], ALU.mult, ALU.add, [pk, tkey] + PVK, [tkey])
                stt(tbuf[:, 0:1], carry[:, ocx, 1:2], w1, tbuf[:, 0:1], ALU.mult, ALU.add,
                    [("carry", ocx), tkey] + PVK, [tkey])
                stt(tbuf[:, 0:2], carry[:, ocx, 0:2], w0, tbuf[:, 0:2], ALU.mult, ALU.add,
                    [("carry", ocx), tkey] + PVK, [tkey])
                act(carry[:, ocx, :], ps[:, N - 2:N], AF.Copy, [pk], [("carry", ocx)])
            act(tg, tg, AF.Gelu_apprx_tanh, [gk_], [gk_])
            tt(hid[:, fc, :], tg, tv, ALU.mult, [gk_, vk_], [("hid", fc)])
        for oc in range(16):
            ps, pk = proj("w_ffn_down", oc, lambda kc: hid[:, kc, :], HID_KEYS, N)
            act(fo_f[:, oc, :], ps[:, 0:N], AF.Copy, [pk], [("fo", oc)])
            s = sqb[oc % 2]
            act(s, ps[:, 0:N], AF.Square, [pk], [("sqb", oc % 2)])
            mm(pst1[:, 0:N], ones_b, s, oc == 0, oc == 15, [("sqb", oc % 2), "ones_b"], ["pst1"])
        rstd_of(pst1[:, 0:N], "pst1", 1.0 / D, EPS, rt[0], "rt0")
        for oc in range(16):
            stt(fo_f[:, oc, :], fo_f[:, oc, :], pv["norm_ffn_post"][:, oc:oc + 1], rt[0], ALU.mult, ALU.mult,
                [("fo", oc), "rt0"] + PVK, [("fo", oc)])
            tt(h[:, oc, :], h[:, oc, :], fo_f[:, oc, :], ALU.add, [("h", oc), ("fo", oc)], [("h", oc)])
        for t0 in range(0, N, 128):
            rows = min(128, N - t0)
            g0 = b * N + t0
            lo = max(g0, H)
            hi = g0 + rows
            if hi <= lo:
                continue
            for c4 in range(4):
                ps, pk = newps()
                for c in range(4):
                    cidx = c4 * 4 + c
                    tr(ps[0:rows, c * 128:(c + 1) * 128], h[:, cidx, t0:t0 + rows], ident, [("h", cidx), "ident"], [pk])
                cp(xst[0:rows, c4 * 512:(c4 + 1) * 512], ps[0:rows, 0:512], [pk], ["xst"],
                   eng=("act" if c4 % 2 else "dve"))
            od = k.dma("sp", outd[lo - H:hi - H, :], xst[lo - g0:rows, :], reads=["xst"], writes=["out"])
            out_dmas.append(od)
        k.handoff(HID_KEYS, R1_MIX)
        k.handoff(FO_KEYS, R2_MIX)
    k.emit(final_wait_ops=out_dmas)
    return nc


def make_in_maps(inputs, NB, N, H, LQ, ncores=8):
    x = np.asarray(inputs["x"], dtype=np.float32)
    B, S, _ = x.shape
    nq = ncores // B
    assert S == nq * LQ
    W = NB * N
    meta = np.asarray(inputs["meta_tokens"], dtype=np.float32)
    shared = {}
    for n in WSPEC:
        shared[n] = np.ascontiguousarray(np.asarray(inputs[n], dtype=np.float32)[0])
    for n, s in VEC_INPUTS.items():
        shared[n] = np.ascontiguousarray(np.asarray(inputs[n], dtype=np.float32).reshape(s // 128, 128))
    shared["log_step"] = np.ascontiguousarray(np.asarray(inputs["log_step"], dtype=np.float32).reshape(16, 2))
    for n in ("ssm_b_re", "ssm_b_im", "ssm_c_re", "ssm_c_im"):
        shared[n] = np.ascontiguousarray(np.asarray(inputs[n], dtype=np.float32)[0])
    shared["conv_dw_w"] = np.ascontiguousarray(np.asarray(inputs["conv_dw_w"], dtype=np.float32)[0])
    shared["ffn_dw_w"] = np.ascontiguousarray(np.asarray(inputs["ffn_dw_w"], dtype=np.float32).reshape(3, 88, 128))
    maps = []
    for core in range(ncores):
        b, q = divmod(core, nq)
        hseq = np.concatenate([meta, x[b]], axis=0)
        a = NMETA + LQ * q - H
        xw = np.zeros((W, D), np.float32)
        mask = np.ones((1, W), np.float32)
        lo = max(a, 0)
        xw[lo - a:W] = hseq[lo:a + W]
        if a < 0:
            mask[0, 0:-a] = 0.0
        cco = np.zeros((1, 24), np.float32)
        for r in range(nq):
            e = q - 1 - r
            if 0 <= e <= 2:
                cco[0, r * 3 + e] = 1.0
        m = dict(shared)
        m["xw"] = xw
        m["mask"] = mask
        m["ccoef"] = cco
        maps.append(m)
    return maps


CFG = dict(NB=10, N=414, H=44, LQ=4096)


def kernel(**inputs):
    cfg = CFG
    nc = build(**cfg)
    maps = make_in_maps(inputs, **cfg)
    res = run_bass_kernel_spmd(nc, maps, core_ids=list(range(8)))
    x = inputs["x"]
    B, S, _ = x.shape
    out = np.empty((B, S, D), np.float32)
    nq = 8 // B
    for core in range(8):
        b, q = divmod(core, nq)
        out[b, q * cfg["LQ"]:(q + 1) * cfg["LQ"]] = res.results[core]["out"]
    return out
```

```python
import contextlib
import math
import numpy as np
import concourse.bass as bass
import concourse.mybir as mybir
from concourse.bass_utils import run_bass_kernel_spmd

F32 = mybir.dt.float32
BF16 = mybir.dt.bfloat16
I32 = mybir.dt.int32
AF = mybir.ActivationFunctionType
ALU = mybir.AluOpType

ENGS = ("pe", "act", "dve", "pool", "sp")
SEM_LIMIT = 30000
SAME_ENGINE_SYNC = {"act", "dve", "pool"}

D = 2048
NMETA = 16
SSMW = 512
CONVW = 1024
CK = 31
DFF = 5632
EPS = 1e-6
LNEPS = 1e-5
TWO_PI = 2.0 * math.pi


class Op:
    __slots__ = ("eng", "fn", "deps", "is_dma", "needs_inc", "tok", "dsem", "inc")

    def __init__(self, eng, fn, deps, is_dma):
        self.eng = eng
        self.fn = fn
        self.deps = deps
        self.is_dma = is_dma
        self.needs_inc = False
        self.tok = None
        self.dsem = None
        self.inc = 16 if is_dma else 1


class Res:
    __slots__ = ("lastw", "readers")

    def __init__(self):
        self.lastw = None
        self.readers = []


class K:
    def __init__(self, nc, ndma_sems=8):
        self.nc = nc
        self.ops = {e: [] for e in ENGS}
        self.res = {}
        self.ndma = ndma_sems
        self.ntiles = 0

    def sb(self, shape, dtype, name=None):
        self.ntiles += 1
        return self.nc.alloc_sbuf_tensor(name or f"sb{self.ntiles}", list(shape), dtype).ap()

    def ps(self, shape, dtype=F32, name=None):
        self.ntiles += 1
        return self.nc.alloc_psum_tensor(name or f"ps{self.ntiles}", list(shape), dtype).ap()

    def _r(self, key):
        r = self.res.get(key)
        if r is None:
            r = self.res[key] = Res()
        return r

    def handoff(self, from_keys, to_keys):
        acc = []
        for fk in from_keys:
            r = self._r(fk)
            if r.lastw is not None:
                acc.append(r.lastw)
            acc.extend(r.readers)
        for tk in to_keys:
            self._r(tk).readers.extend(acc)

    @staticmethod
    def _excl(key):
        if isinstance(key, tuple):
            return key[0] == "pm"
        return key in ("pst1", "pst2", "py")

    def _record(self, eng, fn, reads, writes, is_dma):
        xr = [x for x in reads if self._excl(x)]
        if xr:
            writes = list(writes) + [x for x in xr if x not in writes]
            reads = [x for x in reads if not self._excl(x)]
        deps = []
        for key in reads:
            r = self._r(key)
            if r.lastw is not None:
                deps.append(r.lastw)
        for key in writes:
            r = self._r(key)
            if r.lastw is not None:
                deps.append(r.lastw)
            deps.extend(r.readers)
        op = Op(eng, fn, deps, is_dma)
        for d in deps:
            d.needs_inc = True
        self.ops[eng].append(op)
        for key in writes:
            r = self._r(key)
            r.lastw = op
            r.readers = []
        for key in reads:
            r = self._r(key)
            r.readers = [x for x in r.readers if x.is_dma or x.eng != eng or is_dma]
            r.readers.append(op)
        return op

    def op(self, eng, fn, reads=(), writes=()):
        return self._record(eng, fn, reads, writes, False)

    def dma(self, eng, out, in_, reads=(), writes=(), **kw):
        def fn(e):
            return e.dma_start(out=out, in_=in_, **kw)
        op = self._record(eng, fn, reads, writes, True)
        op.needs_inc = True
        return op

    def emit(self, final_wait_ops=()):
        nc = self.nc
        es = contextlib.ExitStack()
        with es:
            def newsem(name):
                return es.enter_context(nc.semaphore(name))

            for e in ENGS:
                cur = None
                cnt = 0
                nsem = 0
                dcount = 0
                dsems = None
                dvals = None
                for op in self.ops[e]:
                    if op.is_dma:
                        if dsems is None:
                            dsems = [newsem(f"d_{e}_{i}") for i in range(self.ndma)]
                            dvals = [0] * self.ndma
                        slot = dcount % self.ndma
                        dcount += 1
                        dvals[slot] += op.inc
                        op.tok = (dsems[slot], dvals[slot])
                        op.dsem = (dsems[slot], dvals[slot] - op.inc)
                    elif op.needs_inc:
                        if cur is None or cnt >= SEM_LIMIT:
                            cur = newsem(f"c_{e}_{nsem}")
                            nsem += 1
                            cnt = 0
                        cnt += 1
                        op.tok = (cur, cnt)
            final_toks = [o.tok for o in final_wait_ops]

            def run_engine(e, h):
                waited = {}

                def wait(tok):
                    s, v = tok
                    if waited.get(id(s), 0) >= v:
                        return
                    waited[id(s)] = v
                    h.wait_ge(s, v)

                for op in self.ops[e]:
                    for d in op.deps:
                        if d.eng == e and (not d.is_dma) and e not in SAME_ENGINE_SYNC:
                            continue
                        wait(d.tok)
                    if op.is_dma:
                        s, prev = op.dsem
                        if prev > 0:
                            wait((s, prev))
                        ins = op.fn(h)
                        ins.then_inc(op.tok[0], op.inc)
                    else:
                        ins = op.fn(h)
                        if op.needs_inc:
                            ins.then_inc(op.tok[0], 1)
                if e == "sp":
                    for t in final_toks:
                        wait(t)

            with nc.Block() as block:
                @block.tensor
                def _(h):
                    run_engine("pe", h)

                @block.scalar
                def _(h):
                    run_engine("act", h)

                @block.vector
                def _(h):
                    run_engine("dve", h)

                @block.gpsimd
                def _(h):
                    run_engine("pool", h)

                @block.sync
                def _(h):
                    run_engine("sp", h)


WSPEC = {
    "w_in": (D, SSMW + 2 * CONVW + 2 * D),
    "w_ssm_glu": (SSMW, SSMW),
    "w_ssm_proj": (SSMW, D),
    "w_conv_proj": (CONVW, D),
    "w_mix_out": (D, D),
    "w_ffn_up": (D, 2 * DFF),
    "w_ffn_down": (DFF, D),
}
VEC_INPUTS = {
    "norm_mix_pre": D, "norm_mix_post": D, "norm_ffn_pre": D, "norm_ffn_post": D,
    "conv_dw_b": CONVW, "conv_ln_g": CONVW, "conv_ln_b": CONVW,
    "ffn_dw_b": 2 * DFF, "ssm_d": SSMW, "lam_re": 2048, "lam_im": 2048,
}


def build(NB, N, H, LQ, ncores=8, use_cc=True, stop=None):
    W = NB * N
    assert W == H + LQ and N % 2 == 0
    Ns = N // 2
    nc = bass.Bass("TRN2", target_bir_lowering=False, dynamic_dma_scratch_size=8192)
    k = K(nc)
    KC = D // 128

    def din(name, shape):
        return nc.dram_tensor(name, list(shape), F32, kind="ExternalInput").ap()

    xw = din("xw", [W, D])
    maskd = din("mask", [1, W])
    ccoefd = din("ccoef", [1, 24])
    wd = {n: din(n, list(s)) for n, s in WSPEC.items()}
    vd = {n: din(n, [s // 128, 128]) for n, s in VEC_INPUTS.items()}
    log_step_d = din("log_step", [16, 2])
    b_re_d = din("ssm_b_re", [32, 64, 16])
    b_im_d = din("ssm_b_im", [32, 64, 16])
    c_re_d = din("ssm_c_re", [32, 16, 64])
    c_im_d = din("ssm_c_im", [32, 16, 64])
    conv_w_d = din("conv_dw_w", [CK, CONVW])
    ffn_w_d = din("ffn_dw_w", [3, 88, 128])
    outd = nc.dram_tensor("out", [LQ, D], F32, kind="ExternalOutput").ap()
    wscr = {n: nc.dram_tensor("scr_" + n, [s[1] // 128, 128, s[0] // 128, 128], BF16,
                              kind="Internal").ap() for n, s in WSPEC.items()}
    NQ = ncores // 2
    cdg = nc.dram_tensor("scr_convdiag", [8, 128, 32, 128], BF16, kind="Internal").ap()
    cc_in = nc.dram_tensor("cc_in", [8, 512], F32, kind="Internal").ap()
    cc_out = nc.dram_tensor("cc_out", [NQ * 8, 512], F32, kind="Internal", addr_space="Local").ap()

    def act(out, in_, func, reads, writes, scale=None, bias=None):
        kw = {}
        if scale is not None:
            kw["scale"] = scale
        if bias is not None:
            kw["bias"] = bias
        return k.op("act", lambda e: e.activation(out=out, in_=in_, func=func, **kw), reads, writes)

    def tt(out, a, b, op, reads, writes, eng="dve"):
        return k.op(eng, lambda e: e.tensor_tensor(out=out, in0=a, in1=b, op=op), reads, writes)

    def ts(out, a, s1, s2, op0, op1, reads, writes, eng="dve"):
        if op1 is None:
            return k.op(eng, lambda e: e.tensor_scalar(out=out, in0=a, scalar1=s1, scalar2=None, op0=op0),
                        reads, writes)
        return k.op(eng, lambda e: e.tensor_scalar(out=out, in0=a, scalar1=s1, scalar2=s2, op0=op0, op1=op1),
                    reads, writes)

    def stt(out, a, s, b, op0, op1, reads, writes):
        return k.op("dve", lambda e: e.scalar_tensor_tensor(out=out, in0=a, scalar=s, in1=b, op0=op0, op1=op1),
                    reads, writes)

    def cp(out, in_, reads, writes, eng="dve"):
        if eng == "act":
            return act(out, in_, AF.Copy, reads, writes)
        return k.op(eng, lambda e: e.tensor_copy(out=out, in_=in_), reads, writes)

    def mm(out, lhsT, rhs, start, stop, reads, writes):
        return k.op("pe", lambda e: e.matmul(out, lhsT=lhsT, rhs=rhs, start=start, stop=stop), reads, writes)

    def tr(out, in_, ident_ap, reads, writes):
        return k.op("pe", lambda e: e.transpose(out=out, in_=in_, identity=ident_ap), reads, writes)

    NPM = 5
    pm = [k.ps([128, 512], F32, name=f"pm{i}") for i in range(NPM)]
    pst1 = k.ps([128, 512], F32, name="pst1")
    pst2 = k.ps([128, 512], F32, name="pst2")
    py = k.ps([128, 512], F32, name="py")
    pmi = [0]

    def newps():
        i = pmi[0] % NPM
        pmi[0] += 1
        return pm[i], ("pm", i)

    ident = k.sb([128, 128], F32, "ident")
    k.op("pool", lambda e: e.memset(ident, 0.0), writes=["ident"])
    k.op("pool", lambda e: e.affine_select(out=ident, in_=ident, pattern=[[-1, 128]], compare_op=ALU.not_equal,
                                           fill=1.0, base=0, channel_multiplier=1), reads=["ident"], writes=["ident"])
    ones_b = k.sb([128, 128], BF16, "ones_b")
    k.op("pool", lambda e: e.memset(ones_b, 1.0), writes=["ones_b"])

    R1F = (44 * N * 2 + 3) // 4
    R1F = max(R1F, 6400)
    R1 = k.sb([128, R1F], F32, "R1")
    R2F = max(16 * N, 6400)
    R2 = k.sb([128, R2F], F32, "R2")

    class Carver:
        def __init__(self, region):
            self.r = region
            self.off = 0

        def f32(self, a, n):
            v = self.r[:, self.off:self.off + a * n]
            self.off += a * n
            return v.rearrange("p (a n) -> p a n", n=n) if a > 1 else v

        def bf(self, a, n):
            words = (a * n + 1) // 2
            v = self.r[:, self.off:self.off + words].bitcast(BF16)[:, 0:a * n]
            self.off += words
            return v.rearrange("p (a n) -> p a n", n=n) if a > 1 else v

    c1 = Carver(R1)
    u_f = c1.f32(4, N)
    u_b = c1.bf(4, N)
    cin = [c1.f32(1, N + CK - 1) for _ in range(2)]
    cinb = [c_.bitcast(BF16)[:, 0:N + CK - 1] for c_ in cin]
    csil = c1.bf(8, N)
    stmp = [c1.f32(1, Ns) for _ in range(6)]
    qb = [c1.bf(1, Ns) for _ in range(4)]
    ysb = c1.bf(4, N)
    tA = c1.f32(1, N)
    tB = c1.f32(1, N)
    mtmp = c1.f32(1, N)
    assert c1.off <= R1F, (c1.off, R1F)
    mo_f = R1[:, 0:16 * N].rearrange("p (a n) -> p a n", n=N) if 16 * N <= R1F else None
    assert mo_f is not None
    hid = R1[:, 0:22 * N].bitcast(BF16).rearrange("p (a n) -> p a n", n=N)
    stg_f = [R2[:, i * 2048:(i + 1) * 2048] for i in range(2)]
    stg_b = [R2[:, 4096 + i * 1024:4096 + (i + 1) * 1024].bitcast(BF16) for i in range(2)]
    assert 4096 + 2048 <= R2F
    c2 = Carver(R2)
    acc = c2.f32(8, N)
    merged = c2.bf(16, N)
    assert c2.off <= R2F
    fo_f = R2[:, 0:16 * N].rearrange("p (a n) -> p a n", n=N)
    R1_MIX = ["u_f", "u_b", "cin0", "cin1", "csil", "stmp", "qb", "ysb", "tA", "tB", "mtmp"] + \
             [("u_f", i) for i in range(4)] + [("u_b", i) for i in range(4)] + [("csil", i) for i in range(8)] + \
             [("ysb", i) for i in range(4)] + [("stmp", i) for i in range(6)] + [("qb", i) for i in range(4)]
    MO_KEYS = [("mo", i) for i in range(16)]
    HID_KEYS = [("hid", i) for i in range(44)]
    R2_MIX = [("acc", i) for i in range(8)] + [("merged", i) for i in range(16)]
    FO_KEYS = [("fo", i) for i in range(16)]
    STG_KEYS = ["stgf0", "stgf1", "stgb0", "stgb1"]

    csu = [0]

    def sb2(shape, dtype=F32):
        parts = shape[0]
        n = 1
        for d_ in shape[1:]:
            n *= d_
        v = R2[0:parts, csu[0]:csu[0] + n]
        csu[0] += n
        assert csu[0] <= R2F, csu[0]
        if dtype != F32:
            v = v.bitcast(dtype)
        if len(shape) == 3:
            v = v.rearrange("p (a b) -> p a b", b=shape[2])
        return v

    h = k.sb([128, 16, N], F32, "h")
    nb = k.sb([128, 16, N], BF16, "nb")
    sqb = [k.sb([128, N], BF16, f"sqb{i}") for i in range(2)]
    rt = [k.sb([128, N], F32, f"rt{i}") for i in range(5)]
    maskb = k.sb([128, N], F32, "maskb")
    cosT = k.sb([128, 16, Ns], F32, "cosT")
    sinT = k.sb([128, 16, Ns], F32, "sinT")
    CT = k.sb([128, 16, 3, 128], BF16, "CT")
    BT = k.sb([128, 16, 2, 128], BF16, "BT")
    NWB = 9
    wbuf = [k.sb([128, 16, 128], BF16, f"wbuf{i}") for i in range(NWB)]
    xst = k.sb([128, D], F32, "xst")
    stmp2 = [xst[:, i * Ns:(i + 1) * Ns] for i in range(6)]
    qb2 = [xst[:, 6 * Ns + i * ((Ns + 1) // 2):6 * Ns + (i + 1) * ((Ns + 1) // 2)].bitcast(BF16)[:, 0:Ns] for i in range(4)]
    assert 6 * Ns + 4 * ((Ns + 1) // 2) <= D
    SET2 = [("stmp2", i) for i in range(6)] + [("qb2", i) for i in range(4)]
    hist = k.sb([128, 8, CK - 1], F32, "hist")
    carry = k.sb([128, 88, 2], F32, "carry")
    Xre = k.sb([128, 16], F32, "Xre")
    Xim = k.sb([128, 16], F32, "Xim")
    Wend = k.sb([128, 2, 16], F32, "Wend")
    hx = [k.sb([128, 16], F32, f"hx{i}") for i in range(4)]

    pv = {}

    def load_vec(name):
        n = VEC_INPUTS[name] // 128
        t_in = k.sb([n, 128], F32, "vin_" + name)
        k.dma("sp", t_in, vd[name], writes=["vin_" + name])
        ps, pk = newps()
        tr(ps[:, 0:n], t_in, ident[0:n, 0:n], ["vin_" + name, "ident"], [pk])
        t = k.sb([128, n], F32, "pv_" + name)
        cp(t, ps[:, 0:n], [pk], ["pv_" + name])
        pv[name] = t

    for name in VEC_INPUTS:
        load_vec(name)
    PVK = ["pv_" + n for n in VEC_INPUTS]

    cw_in = sb2([CK, CONVW])
    k.dma("sp", cw_in, conv_w_d, writes=["cw_in"])
    convw = k.sb([128, 8, CK], F32, "convw")
    ps, pk = newps()
    for cc in range(8):
        tr(ps[:, cc * CK:(cc + 1) * CK], cw_in[:, cc * 128:(cc + 1) * 128], ident[0:CK, 0:CK], ["cw_in", "ident"], [pk])
    cp(convw.rearrange("p a b -> p (a b)"), ps[:, 0:8 * CK], [pk], ["convw"])
    fw_in = sb2([88, 3, 128])
    k.dma("sp", fw_in, ffn_w_d.rearrange("k r c -> r k c"), writes=["fw_in"])
    ffnw = k.sb([128, 3, 88], F32, "ffnw")
    ps, pk = newps()
    for kk in range(3):
        tr(ps[:, kk * 88:(kk + 1) * 88], fw_in[:, kk, :], ident[0:88, 0:88], ["fw_in", "ident"], [pk])
    cp(ffnw.rearrange("p a b -> p (a b)"), ps[:, 0:264], [pk], ["ffnw"])
    PVK += ["convw", "ffnw"]
    histb = hist.rearrange("p a b -> p (a b)").bitcast(BF16)[:, 0:8 * (CK - 1)].rearrange("p (a b) -> p a b", b=CK - 1)
    for cc in range(8):
        for half in range(2):
            stgw = wbuf[half]
            for j in range(16):
                k_ = half * 16 + j
                if k_ < CK:
                    k.op("pool", lambda e, o_=stgw[:, j, :], s_=convw[:, cc, k_:k_ + 1]: e.tensor_scalar(
                        out=o_, in0=ident, scalar1=s_, scalar2=1.0, op0=ALU.mult, op1=ALU.mult),
                        ["ident", "convw"], [("wbuf", half)])
                else:
                    k.op("pool", lambda e, o_=stgw[:, j, :]: e.memset(o_, 0.0), [], [("wbuf", half)])
            k.dma("sp", cdg[cc, :, half * 16:(half + 1) * 16, :], stgw, reads=[("wbuf", half)],
                  writes=[("cdg", cc, half)])

    if stop == "vec":
        k.emit()
        return nc
    sm = {}

    def smt(name, shape=(128, 16)):
        sm[name] = k.sb(list(shape), F32, "sm_" + name)
        return sm[name]

    ls_in = k.sb([16, 2], F32, "ls_in")
    k.dma("sp", ls_in, log_step_d, writes=["ls_in"])
    ls_x = k.sb([16, 2, 64], F32, "ls_x")
    cp(ls_x, ls_in.unsqueeze(2).to_broadcast([16, 2, 64]), ["ls_in"], ["ls_x"])
    ps, pk = newps()
    tr(ps[:, 0:16], ls_x.rearrange("a g p -> a (g p)"), ident[0:16, 0:16], ["ls_x", "ident"], [pk])
    lsT = smt("lsT")
    cp(lsT, ps[:, 0:16], [pk], ["sm"])
    SM = ["sm"] + PVK
    lr = pv["lam_re"]
    li = pv["lam_im"]
    step = smt("step")
    act(step, lsT, AF.Exp, SM, ["sm"])
    lrs = smt("lrs")
    tt(lrs, lr, step, ALU.mult, SM, ["sm"])
    mag = smt("mag")
    act(mag, lrs, AF.Exp, SM, ["sm"])
    th = smt("th")
    tt(th, li, step, ALU.mult, SM, ["sm"])

    sc_tmp = {}

    def sincos(theta, cos_out, sin_out, shape, key, temps=None):
        if temps is not None:
            t0, t1, ti = temps
        else:
            if shape not in sc_tmp:
                sc_tmp[shape] = (sb2(list(shape)), sb2(list(shape)), sb2(list(shape), I32))
            t0, t1, ti = sc_tmp[shape]
        for shift, outp in ((0.25, cos_out), (0.0, sin_out)):
            ts(t0, theta, 1.0 / TWO_PI, 0.5 + shift, ALU.mult, ALU.add, [key], [key + "_t0"])
            cp(ti, t0, [key + "_t0"], [key + "_ti"])
            cp(t1, ti, [key + "_ti"], [key + "_t1"])
            tt(t0, t0, t1, ALU.subtract, [key + "_t0", key + "_t1"], [key + "_t0"])
            ts(t1, t0, 0.0, None, ALU.is_lt, None, [key + "_t0"], [key + "_t1"])
            tt(t0, t0, t1, ALU.add, [key + "_t0", key + "_t1"], [key + "_t0"])
            ts(t0, t0, -0.5, -0.4999999, ALU.add, ALU.max, [key + "_t0"], [key + "_t0"])
            ts(t0, t0, 0.4999999, None, ALU.min, None, [key + "_t0"], [key + "_t0"])
            act(outp, t0, AF.Sin, [key + "_t0"], [key], scale=TWO_PI)

    cth = smt("cth")
    sth = smt("sth")
    sincos(th, cth, sth, (128, 16), "sm")
    ar = smt("ar")
    ai = smt("ai")
    tt(ar, mag, cth, ALU.mult, SM, ["sm"])
    tt(ai, mag, sth, ALU.mult, SM, ["sm"])
    den = smt("den")
    t_a = smt("t_a")
    t_b = smt("t_b")
    tt(den, lr, lr, ALU.mult, SM, ["sm"])
    tt(t_a, li, li, ALU.mult, SM, ["sm"])
    tt(den, den, t_a, ALU.add, SM, ["sm"])
    rden = smt("rden")
    k.op("dve", lambda e: e.reciprocal(out=rden, in_=den), SM, ["sm"])
    am1 = smt("am1")
    ts(am1, ar, -1.0, None, ALU.add, None, SM, ["sm"])
    cr = smt("cr")
    ci = smt("ci")
    tt(t_a, am1, lr, ALU.mult, SM, ["sm"])
    tt(t_b, ai, li, ALU.mult, SM, ["sm"])
    tt(t_a, t_a, t_b, ALU.add, SM, ["sm"])
    tt(cr, t_a, rden, ALU.mult, SM, ["sm"])
    tt(t_a, ai, lr, ALU.mult, SM, ["sm"])
    tt(t_b, am1, li, ALU.mult, SM, ["sm"])
    tt(t_a, t_a, t_b, ALU.subtract, SM, ["sm"])
    tt(ci, t_a, rden, ALU.mult, SM, ["sm"])

    b_re = sb2([128, 16, 16])
    b_im = sb2([128, 16, 16])
    for t_, d_ in ((b_re, b_re_d), (b_im, b_im_d)):
        src = bass.AP(d_.tensor, 0, [[16, 128], [2048, 16], [1, 16]])
        k.dma("sp", t_, src, writes=["sm"], allow_slow_non_contiguous=True)
    crb = cr.unsqueeze(2).to_broadcast([128, 16, 16])
    cib = ci.unsqueeze(2).to_broadcast([128, 16, 16])
    Bb_re = sb2([128, 16, 16])
    Bb_im = sb2([128, 16, 16])
    t3a = sb2([128, 16, 16])
    tt(Bb_re, b_re, crb, ALU.mult, SM, ["sm"])
    tt(t3a, b_im, cib, ALU.mult, SM, ["sm"])
    tt(Bb_re, Bb_re, t3a, ALU.subtract, SM, ["sm"])
    tt(Bb_im, b_im, crb, ALU.mult, SM, ["sm"])
    tt(t3a, b_re, cib, ALU.mult, SM, ["sm"])
    tt(Bb_im, Bb_im, t3a, ALU.add, SM, ["sm"])
    k.op("pool", lambda e: e.memset(BT.rearrange("p a b c -> p (a b c)"), 0.0), writes=["BT"])
    xb = sb2([128, 128])
    for P in range(16):
        m = P % 4
        for v, Bsrc in ((0, Bb_re), (1, Bb_im)):
            k.op("dve", lambda e: e.memset(xb, 0.0), writes=["xb"])
            for g2 in range(2):
                col = m * 32 + g2 * 16
                cp(xb[g2 * 64:(g2 + 1) * 64, col:col + 16], Bsrc[g2 * 64:(g2 + 1) * 64, P, :], SM, ["xb"])
            ps, pk = newps()
            tr(ps[:, 0:128], xb, ident, ["xb", "ident"], [pk])
            hb = (m // 2) * 64
            cp(BT[hb:hb + 64, P, v, :], ps[hb:hb + 64, 0:128], [pk], ["BT"])
    k.op("pool", lambda e: e.memset(CT.rearrange("p a b c -> p (a b c)"), 0.0), writes=["CT"])
    crow = sb2([128, 2, 128])
    csb = sb2([128, 8, 16])
    for v_re, d_ in ((True, c_re_d), (False, c_im_d)):
        for g2 in range(2):
            src = bass.AP(d_.tensor, g2 * 1024, [[2048, 16], [64, 16], [1, 64]])
            for half in range(2):
                for P8 in range(8):
                    srch = bass.AP(d_.tensor, g2 * 1024 + (half * 8 + P8) * 2048, [[64, 16], [1, 64]])
                    k.dma("sp", crow[P8 * 16:(P8 + 1) * 16, half, g2 * 64:(g2 + 1) * 64], srch, writes=["crow"])
        for half in range(2):
            ps, pk = newps()
            tr(ps[:, 0:128], crow[:, half, :], ident, ["crow", "ident"], [pk])
            cp(csb.rearrange("p a b -> p (a b)"), ps[:, 0:128], [pk], ["csb"])
            for P8 in range(8):
                P = half * 8 + P8
                m = P % 4
                for g2 in range(2):
                    col = m * 32 + g2 * 16
                    src_ = csb[g2 * 64:(g2 + 1) * 64, P8, :]
                    if v_re:
                        cp(CT[g2 * 64:(g2 + 1) * 64, P, 0, col:col + 16], src_, ["csb"], ["CT"])
                        ts(CT[g2 * 64:(g2 + 1) * 64, P, 1, col:col + 16], src_, -1.0, None, ALU.mult, None,
                           ["csb"], ["CT"])
                    else:
                        ts(CT[g2 * 64:(g2 + 1) * 64, P, 2, col:col + 16], src_, -1.0, None, ALU.mult, None,
                           ["csb"], ["CT"])
    io_i = sb2([128, Ns], I32)
    k.op("pool", lambda e: e.iota(io_i, pattern=[[1, Ns]], base=1, channel_multiplier=0), writes=["io"])
    io_f = sb2([128, Ns])
    cp(io_f, io_i, ["io"], ["io_f"])
    TW = 8 * Ns
    angb = R1[:, 0:TW]
    tmpb = (R1[:, TW:2 * TW], R1[:, 2 * TW:3 * TW], R1[:, 3 * TW:4 * TW].bitcast(I32))
    assert 4 * TW <= R1F
    for hh in range(2):
        tt(angb.rearrange("p (a b) -> p a b", b=Ns), io_f.unsqueeze(1).to_broadcast([128, 8, Ns]),
           th[:, hh * 8:(hh + 1) * 8].unsqueeze(2).to_broadcast([128, 8, Ns]), ALU.mult, ["io_f"] + SM, ["tab"])
        sincos(angb, cosT[:, hh * 8:(hh + 1) * 8, :].rearrange("p a b -> p (a b)"),
               sinT[:, hh * 8:(hh + 1) * 8, :].rearrange("p a b -> p (a b)"), (128, TW), "tab", temps=tmpb)
    TAB = ["tab", "sm"]
    k.handoff(["tab", "tab_t0", "tab_t1", "tab_ti"], R1_MIX + MO_KEYS + HID_KEYS)

    if stop == "ssm":
        k.emit()
        return nc
    SETUP_KEYS = ["cw_in", "fw_in", "sm", "xb", "crow", "csb", "io", "io_f", "tab", "tab_t0", "tab_t1", "tab_ti",
                  "sm_t0", "sm_t1", "sm_ti"]
    k.handoff(SETUP_KEYS, STG_KEYS)
    pieces = []
    for name, (Kd, Nout) in WSPEC.items():
        for oc in range(Nout // 128):
            pieces.append((name, oc))

    def conv_piece(name, oc, gate=None):
        src = wd[name][:, oc * 128:(oc + 1) * 128].rearrange("(k p) c -> p k c", p=128)
        k.dma("pool", wscr[name][oc], src, reads=([gate] if gate is not None else []), writes=[("scr", name, oc)])

    def conv_some(n, gate=None):
        for _ in range(n):
            if pieces:
                conv_piece(*pieces.pop(0), gate=gate)

    if stop == "conv":
        k.emit()
        return nc
    wbi = [0]

    def getw(name, oc, k0=0, k1=None):
        kcn = WSPEC[name][0] // 128
        if k1 is None:
            k1 = kcn
        i = wbi[0] % NWB
        wbi[0] += 1
        k.dma("sp", wbuf[i][:, 0:k1 - k0, :], wscr[name][oc, :, k0:k1, :], reads=[("scr", name, oc)],
              writes=[("wbuf", i)])
        return wbuf[i], ("wbuf", i)

    def proj(name, oc, rhs_fn, rhs_keys, ncols, out_ps=None):
        kcn = WSPEC[name][0] // 128
        if out_ps is None:
            ps, pk = newps()
        else:
            ps, pk = out_ps
        for k0 in range(0, kcn, 16):
            k1 = min(kcn, k0 + 16)
            wt, wk = getw(name, oc, k0, k1)
            for kc in range(k0, k1):
                mm(ps[:, 0:ncols], wt[:, kc - k0, :], rhs_fn(kc), kc == 0, kc == kcn - 1,
                   [wk] + list(rhs_keys), [pk])
        return ps, pk

    def load_x_block(b, gate_key=None):
        for t0 in range(0, N, 128):
            rows = min(128, N - t0)
            k.dma("sp", xst[0:rows, :], xw[b * N + t0:b * N + t0 + rows, :],
                  writes=["xst"] + ([gate_key] if (gate_key is not None and t0 == 0) else []))
            for c4 in range(4):
                ps, pk = newps()
                for c in range(4):
                    cidx = c4 * 4 + c
                    tr(ps[:, c * 128:c * 128 + rows], xst[0:rows, cidx * 128:(cidx + 1) * 128], ident[0:rows, 0:rows],
                       ["xst", "ident"], [pk])
                cp(h[:, c4 * 4:(c4 + 1) * 4, t0:t0 + rows],
                   ps[:, 0:512].rearrange("p (c r) -> p c r", r=128)[:, :, 0:rows], [pk],
                   [("h", c4 * 4 + c) for c in range(4)], eng=("act" if c4 % 2 else "dve"))

    def rstd_of(psum_ap, pkey, scale, eps, out, okey):
        ts(out, psum_ap, scale, eps, ALU.mult, ALU.add, [pkey], [okey])
        act(out, out, AF.Sqrt, [okey], [okey])
        k.op("dve", lambda e: e.reciprocal(out=out, in_=out), [okey], [okey])

    def rms_to_nb(gname, extra_mask=False):
        for c in range(16):
            s = sqb[c % 2]
            act(s, h[:, c, :], AF.Square, [("h", c)], [("sqb", c % 2)])
            mm(pst1[:, 0:N], ones_b, s, c == 0, c == 15, [("sqb", c % 2), "ones_b"], ["pst1"])
        rstd_of(pst1[:, 0:N], "pst1", 1.0 / D, EPS, rt[0], "rt0")
        if extra_mask:
            tt(rt[0], rt[0], maskb, ALU.mult, ["rt0", "maskb"], ["rt0"])
        for c in range(16):
            stt(nb[:, c, :], h[:, c, :], pv[gname][:, c:c + 1], rt[0], ALU.mult, ALU.mult,
                [("h", c), "rt0"] + PVK, [("nb", c)])

    NBK = [("nb", c) for c in range(16)]

    def u_proj():
        for oc in range(4):
            ps, pk = proj("w_in", oc, lambda kc: nb[:, kc, :], NBK, N)
            act(u_f[:, oc, :], ps[:, 0:N], AF.Copy, [pk], [("u_f", oc)])
            cp(u_b[:, oc, :], ps[:, 0:N], [pk], [("u_b", oc)])

    def ssm_pass(with_out, extract=None):
        k.handoff(["xst"], SET2)
        _ssm_pass(with_out, extract)
        k.handoff(SET2, ["xst"])

    def _ssm_pass(with_out, extract=None):
        for sbi in range(2):
            c0 = sbi * Ns
            def pair_gen(P, sbi=sbi, c0=c0):
                ch = P // 4
                hb = ((P % 4) // 2) * 64
                psr, pkr = newps()
                psi, pki = newps()
                mm(psr[:, 0:Ns], BT[hb:hb + 64, P, 0, :], u_b[hb:hb + 64, ch, c0:c0 + Ns], True, True,
                   ["BT", ("u_b", ch)], [pkr])
                mm(psi[:, 0:Ns], BT[hb:hb + 64, P, 1, :], u_b[hb:hb + 64, ch, c0:c0 + Ns], True, True,
                   ["BT", ("u_b", ch)], [pki])
                yield
                cs = cosT[:, P, :]
                sn = sinT[:, P, :]
                if P % 2 == 0:
                    m1, m2, m3, m4, wr, wi = stmp
                    K_ = [("stmp", i) for i in range(6)]
                    qbs = qb
                    Q_ = [("qb", i) for i in range(4)]
                else:
                    m1, m2, m3, m4, wr, wi = stmp2
                    K_ = [("stmp2", i) for i in range(6)]
                    qbs = qb2
                    Q_ = [("qb2", i) for i in range(4)]
                tt(m1, psr[:, 0:Ns], cs, ALU.mult, [pkr] + TAB, [K_[0]])
                yield
                tt(m2, psi[:, 0:Ns], sn, ALU.mult, [pki] + TAB, [K_[1]])
                yield
                tt(m3, psi[:, 0:Ns], cs, ALU.mult, [pki] + TAB, [K_[2]])
                yield
                tt(m4, psr[:, 0:Ns], sn, ALU.mult, [pkr] + TAB, [K_[3]])
                yield
                tt(m1, m1, m2, ALU.add, [K_[0], K_[1]], [K_[0]])
                yield
                tt(m3, m3, m4, ALU.subtract, [K_[2], K_[3]], [K_[2]])
                yield
                dec = mag[:, P:P + 1].to_broadcast([128, Ns])
                k.op("dve", lambda e, wr=wr, m1=m1, dec=dec, P=P: e.tensor_tensor_scan(
                    out=wr, data0=dec, data1=m1, initial=Xre[:, P:P + 1], op0=ALU.mult, op1=ALU.add),
                    [K_[0], "X"] + TAB, [K_[4]])
                yield
                k.op("dve", lambda e, wi=wi, m3=m3, dec=dec, P=P: e.tensor_tensor_scan(
                    out=wi, data0=dec, data1=m3, initial=Xim[:, P:P + 1], op0=ALU.mult, op1=ALU.add),
                    [K_[2], "X"] + TAB, [K_[5]])
                yield
                col = Ns - 1
                if extract is not None and extract[0] == sbi:
                    col = extract[1]
                act(Wend[:, 0, P:P + 1], wr[:, col:col + 1], AF.Copy, [K_[4]], ["Wend"])
                act(Wend[:, 1, P:P + 1], wi[:, col:col + 1], AF.Copy, [K_[5]], ["Wend"])
                yield
                if with_out:
                    tt(qbs[0], wr, cs, ALU.mult, [K_[4]] + TAB, [Q_[0]], eng="pool")
                    tt(qbs[1], wi, sn, ALU.mult, [K_[5]] + TAB, [Q_[1]], eng="pool")
                    tt(qbs[2], wi, cs, ALU.mult, [K_[5]] + TAB, [Q_[2]], eng="pool")
                    tt(qbs[3], wr, sn, ALU.mult, [K_[4]] + TAB, [Q_[3]], eng="pool")
                    yield
                    for qi, v in ((0, 0), (1, 1), (2, 2), (3, 2)):
                        mm(py[:, c0:c0 + Ns], CT[:, P, v, :], qbs[qi], (P % 4 == 0 and qi == 0),
                           (P % 4 == 3 and qi == 3), [Q_[qi], "CT"], ["py"])
                    if P % 4 == 3:
                        stt(u_f[:, ch, c0:c0 + Ns], u_f[:, ch, c0:c0 + Ns], pv["ssm_d"][:, ch:ch + 1],
                            py[:, c0:c0 + Ns], ALU.mult, ALU.add, ["py", ("u_f", ch)] + PVK, [("u_f", ch)])
                        act(u_f[:, ch, c0:c0 + Ns], u_f[:, ch, c0:c0 + Ns], AF.Gelu_apprx_tanh,
                            [("u_f", ch)], [("u_f", ch)])

            for P0 in range(0, 16, 2):
                gens = [pair_gen(P0), pair_gen(P0 + 1)]
                while gens:
                    for g_ in list(gens):
                        try:
                            next(g_)
                        except StopIteration:
                            gens.remove(g_)
            col = Ns - 1
            if extract is not None and extract[0] == sbi:
                col = extract[1]
            cN = cosT[:, :, col]
            sN = sinT[:, :, col]
            tt(hx[0], Wend[:, 0, :], cN, ALU.mult, ["Wend"] + TAB, ["hx0"])
            tt(hx[1], Wend[:, 1, :], sN, ALU.mult, ["Wend"] + TAB, ["hx1"])
            tt(hx[2], Wend[:, 1, :], cN, ALU.mult, ["Wend"] + TAB, ["hx2"])
            tt(hx[3], Wend[:, 0, :], sN, ALU.mult, ["Wend"] + TAB, ["hx3"])
            tt(Xre, hx[0], hx[1], ALU.subtract, ["hx0", "hx1"], ["X"])
            tt(Xim, hx[2], hx[3], ALU.add, ["hx2", "hx3"], ["X"])
            if extract is not None and extract[0] == sbi:
                return

    k.op("dve", lambda e: e.memset(Xre, 0.0), writes=["X"])
    k.op("dve", lambda e: e.memset(Xim, 0.0), writes=["X"])
    if use_cc:
        eb, ecol = divmod(LQ - 1, N)
        esb, ecl = divmod(ecol, Ns)
        conv_some(4)
        per_blk = (len(pieces) + eb - 1) // max(eb, 1)
        for b in range(eb + 1):
            load_x_block(b, gate_key=("p1", b))
            conv_some(per_blk, gate=("p1", b))
            rms_to_nb("norm_mix_pre")
            u_proj()
            ssm_pass(False, extract=(esb, ecl) if b == eb else None)
    conv_some(len(pieces))
    if use_cc:
        d1 = k.dma("sp", cc_in[0:4, :].rearrange("a (p c) -> (a p) c", c=16), Xre, reads=["X"], writes=["cc_in"])
        d2 = k.dma("sp", cc_in[4:8, :].rearrange("a (p c) -> (a p) c", c=16), Xim, reads=["X"],
                   writes=["cc_in"])

        def ccfn(e):
            return e.collective_compute("AllGather", ALU.bypass,
                                        replica_groups=[list(range(g * NQ, (g + 1) * NQ)) for g in range(2)],
                                        ins=[cc_in], outs=[cc_out])
        ccop = k._record("pool", ccfn, ["cc_in"], ["cc_out"], True)
        ccop.needs_inc = True
        ccop.inc = 1
        G = k.sb([128, NQ, 2, 16], F32, "G")
        for r in range(NQ):
            for v in range(2):
                k.dma("sp", G[:, r, v, :],
                      cc_out[r * 8 + v * 4:r * 8 + v * 4 + 4, :].rearrange("a (p c) -> (a p) c", c=16),
                      reads=["cc_out"], writes=["G"])
        cco = k.sb([128, 24], F32, "cco")
        k.dma("sp", cco, bass.AP(ccoefd.tensor, 0, [[0, 128], [1, 24]]), writes=["cco"])
        nsq = int(round(math.log2(LQ)))
        assert 2 ** nsq == LQ
        p1r = smt("p1r"); p1i = smt("p1i"); p2r = smt("p2r"); p2i = smt("p2i")
        cp(p1r, ar, SM, ["pw"])
        cp(p1i, ai, SM, ["pw"])
        PW = ["pw", "sm"]

        def csq(outr, outi, inr, ini):
            tt(t_a, inr, inr, ALU.mult, PW, ["pw"])
            tt(t_b, ini, ini, ALU.mult, PW, ["pw"])
            tt(den, inr, ini, ALU.mult, PW, ["pw"])
            tt(outr, t_a, t_b, ALU.subtract, PW, ["pw"])
            ts(outi, den, 2.0, None, ALU.mult, None, PW, ["pw"])
        for _ in range(nsq):
            csq(p1r, p1i, p1r, p1i)
        csq(p2r, p2i, p1r, p1i)
        k.op("dve", lambda e: e.memset(Xre, 0.0), reads=["X"], writes=["X"])
        k.op("dve", lambda e: e.memset(Xim, 0.0), reads=["X"], writes=["X"])
        cfr = smt("cfr"); cfi = smt("cfi")
        for r in range(NQ):
            ts(cfr, p1r, cco[:, r * 3 + 1:r * 3 + 2], cco[:, r * 3:r * 3 + 1], ALU.mult, ALU.add, PW + ["cco"], ["cf"])
            stt(cfr, p2r, cco[:, r * 3 + 2:r * 3 + 3], cfr, ALU.mult, ALU.add, PW + ["cco", "cf"], ["cf"])
            ts(cfi, p1i, cco[:, r * 3 + 1:r * 3 + 2], None, ALU.mult, None, PW + ["cco"], ["cf"])
            stt(cfi, p2i, cco[:, r * 3 + 2:r * 3 + 3], cfi, ALU.mult, ALU.add, PW + ["cco", "cf"], ["cf"])
            Sr = G[:, r, 0, :]
            Si = G[:, r, 1, :]
            tt(t_a, cfr, Sr, ALU.mult, ["cf", "G"], ["pw"])
            tt(Xre, Xre, t_a, ALU.add, ["X", "pw"], ["X"])
            tt(t_a, cfi, Si, ALU.mult, ["cf", "G"], ["pw"])
            tt(Xre, Xre, t_a, ALU.subtract, ["X", "pw"], ["X"])
            tt(t_a, cfr, Si, ALU.mult, ["cf", "G"], ["pw"])
            tt(Xim, Xim, t_a, ALU.add, ["X", "pw"], ["X"])
            tt(t_a, cfi, Sr, ALU.mult, ["cf", "G"], ["pw"])
            tt(Xim, Xim, t_a, ALU.add, ["X", "pw"], ["X"])

    k.handoff(SETUP_KEYS + STG_KEYS, R2_MIX + FO_KEYS)
    k.op("pool", lambda e: e.memset(hist.rearrange("p a b -> p (a b)"), 0.0), writes=["hist"])
    k.op("pool", lambda e: e.memset(carry.rearrange("p a b -> p (a b)"), 0.0), writes=[("carry", i) for i in range(88)])
    out_dmas = []
    for b in range(NB):
        load_x_block(b)
        k.dma("sp", maskb, bass.AP(maskd.tensor, b * N, [[0, 128], [1, N]]), writes=["maskb"])
        if stop == "x":
            k.emit()
            return nc
        rms_to_nb("norm_mix_pre")
        if stop == "norm":
            k.emit()
            return nc
        u_proj()
        if stop == "uproj":
            k.emit()
            return nc
        for cc in range(8):
            ci_ = cin[cc % 2]
            ck = f"cin{cc % 2}"
            psv, pkv = proj("w_in", 4 + cc, lambda kc: nb[:, kc, :], NBK, N)
            psg, pkg = proj("w_in", 12 + cc, lambda kc: nb[:, kc, :], NBK, N)
            act(tA, psg[:, 0:N], AF.Sigmoid, [pkg], ["tA"])
            cb_ = cinb[cc % 2]
            cp(cb_[:, 0:CK - 1], histb[:, cc, :], ["hist"], [ck], eng="act")
            tt(cb_[:, CK - 1:CK - 1 + N], psv[:, 0:N], tA, ALU.mult, [pkv, "tA"], [ck])
            cp(histb[:, cc, :], cb_[:, N:N + CK - 1], [ck], ["hist"], eng="act")
            a_ = acc[:, cc, :]
            psc_, pkc_ = newps()
            for half in range(2):
                wi_ = wbi[0] % NWB
                wbi[0] += 1
                k.dma("sp", wbuf[wi_], cdg[cc, :, half * 16:(half + 1) * 16, :], reads=[("cdg", cc, half)],
                      writes=[("wbuf", wi_)])
                for j in range(16):
                    k_ = half * 16 + j
                    if k_ >= CK:
                        continue
                    mm(psc_[:, 0:N], wbuf[wi_][:, j, :], cb_[:, k_:k_ + N], k_ == 0, k_ == CK - 1,
                       [("wbuf", wi_), ck], [pkc_])
            act(a_, psc_[:, 0:N], AF.Identity, [pkc_] + PVK, [("acc", cc)], bias=pv["conv_dw_b"][:, cc:cc + 1])
            s0 = sqb[0]
            s1 = sqb[1]
            act(s0, a_, AF.Copy, [("acc", cc)], [("sqb", 0)])
            act(s1, a_, AF.Square, [("acc", cc)], [("sqb", 1)])
            mm(pst1[:, 0:N], ones_b, s0, cc == 0, cc == 7, [("sqb", 0), "ones_b"], ["pst1"])
            mm(pst2[:, 0:N], ones_b, s1, cc == 0, cc == 7, [("sqb", 1), "ones_b"], ["pst2"])
        ts(rt[1], pst1[:, 0:N], 1.0 / CONVW, None, ALU.mult, None, ["pst1"], ["rt1"])
        tt(rt[2], rt[1], rt[1], ALU.mult, ["rt1"], ["rt2"])
        stt(rt[2], pst2[:, 0:N], 1.0 / CONVW, rt[2], ALU.mult, ALU.subtract, ["pst2", "rt2"], ["rt2"])
        ts(rt[2], rt[2], LNEPS, None, ALU.add, None, ["rt2"], ["rt2"])
        act(rt[2], rt[2], AF.Sqrt, ["rt2"], ["rt2"])
        k.op("dve", lambda e: e.reciprocal(out=rt[2], in_=rt[2]), ["rt2"], ["rt2"])
        for cc in range(8):
            a_ = acc[:, cc, :]
            tt(a_, a_, rt[1], ALU.subtract, [("acc", cc), "rt1"], [("acc", cc)])
            tt(a_, a_, rt[2], ALU.mult, [("acc", cc), "rt2"], [("acc", cc)])
            act(csil[:, cc, :], a_, AF.Silu, [("acc", cc)] + PVK, [("csil", cc)],
                scale=pv["conv_ln_g"][:, cc:cc + 1], bias=pv["conv_ln_b"][:, cc:cc + 1])
        if stop == "convb":
            k.emit()
            return nc
        ssm_pass(True)
        if stop == "ssmb":
            k.emit()
            return nc
        for ch in range(4):
            cp(u_b[:, ch, :], u_f[:, ch, :], [("u_f", ch)], [("u_b", ch)], eng="act")
        UBK = [("u_b", i) for i in range(4)]
        for oc in range(4):
            ps, pk = proj("w_ssm_glu", oc, lambda kc: u_b[:, kc, :], UBK, N)
            act(tA, ps[:, 0:N], AF.Sigmoid, [pk], ["tA"])
            tt(ysb[:, oc, :], u_f[:, oc, :], tA, ALU.mult, [("u_f", oc), "tA"], [("ysb", oc)])
        YK = [("ysb", i) for i in range(4)]
        CSK = [("csil", i) for i in range(8)]
        for oc in range(16):
            psa, pka = proj("w_in", 20 + oc, lambda kc: nb[:, kc, :], NBK, N)
            act(tA, psa[:, 0:N], AF.Sigmoid, [pka], ["tA"])
            psb, pkb = proj("w_ssm_proj", oc, lambda kc: ysb[:, kc, :], YK, N)
            tt(mtmp, psb[:, 0:N], tA, ALU.mult, [pkb, "tA"], ["mtmp"])
            psc, pkc = proj("w_in", 36 + oc, lambda kc: nb[:, kc, :], NBK, N)
            act(tB, psc[:, 0:N], AF.Sigmoid, [pkc], ["tB"])
            psd, pkd = proj("w_conv_proj", oc, lambda kc: csil[:, kc, :], CSK, N)
            tt(tB, psd[:, 0:N], tB, ALU.mult, [pkd, "tB"], ["tB"])
            tt(merged[:, oc, :], mtmp, tB, ALU.add, ["mtmp", "tB"], [("merged", oc)])
        MK = [("merged", i) for i in range(16)]
        if stop == "merge":
            k.emit()
            return nc
        k.handoff(R1_MIX, MO_KEYS)
        for oc in range(16):
            ps, pk = proj("w_mix_out", oc, lambda kc: merged[:, kc, :], MK, N)
            act(mo_f[:, oc, :], ps[:, 0:N], AF.Copy, [pk], [("mo", oc)])
            s = sqb[oc % 2]
            act(s, ps[:, 0:N], AF.Square, [pk], [("sqb", oc % 2)])
            mm(pst1[:, 0:N], ones_b, s, oc == 0, oc == 15, [("sqb", oc % 2), "ones_b"], ["pst1"])
        rstd_of(pst1[:, 0:N], "pst1", 1.0 / D, EPS, rt[0], "rt0")
        for oc in range(16):
            stt(mo_f[:, oc, :], mo_f[:, oc, :], pv["norm_mix_post"][:, oc:oc + 1], rt[0], ALU.mult, ALU.mult,
                [("mo", oc), "rt0"] + PVK, [("mo", oc)])
            tt(h[:, oc, :], h[:, oc, :], mo_f[:, oc, :], ALU.add, [("h", oc), ("mo", oc)], [("h", oc)])
        rms_to_nb("norm_ffn_pre", extra_mask=True)
        if stop == "mix":
            k.emit()
            return nc
        k.handoff(MO_KEYS, HID_KEYS)
        k.handoff(R2_MIX, FO_KEYS)
        for fc in range(44):
            gi_, vi_ = (1, 2) if fc % 2 == 0 else (3, 4)
            tg = rt[gi_]
            tv = rt[vi_]
            gk_, vk_ = f"rt{gi_}", f"rt{vi_}"
            for (ocx, tbuf, tkey) in ((fc, tg, gk_), (44 + fc, tv, vk_)):
                ps, pk = proj("w_ffn_up", ocx, lambda kc: nb[:, kc, :], NBK, N)
                w0 = ffnw[:, 0, ocx:ocx + 1]
                w1 = ffnw[:, 1, ocx:ocx + 1]
                w2 = ffnw[:, 2, ocx:ocx + 1]
                act(tbuf, ps[:, 0:N], AF.Identity, [pk] + PVK, [tkey], scale=w2,
                    bias=pv["ffn_dw_b"][:, ocx:ocx + 1])
                stt(tbuf[:, 1:N], ps[:, 0:N - 1], w1, tbuf[:, 1:N], ALU.mult, ALU.add, [pk, tkey] + PVK, [tkey])
                stt(tbuf[:, 2:N], ps[:, 0:N - 2], w0, tbuf[:, 2:N], ALU.mult, ALU.add, [pk, tkey] + PVK, [tkey])
                stt(tbuf[:, 0:1], carry[:, ocx, 1:2], w1, tbuf[:, 0:1], ALU.mult, ALU.add,
                    [("carry", ocx), tkey] + PVK, [tkey])
                stt(tbuf[:, 0:2], carry[:, ocx, 0:2], w0, tbuf[:, 0:2], ALU.mult, ALU.add,
                    [("carry", ocx), tkey] + PVK, [tkey])
                act(carry[:, ocx, :], ps[:, N - 2:N], AF.Copy, [pk], [("carry", ocx)])
            act(tg, tg, AF.Gelu_apprx_tanh, [gk_], [gk_])
            tt(hid[:, fc, :], tg, tv, ALU.mult, [gk_, vk_], [("hid", fc)])
        for oc in range(16):
            ps, pk = proj("w_ffn_down", oc, lambda kc: hid[:, kc, :], HID_KEYS, N)
            act(fo_f[:, oc, :], ps[:, 0:N], AF.Copy, [pk], [("fo", oc)])
            s = sqb[oc % 2]
            act(s, ps[:, 0:N], AF.Square, [pk], [("sqb", oc % 2)])
            mm(pst1[:, 0:N], ones_b, s, oc == 0, oc == 15, [("sqb", oc % 2), "ones_b"], ["pst1"])
        rstd_of(pst1[:, 0:N], "pst1", 1.0 / D, EPS, rt[0], "rt0")
        for oc in range(16):
            stt(fo_f[:, oc, :], fo_f[:, oc, :], pv["norm_ffn_post"][:, oc:oc + 1], rt[0], ALU.mult, ALU.mult,
                [("fo", oc), "rt0"] + PVK, [("fo", oc)])
            tt(h[:, oc, :], h[:, oc, :], fo_f[:, oc, :], ALU.add, [("h", oc), ("fo", oc)], [("h", oc)])
        for t0 in range(0, N, 128):
            rows = min(128, N - t0)
            g0 = b * N + t0
            lo = max(g0, H)
            hi = g0 + rows
            if hi <= lo:
                continue
            for c4 in range(4):
                ps, pk = newps()
                for c in range(4):
                    cidx = c4 * 4 + c
                    tr(ps[0:rows, c * 128:(c + 1) * 128], h[:, cidx, t0:t0 + rows], ident, [("h", cidx), "ident"], [pk])
                cp(xst[0:rows, c4 * 512:(c4 + 1) * 512], ps[0:rows, 0:512], [pk], ["xst"],
                   eng=("act" if c4 % 2 else "dve"))
            od = k.dma("sp", outd[lo - H:hi - H, :], xst[lo - g0:rows, :], reads=["xst"], writes=["out"])
            out_dmas.append(od)
        k.handoff(HID_KEYS, R1_MIX)
        k.handoff(FO_KEYS, R2_MIX)
    k.emit(final_wait_ops=out_dmas)
    return nc


def make_in_maps(inputs, NB, N, H, LQ, ncores=8):
    x = np.asarray(inputs["x"], dtype=np.float32)
    B, S, _ = x.shape
    nq = ncores // B
    assert S == nq * LQ
    W = NB * N
    meta = np.asarray(inputs["meta_tokens"], dtype=np.float32)
    shared = {}
    for n in WSPEC:
        shared[n] = np.ascontiguousarray(np.asarray(inputs[n], dtype=np.float32)[0])
    for n, s in VEC_INPUTS.items():
        shared[n] = np.ascontiguousarray(np.asarray(inputs[n], dtype=np.float32).reshape(s // 128, 128))
    shared["log_step"] = np.ascontiguousarray(np.asarray(inputs["log_step"], dtype=np.float32).reshape(16, 2))
    for n in ("ssm_b_re", "ssm_b_im", "ssm_c_re", "ssm_c_im"):
        shared[n] = np.ascontiguousarray(np.asarray(inputs[n], dtype=np.float32)[0])
    shared["conv_dw_w"] = np.ascontiguousarray(np.asarray(inputs["conv_dw_w"], dtype=np.float32)[0])
    shared["ffn_dw_w"] = np.ascontiguousarray(np.asarray(inputs["ffn_dw_w"], dtype=np.float32).reshape(3, 88, 128))
    maps = []
    for core in range(ncores):
        b, q = divmod(core, nq)
        hseq = np.concatenate([meta, x[b]], axis=0)
        a = NMETA + LQ * q - H
        xw = np.zeros((W, D), np.float32)
        mask = np.ones((1, W), np.float32)
        lo = max(a, 0)
        xw[lo - a:W] = hseq[lo:a + W]
        if a < 0:
            mask[0, 0:-a] = 0.0
        cco = np.zeros((1, 24), np.float32)
        for r in range(nq):
            e = q - 1 - r
            if 0 <= e <= 2:
                cco[0, r * 3 + e] = 1.0
        m = dict(shared)
        m["xw"] = xw
        m["mask"] = mask
        m["ccoef"] = cco
        maps.append(m)
    return maps


CFG = dict(NB=10, N=414, H=44, LQ=4096)


def kernel(**inputs):
    cfg = CFG
    nc = build(**cfg)
    maps = make_in_maps(inputs, **cfg)
    res = run_bass_kernel_spmd(nc, maps, core_ids=list(range(8)))
    x = inputs["x"]
    B, S, _ = x.shape
    out = np.empty((B, S, D), np.float32)
    nq = 8 // B
    for core in range(8):
        b, q = divmod(core, nq)
        out[b, q * cfg["LQ"]:(q + 1) * cfg["LQ"]] = res.results[core]["out"]
    return out
```
